# Optimizing a Trainium2 kernel written in Bass

```python
import math
import jax, jax.numpy as jnp
from jax import lax
import numpy as np

D_MODEL = 1024
BATCH = 8
SEQ = 4096
DEPTH = 4

HEAD_DIM = 64
A_WIDTH = D_MODEL // 2
B_WIDTH = D_MODEL // 2
A_CONV = 31
B_CONV = 3
EVEN_SPLIT = (A_WIDTH, A_WIDTH, B_WIDTH, B_WIDTH, B_WIDTH)
EVEN_IN = sum(EVEN_SPLIT)
EVEN_MIX = A_WIDTH + B_WIDTH
C_HEADS = 8
D_HEADS = 4
D_VDIM = 2 * HEAD_DIM
DILATED_PAIRS = ((128, 1), (512, 4), (2048, 16))
ATTN_BLOCK = 128
RET_CHUNK = 128
ODD_SPLIT = (C_HEADS * HEAD_DIM,) * 3 + (D_HEADS * HEAD_DIM,) * 2 + (D_HEADS * D_VDIM,) * 2
ODD_IN = sum(ODD_SPLIT)
ODD_MIX = C_HEADS * HEAD_DIM + D_HEADS * D_VDIM
D_FF = 2816
FFN_CONV = 3
ROPE_THETA = 10000.0
NORM_EPS = 1e-6
NEG_BIG = -1e30
N_EVEN = (DEPTH + 1) // 2
N_ODD = DEPTH // 2

kernel_name = 'hybrid_conformer_shortconv_dilated_retention_trunk'


def rmsnorm(x, g):
    xf = x.astype(jnp.float32)
    y = xf * lax.rsqrt(jnp.mean(xf * xf, axis=-1, keepdims=True) + NORM_EPS)
    return (y * g.astype(jnp.float32)).astype(x.dtype)


def layernorm(x, g, b):
    xf = x.astype(jnp.float32)
    mu = jnp.mean(xf, axis=-1, keepdims=True)
    xc = xf - mu
    y = xc * lax.rsqrt(jnp.mean(xc * xc, axis=-1, keepdims=True) + NORM_EPS)
    return (y * g.astype(jnp.float32) + b.astype(jnp.float32)).astype(x.dtype)


def head_norm(y):
    mu = jnp.mean(y, axis=-1, keepdims=True)
    yc = y - mu
    return yc * lax.rsqrt(jnp.mean(yc * yc, axis=-1, keepdims=True) + NORM_EPS)


def split_cols(p, sizes):
    return jnp.split(p, np.cumsum(np.array(sizes))[:-1].tolist(), axis=-1)


def causal_dwconv(x, w):
    K, C = w.shape
    return lax.conv_general_dilated(
        x, w[:, None, :].astype(x.dtype), window_strides=(1,), padding=[(K - 1, 0)],
        dimension_numbers=('NWC', 'WIO', 'NWC'), feature_group_count=C)


def rope(x):
    S, hd = x.shape[1], x.shape[-1]
    inv = ROPE_THETA ** (-jnp.arange(0, hd, 2, dtype=jnp.float32) / hd)
    ang = jnp.arange(S, dtype=jnp.float32)[:, None] * inv[None, :]
    cos = jnp.cos(ang)[None, :, None, :]
    sin = jnp.sin(ang)[None, :, None, :]
    xf = x.astype(jnp.float32)
    x1, x2 = xf[..., : hd // 2], xf[..., hd // 2:]
    return jnp.concatenate([x1 * cos - x2 * sin, x1 * sin + x2 * cos], axis=-1).astype(x.dtype)


def dilated_branch(q, k, v, window, dilation):
    Bsz, S, H, hd = q.shape
    L = S // dilation
    W = window // dilation
    nb = -(-L // ATTN_BLOCK)
    Lp = nb * ATTN_BLOCK

    def sub(t):
        t = t.reshape(Bsz, L, dilation, H, hd)
        return jnp.pad(t, ((0, 0), (0, Lp - L), (0, 0), (0, 0), (0, 0)))

    def band(t):
        t = jnp.pad(t, ((0, 0), (ATTN_BLOCK, 0), (0, 0), (0, 0), (0, 0)))
        t = t.reshape(Bsz, nb + 1, ATTN_BLOCK, dilation, H, hd)
        return jnp.concatenate([t[:, :-1], t[:, 1:]], axis=2)

    qb = sub(q).reshape(Bsz, nb, ATTN_BLOCK, dilation, H, hd)
    kb = band(sub(k))
    vb = band(sub(v))
    s = jnp.einsum('bnqrhd,bnkrhd->bnrhqk', qb, kb)
    qi = jnp.arange(ATTN_BLOCK)[:, None]
    ki = jnp.arange(2 * ATTN_BLOCK)[None, :]
    rel = qi + ATTN_BLOCK - ki
    kpos = jnp.arange(nb)[:, None, None] * ATTN_BLOCK + ki - ATTN_BLOCK
    valid = (rel >= 0) & (rel <= W) & (kpos >= 0)
    s = jnp.where(valid[None, :, None, None], s, NEG_BIG)
    m = jnp.max(s, axis=-1, keepdims=True)
    p = jnp.exp(s - m)
    den = jnp.sum(p, axis=-1)
    num = jnp.einsum('bnrhqk,bnkrhd->bnqrhd', p, vb)
    num = num.reshape(Bsz, Lp, dilation, H, hd)[:, :L].reshape(Bsz, S, H, hd)

    def rows(t):
        t = jnp.transpose(t, (0, 1, 4, 2, 3)).reshape(Bsz, Lp, dilation, H)
        return t[:, :L].reshape(Bsz, S, H)

    return num, rows(den), rows(m[..., 0])


def dilated_attention(q, k, v):
    q = q.astype(jnp.float32)
    k = k.astype(jnp.float32)
    v = v.astype(jnp.float32)
    branches = [dilated_branch(q, k, v, w, d) for (w, d) in DILATED_PAIRS]
    m_all = branches[0][2]
    for _, _, m in branches[1:]:
        m_all = jnp.maximum(m_all, m)
    num = jnp.zeros_like(q)
    den = jnp.zeros_like(m_all)
    for n, dn, m in branches:
        scale = jnp.exp(m - m_all)
        num = num + n * scale[..., None]
        den = den + dn * scale
    return num / den[..., None]


def retention(q, k, v):
    Bsz, S, H, dk = q.shape
    dv = v.shape[-1]
    C = RET_CHUNK
    nc = S // C
    log_g = jnp.log1p(-(2.0 ** (-5.0 - jnp.arange(H, dtype=jnp.float32))))
    qc = q.astype(jnp.float32).reshape(Bsz, nc, C, H, dk)
    kc = k.astype(jnp.float32).reshape(Bsz, nc, C, H, dk) * (dk ** -0.5)
    vc = v.astype(jnp.float32).reshape(Bsz, nc, C, H, dv)
    i = jnp.arange(C, dtype=jnp.float32)
    diff = i[:, None] - i[None, :]
    dmat = jnp.where(diff[None] >= 0, jnp.exp(jnp.maximum(diff, 0.0)[None] * log_g[:, None, None]), 0.0)
    scores = jnp.einsum('bnihd,bnjhd->bnhij', qc, kc) * dmat
    y_intra = jnp.einsum('bnhij,bnjhe->bnihe', scores, vc)
    kdec = jnp.exp((C - 1 - i)[:, None] * log_g[None, :])
    kv = jnp.einsum('bnjhd,bnjhe->nbhde', kc * kdec[None, None, :, :, None], vc)
    chunk_decay = jnp.exp(C * log_g)[None, :, None, None]

    def step(state, kv_n):
        return state * chunk_decay + kv_n, state

    _, prev = lax.scan(step, jnp.zeros((Bsz, H, dk, dv), jnp.float32), kv)
    qdec = jnp.exp((i + 1)[:, None] * log_g[None, :])
    y_cross = jnp.einsum('bnihd,nbhde->bnihe', qc * qdec[None, None, :, :, None], prev)
    return (y_intra + y_cross).reshape(Bsz, S, H, dv)


def even_mixer(h, w_in, a_conv, a_conv_b, a_ln_g, a_ln_b, b_conv, w_out):
    a_val, a_gate, b_b, b_c, b_h = split_cols(h @ w_in, EVEN_SPLIT)
    a = a_val * jax.nn.sigmoid(a_gate)
    a = causal_dwconv(a, a_conv) + a_conv_b.astype(a.dtype)
    a = jax.nn.silu(layernorm(a, a_ln_g, a_ln_b))
    b = b_b * causal_dwconv(b_c * b_h, b_conv)
    return jnp.concatenate([a, b], axis=-1) @ w_out


def odd_mixer(h, w_in, w_out):
    Bsz, S, _ = h.shape
    cq, ck, cv, rq, rk, rv, rg = split_cols(h @ w_in, ODD_SPLIT)
    cq = rope(cq.reshape(Bsz, S, C_HEADS, HEAD_DIM)) * (HEAD_DIM ** -0.5)
    ck = rope(ck.reshape(Bsz, S, C_HEADS, HEAD_DIM))
    cv = cv.reshape(Bsz, S, C_HEADS, HEAD_DIM)
    c_out = dilated_attention(cq, ck, cv).reshape(Bsz, S, C_HEADS * HEAD_DIM)
    rq = rope(rq.reshape(Bsz, S, D_HEADS, HEAD_DIM))
    rk = rope(rk.reshape(Bsz, S, D_HEADS, HEAD_DIM))
    rv = rv.reshape(Bsz, S, D_HEADS, D_VDIM)
    y = head_norm(retention(rq, rk, rv)).reshape(Bsz, S, D_HEADS * D_VDIM)
    r_out = jax.nn.silu(rg.astype(jnp.float32)) * y
    mix = jnp.concatenate([c_out, r_out], axis=-1).astype(h.dtype)
    return mix @ w_out


def conv_ffn(h, w_up, conv_w, conv_b, w_down):
    u = causal_dwconv(h @ w_up, conv_w) + conv_b.astype(h.dtype)
    gate, up = jnp.split(u, 2, axis=-1)
    return (jax.nn.silu(gate) * up) @ w_down


def setup_inputs(seed: int = 0) -> dict:
    key = jax.random.key(seed)
    ks = jax.random.split(key, 18)
    f32 = jnp.float32

    def nrm(k, shape, scale):
        return jax.random.normal(k, shape, f32) * scale

    return {
        'x': jax.random.normal(ks[0], (BATCH, SEQ, D_MODEL), f32),
        'ev_norm': 1.0 + nrm(ks[1], (N_EVEN, D_MODEL), 0.02),
        'ev_w_in': nrm(ks[2], (N_EVEN, D_MODEL, EVEN_IN), D_MODEL ** -0.5),
        'ev_a_conv': nrm(ks[3], (N_EVEN, A_CONV, A_WIDTH), A_CONV ** -0.5),
        'ev_a_conv_b': nrm(ks[4], (N_EVEN, A_WIDTH), 0.02),
        'ev_a_ln_g': 1.0 + nrm(ks[5], (N_EVEN, A_WIDTH), 0.02),
        'ev_a_ln_b': nrm(ks[6], (N_EVEN, A_WIDTH), 0.02),
        'ev_b_conv': nrm(ks[7], (N_EVEN, B_CONV, B_WIDTH), B_CONV ** -0.5),
        'ev_w_out': nrm(ks[8], (N_EVEN, EVEN_MIX, D_MODEL), EVEN_MIX ** -0.5),
        'od_norm': 1.0 + nrm(ks[9], (N_ODD, D_MODEL), 0.02),
        'od_w_in': nrm(ks[10], (N_ODD, D_MODEL, ODD_IN), D_MODEL ** -0.5),
        'od_w_out': nrm(ks[11], (N_ODD, ODD_MIX, D_MODEL), ODD_MIX ** -0.5),
        'ffn_norm': 1.0 + nrm(ks[12], (DEPTH, D_MODEL), 0.02),
        'ffn_w_up': nrm(ks[13], (DEPTH, D_MODEL, 2 * D_FF), D_MODEL ** -0.5),
        'ffn_conv': nrm(ks[14], (DEPTH, FFN_CONV, 2 * D_FF), FFN_CONV ** -0.5),
        'ffn_conv_b': nrm(ks[15], (DEPTH, 2 * D_FF), 0.02),
        'ffn_w_down': nrm(ks[16], (DEPTH, D_FF, D_MODEL), D_FF ** -0.5),
        'final_norm': 1.0 + nrm(ks[17], (D_MODEL,), 0.02),
    }


def reference(x, ev_norm, ev_w_in, ev_a_conv, ev_a_conv_b, ev_a_ln_g, ev_a_ln_b, ev_b_conv, ev_w_out,
              od_norm, od_w_in, od_w_out, ffn_norm, ffn_w_up, ffn_conv, ffn_conv_b, ffn_w_down, final_norm):
    for layer in range(DEPTH):
        j = layer // 2
        if layer % 2 == 0:
            x = x + even_mixer(rmsnorm(x, ev_norm[j]), ev_w_in[j], ev_a_conv[j], ev_a_conv_b[j],
                               ev_a_ln_g[j], ev_a_ln_b[j], ev_b_conv[j], ev_w_out[j])
        else:
            x = x + odd_mixer(rmsnorm(x, od_norm[j]), od_w_in[j], od_w_out[j])
        x = x + conv_ffn(rmsnorm(x, ffn_norm[layer]), ffn_w_up[layer], ffn_conv[layer],
                         ffn_conv_b[layer], ffn_w_down[layer])
    return rmsnorm(x, final_norm)
```

```python
import numpy as np
from contextlib import ExitStack
import concourse.bass as bass
import concourse.mybir as mybir
from concourse.bass_utils import run_bass_kernel_spmd

F32 = mybir.dt.float32
BF16 = mybir.dt.bfloat16
AF = mybir.ActivationFunctionType
ALU = mybir.AluOpType

D = 1024
S_LEN = 4096
TT = 512
NT = S_LEN // TT
DEPTH = 4
DFF = 2816
EPS = 1e-6
SEM_LIMIT = 30000


class T:
    def __init__(self, name="", excl=False):
        self.name = name
        self.w = None
        self.r = {}
        self.excl = excl
        self.dsem = None
        self.dcount = 0


class Sched:
    def __init__(self, nc, ctx):
        self.nc = nc
        self.ctx = ctx
        self.eng = {"pe": nc.tensor, "act": nc.scalar, "dve": nc.vector,
                    "pool": nc.gpsimd, "sp": nc.sync}
        self.sems = {}
        self.semeng = {}
        self.semcnt = {}
        self.nsem = 0
        self.cursem = {}
        self.cnt = {}
        self.pending = {}
        self.waited = {e: {} for e in self.eng}
        self.dma_pool = []
        for e in self.eng:
            self.cursem[e] = self._newsem(e)
            self.cnt[e] = 0
            self.pending[e] = False
        self.ninstr = 0

    def _newsem(self, e):
        key = self.nsem
        self.nsem += 1
        self.sems[key] = self.ctx.enter_context(self.nc.semaphore(f"s{key}"))
        self.semeng[key] = e
        self.semcnt[key] = 0
        return key

    def _deps(self, e, reads, writes):
        deps = {}

        def add(tok):
            if tok is None:
                return
            k, v = tok
            if deps.get(k, 0) < v:
                deps[k] = v

        for t in reads:
            add(t.w)
            if t.excl:
                for k, v in t.r.items():
                    if self.semeng[k] != e:
                        add((k, v))
        for t in writes:
            add(t.w)
            for k, v in t.r.items():
                add((k, v))
        out = []
        for k, v in deps.items():
            if e == "pe" and self.semeng[k] == "pe":
                continue
            if self.waited[e].get(k, 0) >= v:
                continue
            self.waited[e][k] = v
            out.append((k, v))
        return out

    def _emit(self, e, fn, waits, inc):
        eng = self.eng[e]
        for (k, v) in waits[1:]:
            eng.wait_ge(self.sems[k], v)
            self.ninstr += 1
        ins = fn(eng)
        if waits:
            k, v = waits[0]
            ins._wait_ge(self.sems[k], v)
        if inc is not None:
            ins.then_inc(self.sems[inc[0]], inc[1])
            self.semcnt[inc[0]] += inc[1]
        self.ninstr += 1
        return ins

    def op(self, e, fn, reads=(), writes=(), inc=True):
        waits = self._deps(e, reads, writes)
        if inc and not self.pending[e] and self.cnt[e] >= SEM_LIMIT:
            self.cursem[e] = self._newsem(e)
            self.cnt[e] = 0
        sk = self.cursem[e]
        if inc:
            self.cnt[e] += 1
            tok = (sk, self.cnt[e])
            self.pending[e] = False
        else:
            tok = (sk, self.cnt[e] + 1)
            self.pending[e] = True
        self._emit(e, fn, waits, (sk, 1) if inc else None)
        for t in reads:
            if t.r.get(tok[0], 0) < tok[1]:
                t.r[tok[0]] = tok[1]
        for t in writes:
            t.w = tok
            t.r = {}
        return tok

    def dma(self, q, out_ap, in_ap, semtile, reads=(), writes=(), group=False):
        if semtile.dsem is None:
            if self.dma_pool:
                semtile.dsem = self.dma_pool.pop()
                semtile.dcount = self.semcnt[semtile.dsem]
            else:
                semtile.dsem = self._newsem(None)
        sk = semtile.dsem
        saved = []
        if group:
            saved = [(t, t.w) for t in writes if t.w is not None and t.w[0] == sk]
            for t, _ in saved:
                t.w = None
        waits = self._deps(q, reads, writes)
        for t, w in saved:
            t.w = w
        semtile.dcount += 16
        tok = (sk, semtile.dcount)
        self._emit(q, lambda eng: eng.dma_start(out=out_ap, in_=in_ap), waits, (sk, 16))
        for t in reads:
            if t.r.get(sk, 0) < tok[1]:
                t.r[sk] = tok[1]
        for t in writes:
            t.w = tok
            t.r = {}
        return tok

    def barrier(self, release=()):
        toks = []
        for f in self.eng:
            assert not self.pending[f]
            if self.cnt[f] > 0:
                toks.append((self.cursem[f], self.cnt[f]))
        for k, e in self.semeng.items():
            if e is None and self.semcnt[k] > 0:
                toks.append((k, self.semcnt[k]))
        for e in self.eng:
            for (k, v) in toks:
                if self.waited[e].get(k, 0) >= v:
                    continue
                self.waited[e][k] = v
                self.eng[e].wait_ge(self.sems[k], v)
                self.ninstr += 1
        for t in release:
            if t.dsem is not None:
                self.dma_pool.append(t.dsem)
                t.dsem = None


def _cols(v):
    v = np.asarray(v, np.float32)
    return np.ascontiguousarray(v.reshape(-1, 128).T)


def _conv_cols(w):
    w = np.asarray(w, np.float32)
    K, C = w.shape
    return np.ascontiguousarray(w.T.reshape(C // 128, 128, K).transpose(1, 0, 2).reshape(128, -1))


class PCols:
    def __init__(self):
        self.blocks = []
        self.off = {}
        self.n = 0

    def add(self, name, arr):
        self.off[name] = self.n
        self.blocks.append(arr)
        self.n += arr.shape[1]

    def build(self):
        return np.ascontiguousarray(np.concatenate(self.blocks, axis=1))


def param_layout(inp):
    pc = PCols()
    for j in range(2):
        pc.add(f"ev_norm{j}", _cols(inp["ev_norm"][j]))
        pc.add(f"ev_aconv{j}", _conv_cols(inp["ev_a_conv"][j]))
        pc.add(f"ev_aconvb{j}", _cols(inp["ev_a_conv_b"][j]))
        pc.add(f"ev_lng{j}", _cols(inp["ev_a_ln_g"][j]))
        pc.add(f"ev_lnb{j}", _cols(inp["ev_a_ln_b"][j]))
        pc.add(f"ev_bconv{j}", _conv_cols(inp["ev_b_conv"][j]))
        pc.add(f"od_norm{j}", _cols(inp["od_norm"][j]))
    for l in range(DEPTH):
        pc.add(f"ffn_norm{l}", _cols(inp["ffn_norm"][l]))
        pc.add(f"ffn_conv{l}", _conv_cols(inp["ffn_conv"][l]))
        pc.add(f"ffn_convb{l}", _cols(inp["ffn_conv_b"][l]))
    pc.add("final_norm", _cols(inp["final_norm"]))
    return pc


class Builder:
    def __init__(self, n_layers, pc_off, pc_n):
        self.n_layers = n_layers
        self.off = pc_off
        nc = self.nc = bass.Bass("TRN2", target_bir_lowering=False)
        dt = nc.dram_tensor
        self.xin = dt("xT", [D, S_LEN], F32, kind="ExternalInput").ap()
        self.prm_d = dt("prm", [128, pc_n], F32, kind="ExternalInput").ap()
        self.ev_w_in = dt("ev_w_in", [2, D, 2560], F32, kind="ExternalInput").ap()
        self.ev_w_out = dt("ev_w_out", [2, D, D], F32, kind="ExternalInput").ap()
        self.od_w_in = dt("od_w_in", [2, D, 3072], F32, kind="ExternalInput").ap()
        self.od_w_sw = dt("od_w_sw", [2, D, 1536], F32, kind="ExternalInput").ap()
        self.od_w_out = dt("od_w_out", [2, D, D], F32, kind="ExternalInput").ap()
        self.w_up = dt("ffn_w_up", [DEPTH, D, 2 * DFF], F32, kind="ExternalInput").ap()
        self.w_down = dt("ffn_w_down", [DEPTH, DFF, D], F32, kind="ExternalInput").ap()
        self.rope_d = dt("rope", [4, 128, S_LEN], F32, kind="ExternalInput").ap()
        self.ctab_d = dt("ctab", [128, 1024], F32, kind="ExternalInput").ap()
        self.maskb_d = dt("maskb", [128, 640], F32, kind="ExternalInput").ap()
        self.yout = dt("yT", [D, S_LEN], F32, kind="ExternalOutput").ap()
        self.XT = dt("XT_s", [D, S_LEN], F32, kind="Internal").ap()
        self.HT = dt("HT_s", [D, S_LEN], BF16, kind="Internal").ap()
        self.QT = dt("QT_s", [768, S_LEN], BF16, kind="Internal").ap()
        self.KT = dt("KT_s", [768, S_LEN], BF16, kind="Internal").ap()
        self.V = dt("V_s", [S_LEN, 512], BF16, kind="Internal").ap()
        self.RV = dt("RV_s", [S_LEN, 512], BF16, kind="Internal").ap()
        self.SG = dt("SG_s", [512, S_LEN], F32, kind="Internal").ap()
        self.MIXT = dt("MIXT_s", [D, S_LEN], BF16, kind="Internal").ap()
        self.t_XT, self.t_HT, self.t_QT, self.t_KT = T("XT"), T("HT"), T("QT"), T("KT")
        self.t_V, self.t_RV, self.t_SG, self.t_MIXT, self.t_Y = T("V"), T("RV"), T("SG"), T("MIXT"), T("Y")

    def sb(self, ph, name, shape, dtype):
        self._uid = getattr(self, "_uid", 0) + 1
        return ph.enter_context(self.nc.sbuf_tensor(f"{name}_u{self._uid}", shape, dtype))

    def mm(self, bank_t, out_ap, pairs, reads, start=True, stop=True, skip=False):
        n = len(pairs)
        for i, (l, r) in enumerate(pairs):
            st = start and i == 0
            sp_ = stop and i == n - 1
            kw = dict(skip_group_check=True) if skip else {}
            self.S.op("pe", lambda e, l=l, r=r, st=st, sp_=sp_, kw=kw: e.matmul(out_ap, lhsT=l, rhs=r, start=st, stop=sp_, **kw),
                      reads=reads, writes=[bank_t], inc=(i == n - 1))

    def pcol(self, name, c):
        o = self.off[name] + c
        return self.prm[:, o:o + 1]

    def xview(self, dram, t0):
        return dram.rearrange("(c p) t -> p c t", p=128)[:, :, t0:t0 + TT]

    def load_w(self, ph, name, src2d, rows, cols, q="pool"):
        kc = rows // 128
        w = self.sb(ph, name, [128, kc, cols], BF16)
        t = T(name)
        v = src2d.rearrange("(kc p) n -> p kc n", p=128)
        for k in range(kc):
            self.S.dma(q, w[:, k, :], v[:, k, :], t, writes=[t], group=True)
        return w, t

    def load_wg(self, ph, name, src2d, rows, cols, groups, w=None, dst0=0, q="pool"):
        kc = rows // 128
        if w is None:
            w = self.sb(ph, name, [128, kc, cols], BF16)
        v = src2d.rearrange("(kc p) n -> p kc n", p=128)
        tiles = []
        for gi, (c0, c1) in enumerate(groups):
            t = T(f"{name}_g{gi}")
            for k in range(kc):
                self.S.dma(q, w[:, k, dst0 + c0:dst0 + c1], v[:, k, c0:c1], t, writes=[t], group=True)
            tiles.append((c0, c1, t))

        def tile_for(col):
            for (c0, c1, t) in tiles:
                if c0 <= col < c1:
                    return t
            raise KeyError(col)
        return w, tile_for, [t for (_, _, t) in tiles]

    def rms_hT(self, xt, t_xt, hT, t_hT, gname, sq, t_sq, rstd, t_rstd, bank, t_bank):
        S = self.S
        for c in range(8):
            b = c % 2
            S.op("act", lambda e, c=c, b=b: e.activation(out=sq[b][:], in_=xt[:, c, :], func=AF.Square),
                 reads=[t_xt], writes=[t_sq[b]])
            S.op("pe", lambda e, c=c, b=b: e.matmul(bank[:], lhsT=self.ones_f[:], rhs=sq[b][:], start=(c == 0), stop=(c == 7)),
                 reads=[t_sq[b], self.t_const], writes=[t_bank], inc=True)
        S.op("act", lambda e: e.activation(out=rstd[:], in_=bank[:], func=AF.Sqrt, bias=self.eps_c[:], scale=1.0 / D),
             reads=[t_bank, self.t_const], writes=[t_rstd])
        S.op("dve", lambda e: e.reciprocal(out=rstd[:], in_=rstd[:]), reads=[t_rstd], writes=[t_rstd])
        for c in range(8):
            S.op("dve", lambda e, c=c: e.scalar_tensor_tensor(out=hT[:, c, :], in0=xt[:, c, :], scalar=self.pcol(gname, c),
                                                              in1=rstd[:], op0=ALU.mult, op1=ALU.mult),
                 reads=[t_xt, t_rstd, self.t_const], writes=[t_hT[c]])

    def build(self):
        nc = self.nc
        with ExitStack() as ctx:
            S = self.S = Sched(nc, ctx)
            self.prm = nc.alloc_sbuf_tensor("prm_sb", [128, self.prm_d.shape[1]], F32)
            self.ones_f = nc.alloc_sbuf_tensor("ones_f", [128, 128], F32)
            self.eps_c = nc.alloc_sbuf_tensor("eps_c", [128, 1], F32)
            self.t_const = T("const")
            S.dma("sp", self.prm[:], self.prm_d, self.t_const, writes=[self.t_const])
            S.op("pool", lambda e: e.memset(self.ones_f[:], 1.0), writes=[self.t_const])
            S.op("pool", lambda e: e.memset(self.eps_c[:], EPS), writes=[self.t_const])
            self.ctab = nc.alloc_sbuf_tensor("ctab_sb", [128, 1024], F32)
            self.maskb = nc.alloc_sbuf_tensor("maskb_sb", [128, 256], BF16)
            self.ident = nc.alloc_sbuf_tensor("ident_sb", [128, 128], BF16)
            self.ones_b = nc.alloc_sbuf_tensor("ones_b", [128, 64], BF16)
            self.t_cb = T("constb")
            S.dma("sp", self.ctab[:], self.ctab_d, self.t_cb, writes=[self.t_cb])
            S.barrier()
            S.dma("pool", self.maskb[:], self.maskb_d[:, 0:256], self.t_cb, writes=[self.t_cb])
            S.barrier()
            S.dma("pool", self.ident[:], self.maskb_d[:, 256:384], self.t_cb, writes=[self.t_cb])
            S.barrier()
            self.mask01 = nc.alloc_sbuf_tensor("mask01_sb", [128, 256], BF16)
            S.dma("pool", self.mask01[:], self.maskb_d[:, 384:640], self.t_cb, writes=[self.t_cb])
            S.op("pool", lambda e: e.memset(self.ones_b[:], 1.0), writes=[self.t_cb])
            self.ps = [nc.alloc_psum_tensor(f"ps{i}", [128, 512], F32) for i in range(7)]
            self.t_ps = [T(f"ps{i}", excl=True) for i in range(7)]
            self.psb = nc.alloc_psum_tensor("psb", [128, 1024], BF16)
            self.t_psb = T("psb", excl=True)
            S.barrier()

            src = self.xin
            for layer in range(self.n_layers):
                j = layer // 2
                if layer % 2 == 0:
                    self.phase_even(j, src)
                else:
                    self.phase_odd(j)
                src = self.XT
                self.phase_ffn(layer, 0, fuse_j=(j if layer % 2 == 1 else None))
                self.phase_ffn(layer, 1, final=(layer == self.n_layers - 1))
            print("ninstr", S.ninstr, "nsem", S.nsem, flush=True)
        return nc

    def phase_even(self, j, src):
        nc, S = self.nc, self.S
        with ExitStack() as ph:
            w_in, win_t, win_tl = self.load_wg(ph, "ev_win", self.ev_w_in[j], D, 2560,
                                               [(512, 1024), (0, 512), (1536, 2048), (2048, 2560), (1024, 1536)])
            w_out, t_wout = self.load_w(ph, "ev_wout", self.ev_w_out[j], D, D)
            xt = [self.sb(ph, f"xt{i}", [128, 8, TT], F32) for i in range(2)]
            t_xt = [T(f"xt{i}") for i in range(2)]
            hTb = [self.sb(ph, f"hT{i}", [128, 8, TT], BF16) for i in range(2)]
            t_hTb = [[T(f"hT{i}_{c}") for c in range(8)] for i in range(2)]
            sq = [self.sb(ph, f"sq{i}", [128, TT], F32) for i in range(2)]
            t_sq = [T(f"sq{i}") for i in range(2)]
            rstd = self.sb(ph, "rstd", [128, TT], F32)
            t_rstd = T("rstd")
            abuf = [self.sb(ph, f"abuf{c}", [128, 30 + TT], BF16) for c in range(4)]
            dg = self.sb(ph, "dg", [128, 124, 128], BF16)
            t_dg = T("dg")
            t_abuf = [T(f"abuf{c}") for c in range(4)]
            acv = [self.sb(ph, f"acv{c}", [128, TT], F32) for c in range(4)]
            t_acv = [T(f"acv{c}") for c in range(4)]
            acp = [self.sb(ph, f"acp{c}", [128, TT], F32) for c in range(2)]
            t_acp = [T(f"acp{c}") for c in range(2)]
            sig = [self.sb(ph, f"sig{i}", [128, TT], F32) for i in range(2)]
            t_sig = [T(f"sig{i}") for i in range(2)]
            ctmp = [self.sb(ph, f"ctmp{i}", [128, TT], F32) for i in range(3)]
            t_ctmp = [T(f"ctmp{i}") for i in range(3)]
            ktmp = 0
            bbuf = [self.sb(ph, f"bbuf{c}", [128, 2 + TT], F32) for c in range(4)]
            t_bbuf = [T(f"bbuf{c}") for c in range(4)]
            bacc = [self.sb(ph, f"bacc{i}", [128, TT], F32) for i in range(2)]
            t_bacc = [T(f"bacc{i}") for i in range(2)]
            mix = self.sb(ph, "mix", [128, 8, TT], BF16)
            t_mix = [T(f"mix{c}") for c in range(8)]
            mean = self.sb(ph, "mean", [128, TT], F32)
            t_mean = T("mean")
            lrs = self.sb(ph, "lrs", [128, TT], F32)
            t_lrs = T("lrs")
            ps, t_ps = self.ps, self.t_ps
            for c in range(4):
                S.op("pool", lambda e, c=c: e.memset(abuf[c][:, 0:30], 0.0), writes=[t_abuf[c]])
                S.op("pool", lambda e, c=c: e.memset(bbuf[c][:, 0:2], 0.0), writes=[t_bbuf[c]])
            gname = f"ev_norm{j}"
            for c in range(4):
                for k in range(31):
                    wcol = self.off[f"ev_aconv{j}"] + c * 31 + k
                    S.op("dve", lambda e, c=c, k=k, wcol=wcol: e.tensor_scalar(out=dg[:, c * 31 + k, :], in0=self.ident[:],
                                                                              scalar1=self.prm[:, wcol:wcol + 1], scalar2=None, op0=ALU.mult),
                         reads=[self.t_cb, self.t_const], writes=[t_dg])
            bk = 0

            def nb():
                nonlocal bk
                b = 1 + (bk % 6)
                bk += 1
                return b

            def prologue(i):
                S.dma("sp", xt[i % 2][:], self.xview(src, i * TT), t_xt[i % 2], reads=[self.t_XT], writes=[t_xt[i % 2]])
                self.rms_hT(xt[i % 2], t_xt[i % 2], hTb[i % 2], t_hTb[i % 2], gname, sq, t_sq, rstd, t_rstd, ps[0], t_ps[0])

            prologue(0)
            for i in range(NT):
                t0 = i * TT
                x, tx = xt[i % 2], t_xt[i % 2]
                hT, t_hT = hTb[i % 2], t_hTb[i % 2]

                def proj(col0, b):
                    self.mm(t_ps[b], ps[b][:], [(w_in[:, k, col0:col0 + 128], hT[:, k, :]) for k in range(8)],
                            reads=[win_t(col0)] + t_hT)

                for c in range(4):
                    bg, bv = nb(), nb()
                    proj(512 + c * 128, bg)
                    proj(c * 128, bv)
                    sg, tsg = sig[c % 2], t_sig[c % 2]
                    S.op("act", lambda e, bg=bg, sg=sg: e.activation(out=sg[:], in_=ps[bg][:], func=AF.Sigmoid),
                         reads=[t_ps[bg]], writes=[tsg])
                    S.op("dve", lambda e, bv=bv, sg=sg, c=c: e.tensor_tensor(out=abuf[c][:, 30:30 + TT], in0=ps[bv][:], in1=sg[:], op=ALU.mult),
                         reads=[t_ps[bv], tsg], writes=[t_abuf[c]])
                    wc = self.off[f"ev_aconv{j}"] + c * 31
                    bcv = nb()
                    self.mm(t_ps[bcv], ps[bcv][:], [(dg[:, c * 31 + k, :], abuf[c][:, k:k + TT]) for k in range(31)],
                            reads=[t_dg, t_abuf[c]])
                    S.op("act", lambda e, c=c, bcv=bcv: e.activation(out=acv[c][:], in_=ps[bcv][:], func=AF.Identity,
                                                                     bias=self.pcol(f"ev_aconvb{j}", c), scale=1.0),
                         reads=[t_ps[bcv], self.t_const], writes=[t_acv[c]])
                    S.op("pool", lambda e, c=c: e.tensor_copy(out=abuf[c][:, 0:30], in_=abuf[c][:, TT:TT + 30]),
                         reads=[t_abuf[c]], writes=[t_abuf[c]])
                bm, bv2 = nb(), nb()
                for c in range(4):
                    S.op("pe", lambda e, c=c, bm=bm: e.matmul(ps[bm][:], lhsT=self.ones_f[:], rhs=acv[c][:], start=(c == 0), stop=(c == 3)),
                         reads=[t_acv[c], self.t_const], writes=[t_ps[bm]])
                for c in range(4):
                    b = c % 2
                    S.op("act", lambda e, c=c, b=b: e.activation(out=sq[b][:], in_=acv[c][:], func=AF.Square),
                         reads=[t_acv[c]], writes=[t_sq[b]])
                    S.op("pe", lambda e, c=c, b=b, bv2=bv2: e.matmul(ps[bv2][:], lhsT=self.ones_f[:], rhs=sq[b][:], start=(c == 0), stop=(c == 3)),
                         reads=[t_sq[b], self.t_const], writes=[t_ps[bv2]])
                S.op("act", lambda e, bm=bm: e.activation(out=mean[:], in_=ps[bm][:], func=AF.Identity, scale=1.0 / 512),
                     reads=[t_ps[bm]], writes=[t_mean])
                S.op("act", lambda e: e.activation(out=lrs[:], in_=mean[:], func=AF.Square), reads=[t_mean], writes=[t_lrs])
                S.op("dve", lambda e, bv2=bv2: e.scalar_tensor_tensor(out=lrs[:], in0=ps[bv2][:], scalar=1.0 / 512, in1=lrs[:],
                                                                      op0=ALU.mult, op1=ALU.subtract),
                     reads=[t_ps[bv2], t_lrs], writes=[t_lrs])
                S.op("act", lambda e: e.activation(out=lrs[:], in_=lrs[:], func=AF.Sqrt, bias=self.eps_c[:], scale=1.0),
                     reads=[t_lrs, self.t_const], writes=[t_lrs])
                S.op("dve", lambda e: e.reciprocal(out=lrs[:], in_=lrs[:]), reads=[t_lrs], writes=[t_lrs])
                for c in range(4):
                    S.op("dve", lambda e, c=c: e.tensor_tensor(out=acv[c][:], in0=acv[c][:], in1=mean[:], op=ALU.subtract),
                         reads=[t_acv[c], t_mean], writes=[t_acv[c]])
                    S.op("dve", lambda e, c=c: e.tensor_tensor(out=acv[c][:], in0=acv[c][:], in1=lrs[:], op=ALU.mult),
                         reads=[t_acv[c], t_lrs], writes=[t_acv[c]])
                    S.op("act", lambda e, c=c: e.activation(out=mix[:, c, :], in_=acv[c][:], func=AF.Silu,
                                                            bias=self.pcol(f"ev_lnb{j}", c), scale=self.pcol(f"ev_lng{j}", c)),
                         reads=[t_acv[c], self.t_const], writes=[t_mix[c]])
                for c in range(4):
                    bc_, bh_, bb_ = nb(), nb(), nb()
                    proj(1536 + c * 128, bc_)
                    proj(2048 + c * 128, bh_)
                    proj(1024 + c * 128, bb_)
                    sg, tsg = sig[c % 2], t_sig[c % 2]
                    S.op("act", lambda e, bh_=bh_, sg=sg: e.activation(out=sg[:], in_=ps[bh_][:], func=AF.Copy),
                         reads=[t_ps[bh_]], writes=[tsg])
                    S.op("dve", lambda e, bc_=bc_, sg=sg, c=c: e.tensor_tensor(out=bbuf[c][:, 2:2 + TT], in0=ps[bc_][:], in1=sg[:], op=ALU.mult),
                         reads=[t_ps[bc_], tsg], writes=[t_bbuf[c]])
                    wc = self.off[f"ev_bconv{j}"] + c * 3
                    ba, tba = bacc[c % 2], t_bacc[c % 2]
                    S.op("act", lambda e, c=c, wc=wc, ba=ba: e.activation(out=ba[:], in_=bbuf[c][:, 2:2 + TT], func=AF.Identity,
                                                                          scale=self.prm[:, wc + 2:wc + 3]),
                         reads=[t_bbuf[c], self.t_const], writes=[tba])
                    for k in range(2):
                        S.op("dve", lambda e, c=c, wc=wc, k=k, ba=ba: e.scalar_tensor_tensor(
                            out=ba[:], in0=bbuf[c][:, k:k + TT], scalar=self.prm[:, wc + k:wc + k + 1], in1=ba[:],
                            op0=ALU.mult, op1=ALU.add), reads=[t_bbuf[c], tba, self.t_const], writes=[tba])
                    S.op("dve", lambda e, c=c, bb_=bb_, ba=ba: e.tensor_tensor(out=mix[:, 4 + c, :], in0=ps[bb_][:], in1=ba[:], op=ALU.mult),
                         reads=[t_ps[bb_], tba], writes=[t_mix[4 + c]])
                    S.op("pool", lambda e, c=c: e.tensor_copy(out=bbuf[c][:, 0:2], in_=bbuf[c][:, TT:TT + 2]),
                         reads=[t_bbuf[c]], writes=[t_bbuf[c]])
                if i + 1 < NT:
                    prologue(i + 1)
                for oc in range(8):
                    b = nb()
                    self.mm(t_ps[b], ps[b][:], [(w_out[:, k, oc * 128:(oc + 1) * 128], mix[:, k, :]) for k in range(8)],
                            reads=[t_wout] + t_mix)
                    S.op("dve", lambda e, oc=oc, b=b, x=x: e.tensor_tensor(out=x[:, oc, :], in0=ps[b][:], in1=x[:, oc, :], op=ALU.add),
                         reads=[t_ps[b], tx], writes=[tx])
                S.dma("sp", self.xview(self.XT, t0), x[:], tx, reads=[tx], writes=[self.t_XT])
            S.barrier(release=win_tl + [t_wout] + t_xt)

    def phase_ffn(self, layer, hf, final=False, fuse_j=None):
        nc, S = self.nc, self.S
        HC = 11
        with ExitStack() as ph:
            wu = self.sb(ph, "wu", [128, 8, 2 * HC * 128], BF16)
            upv = self.w_up[layer].rearrange("(kc p) n -> p kc n", p=128)
            GP = ((0, 2), (2, 5), (5, 8), (8, 11))
            t_wug = [T(f"wu_g{gi}") for gi in range(len(GP))]
            for gi, (p0, p1) in enumerate(GP):
                for k in range(8):
                    S.dma("pool", wu[:, k, p0 * 128:p1 * 128], upv[:, k, (hf * HC + p0) * 128:(hf * HC + p1) * 128], t_wug[gi],
                          writes=[t_wug[gi]], group=True)
                    S.dma("pool", wu[:, k, (HC + p0) * 128:(HC + p1) * 128], upv[:, k, DFF + (hf * HC + p0) * 128:DFF + (hf * HC + p1) * 128],
                          t_wug[gi], writes=[t_wug[gi]], group=True)

            def t_wu_for(pc_):
                for gi, (p0, p1) in enumerate(GP):
                    if p0 <= pc_ < p1:
                        return t_wug[gi]
            wd, t_wd = self.load_w(ph, "wd", self.w_down[layer][hf * HC * 128:(hf + 1) * HC * 128, :], HC * 128, D)
            xt = [self.sb(ph, f"xt{i}", [128, 8, TT], F32) for i in range(2)]
            t_xt = [T(f"xt{i}") for i in range(2)]
            hTb = [self.sb(ph, f"hT{i}", [128, 8, TT], BF16) for i in range(2)]
            t_hTb = [[T(f"hT{i}_{c}") for c in range(8)] for i in range(2)]
            t_hTall = [T(f"hTall{i}") for i in range(2)]
            sq8 = self.sb(ph, "sq8", [128, 8, TT], F32)
            t_sq8 = [T(f"sq8_{c}") for c in range(8)]
            rstd = self.sb(ph, "rstd", [128, TT], F32)
            t_rstd = T("rstd")
            NU = 4
            ug = [self.sb(ph, f"ug{i}", [128, TT], F32) for i in range(NU)]
            t_ug = [T(f"ug{i}") for i in range(NU)]
            uu = [self.sb(ph, f"uu{i}", [128, TT], F32) for i in range(NU)]
            t_uu = [T(f"uu{i}") for i in range(NU)]
            Bg = [self.sb(ph, f"Bg{i}", [128, TT], F32) for i in range(3)]
            t_Bg = [T(f"Bg{i}") for i in range(3)]
            hal = self.sb(ph, "hal", [128, 2, 2 * HC, 2], F32)
            t_hal = [[T(f"hal{p}_{c}") for c in range(2 * HC)] for p in range(2)]
            g = self.sb(ph, "g", [128, HC, TT], BF16)
            t_g = [T(f"g{c}") for c in range(HC)]
            if fuse_j is not None:
                w_out, t_wout = self.load_w(ph, "od_wout", self.od_w_out[fuse_j], D, D)
                mx = self.sb(ph, "mx", [128, 8, TT], BF16)
                t_mx = T("mx")
            if final:
                yo = self.sb(ph, "yo", [128, 8, TT], F32)
                t_yo = [T(f"yo{c}") for c in range(8)]
                t_yoall = T("yoall")
                rstd_f = self.sb(ph, "rstd_f", [128, TT], F32)
                t_rstd_f = T("rstd_f")
            ps, t_ps = self.ps, self.t_ps
            S.op("pool", lambda e: e.memset(hal[:], 0.0), writes=t_hal[0] + t_hal[1])
            gname = f"ffn_norm{layer}"
            cw = self.off[f"ffn_conv{layer}"]
            cb = self.off[f"ffn_convb{layer}"]
            bk = 0

            def nb():
                nonlocal bk
                b = 1 + (bk % 6)
                bk += 1
                return b

            def pro_load(i):
                t0 = i * TT
                x, tx = xt[i % 2], t_xt[i % 2]
                S.dma("sp", x[:], self.xview(self.XT, t0), tx, reads=[self.t_XT], writes=[tx])
                if hf == 1:
                    S.dma("sp", hTb[i % 2][:], self.xview(self.HT, t0), t_hTall[i % 2], reads=[self.t_HT], writes=[t_hTall[i % 2]])
                if fuse_j is not None:
                    S.dma("sp", mx[:], self.xview(self.MIXT, t0), t_mx, reads=[self.t_MIXT], writes=[t_mx])

            def pro_out(i):
                if fuse_j is None:
                    return
                x, tx = xt[i % 2], t_xt[i % 2]
                for oc in range(8):
                    b = nb()
                    self.mm(t_ps[b], ps[b][:], [(w_out[:, k, oc * 128:(oc + 1) * 128], mx[:, k, :]) for k in range(8)], reads=[t_wout, t_mx])
                    S.op("dve", lambda e, oc=oc, b=b, x=x: e.tensor_tensor(out=x[:, oc, :], in0=ps[b][:], in1=x[:, oc, :], op=ALU.add),
                         reads=[t_ps[b], tx], writes=[tx])

            def pro_sq(i):
                if hf == 1:
                    return
                x, tx = xt[i % 2], t_xt[i % 2]
                for c in range(8):
                    S.op("pool", lambda e, c=c, x=x: e.tensor_tensor(out=sq8[:, c, :], in0=x[:, c, :], in1=x[:, c, :], op=ALU.mult),
                         reads=[tx], writes=[t_sq8[c]])

            def pro_stat(i):
                if hf == 1:
                    return
                for c in range(8):
                    S.op("pe", lambda e, c=c: e.matmul(ps[0][:], lhsT=self.ones_f[:], rhs=sq8[:, c, :], start=(c == 0), stop=(c == 7)),
                         reads=[t_sq8[c], self.t_const], writes=[t_ps[0]], inc=(c == 7))
                S.op("act", lambda e: e.activation(out=rstd[:], in_=ps[0][:], func=AF.Sqrt, bias=self.eps_c[:], scale=1.0 / D),
                     reads=[t_ps[0], self.t_const], writes=[t_rstd])

            def pro_norm(i):
                if hf == 1:
                    return
                t0 = i * TT
                x, tx = xt[i % 2], t_xt[i % 2]
                h_, th_ = hTb[i % 2], t_hTb[i % 2]
                S.op("dve", lambda e: e.reciprocal(out=rstd[:], in_=rstd[:]), reads=[t_rstd], writes=[t_rstd])
                for c in range(8):
                    S.op("dve", lambda e, c=c, x=x, h_=h_: e.scalar_tensor_tensor(out=h_[:, c, :], in0=x[:, c, :], scalar=self.pcol(gname, c),
                                                                                  in1=rstd[:], op0=ALU.mult, op1=ALU.mult),
                         reads=[tx, t_rstd, self.t_const], writes=[th_[c]])
                S.dma("sp", self.xview(self.HT, t0), h_[:], t_hTall[i % 2], reads=th_, writes=[self.t_HT])

            def evac_back(i, pc_):
                u_g, tg_ = ug[pc_ % NU], t_ug[pc_ % NU]
                u_u, tu_ = uu[pc_ % NU], t_uu[pc_ % NU]
                S.op("act", lambda e, u_g=u_g: e.activation(out=u_g[:], in_=u_g[:], func=AF.Silu), reads=[tg_], writes=[tg_])
                S.op("pool", lambda e, u_g=u_g, u_u=u_u, pc_=pc_: e.tensor_tensor(out=g[:, pc_, :], in0=u_g[:], in1=u_u[:], op=ALU.mult),
                     reads=[tg_, tu_], writes=[t_g[pc_]])

            def up_pair(i, pc_):
                hT = hTb[i % 2]
                hreads = t_hTb[i % 2] if hf == 0 else [t_hTall[i % 2]]
                hp_, hn_ = i % 2, (i + 1) % 2
                banks = []
                for role in range(2):
                    b = nb()
                    ch = role * HC + pc_
                    self.mm(t_ps[b], ps[b][:], [(wu[:, k, ch * 128:(ch + 1) * 128], hT[:, k, :]) for k in range(8)],
                            reads=[t_wu_for(pc_)] + hreads)
                    banks.append(b)
                for role in range(2):
                    b = banks[role]
                    ch = role * HC + pc_
                    gch = role * 22 + hf * HC + pc_
                    u = (ug if role == 0 else uu)[pc_ % NU]
                    tu = (t_ug if role == 0 else t_uu)[pc_ % NU]
                    w0 = self.prm[:, cw + gch * 3:cw + gch * 3 + 1]
                    w1 = self.prm[:, cw + gch * 3 + 1:cw + gch * 3 + 2]
                    w2 = self.prm[:, cw + gch * 3 + 2:cw + gch * 3 + 3]
                    bias = self.prm[:, cb + gch:cb + gch + 1]
                    S.op("act", lambda e, b=b, ch=ch: e.activation(out=hal[:, hn_, ch, :], in_=ps[b][:, TT - 2:TT], func=AF.Copy),
                         reads=[t_ps[b]], writes=[t_hal[hn_][ch]])
                    S.op("act", lambda e, u=u, b=b, w2=w2, bias=bias: e.activation(out=u[:], in_=ps[b][:], func=AF.Identity, bias=bias, scale=w2),
                         reads=[t_ps[b], self.t_const], writes=[tu])
                    if role == 0:
                        bg_, tbg_ = Bg[pc_ % 3], t_Bg[pc_ % 3]
                        S.op("act", lambda e, bg_=bg_, b=b, w1=w1: e.activation(out=bg_[:], in_=ps[b][:], func=AF.Identity, scale=w1),
                             reads=[t_ps[b], self.t_const], writes=[tbg_])
                    else:
                        S.op("dve", lambda e, u=u, b=b, w1=w1: e.scalar_tensor_tensor(out=u[:, 1:TT], in0=ps[b][:, 0:TT - 1], scalar=w1, in1=u[:, 1:TT],
                                                                                     op0=ALU.mult, op1=ALU.add),
                             reads=[t_ps[b], tu, self.t_const], writes=[tu])
                    S.op("dve", lambda e, u=u, b=b, w0=w0: e.scalar_tensor_tensor(out=u[:, 2:TT], in0=ps[b][:, 0:TT - 2], scalar=w0, in1=u[:, 2:TT],
                                                                                 op0=ALU.mult, op1=ALU.add),
                         reads=[t_ps[b], tu, self.t_const], writes=[tu])
                    S.op("dve", lambda e, u=u, ch=ch, w0=w0: e.scalar_tensor_tensor(out=u[:, 0:2], in0=hal[:, hp_, ch, 0:2], scalar=w0, in1=u[:, 0:2],
                                                                                   op0=ALU.mult, op1=ALU.add),
                         reads=[t_hal[hp_][ch], tu, self.t_const], writes=[tu])
                    S.op("dve", lambda e, u=u, ch=ch, w1=w1: e.scalar_tensor_tensor(out=u[:, 0:1], in0=hal[:, hp_, ch, 1:2], scalar=w1, in1=u[:, 0:1],
                                                                                   op0=ALU.mult, op1=ALU.add),
                         reads=[t_hal[hp_][ch], tu, self.t_const], writes=[tu])
                    if role == 0:
                        S.op("pool", lambda e, u=u, bg_=bg_: e.tensor_tensor(out=u[:, 1:TT], in0=u[:, 1:TT], in1=bg_[:, 0:TT - 1], op=ALU.add),
                             reads=[tu, tbg_], writes=[tu])
                if pc_ >= 2:
                    evac_back(i, pc_ - 2)
                if i + 1 < NT:
                    if pc_ == 2:
                        pro_load(i + 1)
                    elif pc_ == 3:
                        pro_out(i + 1)
                    elif pc_ == 5:
                        pro_sq(i + 1)

            LEAD = 0 if hf == 0 else 2
            pro_load(0)
            pro_out(0)
            pro_sq(0)
            pro_stat(0)
            pro_norm(0)
            for i in range(NT):
                t0 = i * TT
                x, tx = xt[i % 2], t_xt[i % 2]
                for pc_ in range(LEAD if i > 0 else 0, HC):
                    up_pair(i, pc_)
                if i + 1 < NT:
                    pro_stat(i + 1)
                evac_back(i, HC - 2)
                evac_back(i, HC - 1)
                if i + 1 < NT:
                    pro_norm(i + 1)
                    for pc_ in range(LEAD):
                        up_pair(i + 1, pc_)
                for oc in range(8):
                    b = nb()
                    self.mm(t_ps[b], ps[b][:], [(wd[:, k, oc * 128:(oc + 1) * 128], g[:, k, :]) for k in range(HC)],
                            reads=[t_wd] + t_g)
                    S.op("dve", lambda e, oc=oc, b=b, x=x: e.tensor_tensor(out=x[:, oc, :], in0=ps[b][:], in1=x[:, oc, :], op=ALU.add),
                         reads=[t_ps[b], tx], writes=[tx])
                if not final:
                    S.dma("sp", self.xview(self.XT, t0), x[:], tx, reads=[tx], writes=[self.t_XT])
                else:
                    for c in range(8):
                        S.op("pool", lambda e, c=c, x=x: e.tensor_tensor(out=sq8[:, c, :], in0=x[:, c, :], in1=x[:, c, :], op=ALU.mult),
                             reads=[tx], writes=[t_sq8[c]])
                    for c in range(8):
                        S.op("pe", lambda e, c=c: e.matmul(ps[0][:], lhsT=self.ones_f[:], rhs=sq8[:, c, :], start=(c == 0), stop=(c == 7)),
                             reads=[t_sq8[c], self.t_const], writes=[t_ps[0]], inc=(c == 7))
                    S.op("act", lambda e: e.activation(out=rstd_f[:], in_=ps[0][:], func=AF.Sqrt, bias=self.eps_c[:], scale=1.0 / D),
                         reads=[t_ps[0], self.t_const], writes=[t_rstd_f])
                    S.op("dve", lambda e: e.reciprocal(out=rstd_f[:], in_=rstd_f[:]), reads=[t_rstd_f], writes=[t_rstd_f])
                    for c in range(8):
                        S.op("dve", lambda e, c=c, x=x: e.scalar_tensor_tensor(out=yo[:, c, :], in0=x[:, c, :], scalar=self.pcol("final_norm", c),
                                                                              in1=rstd_f[:], op0=ALU.mult, op1=ALU.mult),
                             reads=[tx, t_rstd_f, self.t_const], writes=[t_yo[c]])
                    S.dma("sp", self.xview(self.yout, t0), yo[:], t_yoall, reads=t_yo, writes=[self.t_Y])
            S.barrier(release=t_wug + [t_wd] + t_hTall + t_xt + ([t_yoall] if final else []) + ([t_wout, t_mx] if fuse_j is not None else []))

    def phase_final(self, src):
        nc, S = self.nc, self.S
        with ExitStack() as ph:
            xt = [self.sb(ph, f"xt{i}", [128, 8, TT], F32) for i in range(2)]
            t_xt = [T(f"xt{i}") for i in range(2)]
            yo = [self.sb(ph, f"yo{i}", [128, 8, TT], F32) for i in range(2)]
            t_yo = [[T(f"yo{i}_{c}") for c in range(8)] for i in range(2)]
            t_yoall = [T(f"yoall{i}") for i in range(2)]
            sq = [self.sb(ph, f"sq{i}", [128, TT], F32) for i in range(2)]
            t_sq = [T(f"sq{i}") for i in range(2)]
            rstd = self.sb(ph, "rstd", [128, TT], F32)
            t_rstd = T("rstd")
            for i in range(NT):
                t0 = i * TT
                x, tx = xt[i % 2], t_xt[i % 2]
                S.dma("sp", x[:], self.xview(src, t0), tx, reads=[self.t_XT], writes=[tx])
                self.rms_hT(x, tx, yo[i % 2], t_yo[i % 2], "final_norm", sq, t_sq, rstd, t_rstd, self.ps[0], self.t_ps[0])
                S.dma("sp", self.xview(self.yout, t0), yo[i % 2][:], t_yoall[i % 2], reads=t_yo[i % 2], writes=[self.t_Y])
            S.barrier(release=t_xt + t_yoall)

    def phase_odd(self, j):
        self.odd_proj(j)
        self.odd_attn_all()
        self.odd_ret_out(j)

    def odd_proj(self, j):
        nc, S = self.nc, self.S
        with ExitStack() as ph:
            w_in = self.sb(ph, "od_win", [128, 8, 3072], BF16)
            w_sw = self.sb(ph, "od_wsw", [128, 8, 1536], BF16)
            _, win_a, tl_a = self.load_wg(ph, "od_win_a", self.od_w_in[j], D, 3072, [(0, 512)], w=w_in)
            _, wsw_a, tl_b = self.load_wg(ph, "od_wsw_a", self.od_w_sw[j], D, 1536, [(0, 512)], w=w_sw)
            _, win_b, tl_c = self.load_wg(ph, "od_win_b", self.od_w_in[j], D, 3072, [(512, 1024)], w=w_in)
            _, wsw_b, tl_d = self.load_wg(ph, "od_wsw_b", self.od_w_sw[j], D, 1536, [(512, 1024)], w=w_sw)
            _, win_c, tl_e = self.load_wg(ph, "od_win_c", self.od_w_in[j], D, 3072, [(1536, 2048)], w=w_in)
            _, wsw_c, tl_f = self.load_wg(ph, "od_wsw_c", self.od_w_sw[j], D, 1536, [(1024, 1536)], w=w_sw)
            _, win_d, tl_g = self.load_wg(ph, "od_win_d", self.od_w_in[j], D, 3072, [(1024, 1536), (2048, 2560), (2560, 3072)], w=w_in)
            wtl = tl_a + tl_b + tl_c + tl_d + tl_e + tl_f + tl_g

            def t_win(col):
                for f in (win_a, win_b, win_c, win_d):
                    try:
                        return f(col)
                    except KeyError:
                        pass

            def t_wsw(col):
                for f in (wsw_a, wsw_b, wsw_c):
                    try:
                        return f(col)
                    except KeyError:
                        pass
            xt = [self.sb(ph, f"xt{i}", [128, 8, TT], F32) for i in range(2)]
            t_xt = [T(f"xt{i}") for i in range(2)]
            hTb = [self.sb(ph, f"hT{i}", [128, 8, TT], BF16) for i in range(2)]
            t_hTb = [[T(f"hT{i}_{c}") for c in range(8)] for i in range(2)]
            sq = [self.sb(ph, f"sq{i}", [128, TT], F32) for i in range(2)]
            t_sq = [T(f"sq{i}") for i in range(2)]
            rstd = self.sb(ph, "rstd", [128, TT], F32)
            t_rstd = T("rstd")
            rp_ = [self.sb(ph, f"rope{i}", [128, 4, TT], F32) for i in range(2)]
            t_rp = [T(f"rope{i}") for i in range(2)]
            t1 = [self.sb(ph, f"t1_{i}", [128, TT], F32) for i in range(2)]
            t_t1 = [T(f"t1_{i}") for i in range(2)]
            t2 = [self.sb(ph, f"t2_{i}", [128, TT], F32) for i in range(2)]
            t_t2 = [T(f"t2_{i}") for i in range(2)]
            ob = [self.sb(ph, f"ob{i}", [128, TT], BF16) for i in range(4)]
            t_ob = [T(f"ob{i}") for i in range(4)]
            vb = [self.sb(ph, f"vb{i}", [128, 4, 512], BF16) for i in range(2)]
            t_vb = [T(f"vb{i}") for i in range(2)]
            sgo = [self.sb(ph, f"sgo{i}", [128, TT], F32) for i in range(2)]
            t_sgo = [T(f"sgo{i}") for i in range(2)]
            ps, t_ps = self.ps, self.t_ps
            gname = f"od_norm{j}"
            bk = 0
            cnt = 0

            def nb():
                nonlocal bk
                b = 1 + (bk % 6)
                bk += 1
                return b

            qk = []
            for c in range(4):
                qk.append((c * 128, c * 128, self.QT, self.t_QT, c * 128, True))
            for c in range(4):
                qk.append((512 + c * 128, 512 + c * 128, self.KT, self.t_KT, c * 128, False))
            for c in range(2):
                qk.append((1536 + c * 128, 1024 + c * 128, self.QT, self.t_QT, 512 + c * 128, False))
            for c in range(2):
                qk.append((1792 + c * 128, 1280 + c * 128, self.KT, self.t_KT, 512 + c * 128, True))

            def prologue(i):
                S.dma("sp", xt[i % 2][:], self.xview(self.XT, i * TT), t_xt[i % 2], reads=[self.t_XT], writes=[t_xt[i % 2]])
                S.dma("sp", rp_[i % 2][:], self.rope_d.rearrange("a p t -> p a t")[:, :, i * TT:(i + 1) * TT], t_rp[i % 2], writes=[t_rp[i % 2]])
                self.rms_hT(xt[i % 2], t_xt[i % 2], hTb[i % 2], t_hTb[i % 2], gname, sq, t_sq, rstd, t_rstd, ps[0], t_ps[0])

            prologue(0)
            for i in range(NT):
                t0 = i * TT
                x, tx = xt[i % 2], t_xt[i % 2]
                rt, trt = rp_[i % 2], t_rp[i % 2]
                hT, t_hT = hTb[i % 2], t_hTb[i % 2]
                for (mc, sc, dst, tdst, row0, scaled) in qk:
                    ba, bb = nb(), nb()
                    self.mm(t_ps[ba], ps[ba][:], [(w_in[:, k, mc:mc + 128], hT[:, k, :]) for k in range(8)], reads=[t_win(mc)] + t_hT)
                    self.mm(t_ps[bb], ps[bb][:], [(w_sw[:, k, sc:sc + 128], hT[:, k, :]) for k in range(8)], reads=[t_wsw(sc)] + t_hT)
                    ci, si = (2, 3) if scaled else (0, 1)
                    a1, ta1 = t1[cnt % 2], t_t1[cnt % 2]
                    a2, ta2 = t2[cnt % 2], t_t2[cnt % 2]
                    o, to = ob[cnt % 4], t_ob[cnt % 4]
                    cnt += 1
                    S.op("dve", lambda e, a1=a1, ba=ba, ci=ci, rt=rt: e.tensor_tensor(out=a1[:], in0=ps[ba][:], in1=rt[:, ci, :], op=ALU.mult),
                         reads=[t_ps[ba], trt], writes=[ta1])
                    S.op("dve", lambda e, a2=a2, bb=bb, si=si, rt=rt: e.tensor_tensor(out=a2[:], in0=ps[bb][:], in1=rt[:, si, :], op=ALU.mult),
                         reads=[t_ps[bb], trt], writes=[ta2])
                    S.op("pool", lambda e, a1=a1, a2=a2, o=o: e.tensor_tensor(out=o[:], in0=a1[:], in1=a2[:], op=ALU.add),
                         reads=[ta1, ta2], writes=[to])
                    S.dma("sp", dst[row0:row0 + 128, t0:t0 + TT], o[:], to, reads=[to], writes=[tdst])
                if i + 1 < NT:
                    prologue(i + 1)
                for vi, (c0, dst, tdst) in enumerate(((1024, self.V, self.t_V), (2048, self.RV, self.t_RV))):
                    v, tv = vb[vi], t_vb[vi]
                    for ts in range(4):
                        b = nb()
                        self.mm(t_ps[b], ps[b][:], [(hT[:, k, ts * 128:(ts + 1) * 128], w_in[:, k, c0:c0 + 512]) for k in range(8)],
                                reads=[t_win(c0)] + t_hT)
                        S.op("act", lambda e, v=v, ts=ts, b=b: e.activation(out=v[:, ts, :], in_=ps[b][:], func=AF.Copy),
                             reads=[t_ps[b]], writes=[tv])
                    S.dma("sp", dst[t0:t0 + TT, :].rearrange("(ts p) f -> p ts f", p=128), v[:], tv, reads=[tv], writes=[tdst])
                for c in range(4):
                    b = nb()
                    self.mm(t_ps[b], ps[b][:], [(w_in[:, k, 2560 + c * 128:2560 + (c + 1) * 128], hT[:, k, :]) for k in range(8)],
                            reads=[t_win(2560)] + t_hT)
                    sg, tsg = sgo[c % 2], t_sgo[c % 2]
                    S.op("act", lambda e, sg=sg, b=b: e.activation(out=sg[:], in_=ps[b][:], func=AF.Silu), reads=[t_ps[b]], writes=[tsg])
                    S.dma("sp", self.SG[c * 128:(c + 1) * 128, t0:t0 + TT], sg[:], tsg, reads=[tsg], writes=[self.t_SG])
            S.barrier(release=wtl + t_xt + t_rp + t_ob + t_vb + t_sgo)

    def odd_attn_all(self):
        nc, S = self.nc, self.S
        DS = (1, 4, 16)
        with ExitStack() as ph:
            qN = [self.sb(ph, f"qN{i}", [128, S_LEN], BF16) for i in range(2)]
            kN = [self.sb(ph, f"kN{i}", [128, S_LEN], BF16) for i in range(2)]
            t_qN = [T(f"qN{i}") for i in range(2)]
            t_kN = [T(f"kN{i}") for i in range(2)]
            qP = [self.sb(ph, f"qP16_{i}", [128, S_LEN], BF16) for i in range(2)]
            kP = [self.sb(ph, f"kP16_{i}", [128, S_LEN], BF16) for i in range(2)]
            t_qP = [[T(f"qP16_{i}_{k}") for k in range(4)] for i in range(2)]
            t_kP = [[T(f"kP16_{i}_{k}") for k in range(4)] for i in range(2)]
            vt = [{d: self.sb(ph, f"vt{i}_{d}", [128, 32, 128], BF16) for d in DS} for i in range(2)]
            t_vt = [{d: T(f"vt{i}_{d}") for d in DS} for i in range(2)]
            acc = self.sb(ph, "accnd", [64, 2, S_LEN], F32)
            t_acc = T("acc")
            pt = [self.sb(ph, f"pt{i}", [128, 256], BF16) for i in range(6)]
            t_pt = [T(f"pt{i}") for i in range(6)]
            cout = self.sb(ph, "cout", [64, S_LEN], BF16)
            t_cout = T("cout")
            rec = [self.sb(ph, f"rec{i}", [64, TT], F32) for i in range(2)]
            t_rec = [T(f"rec{i}") for i in range(2)]
            ps, t_ps = self.ps, self.t_ps

            def load(hp):
                i = hp % 2
                S.dma("sp", qN[i][:], self.QT[hp * 128:(hp + 1) * 128, :], t_qN[i], reads=[self.t_QT], writes=[t_qN[i]])
                S.dma("sp", kN[i][:], self.KT[hp * 128:(hp + 1) * 128, :], t_kN[i], reads=[self.t_KT], writes=[t_kN[i]])
                for d in DS:
                    nbk = 32 // d
                    vsrc = self.V[:, hp * 128:(hp + 1) * 128].rearrange("(n p r) f -> r p n f", p=128, r=d)
                    for r in range(d):
                        S.dma("sp", vt[i][d][:, r * nbk:(r + 1) * nbk, :], vsrc[r], t_vt[i][d], reads=[self.t_V], writes=[t_vt[i][d]], group=True)

            def perm_piece(hp, k):
                i = hp % 2
                src, tsrc, dst, tdst = (qN[i], t_qN[i], qP[i], t_qP[i]) if k < 4 else (kN[i], t_kN[i], kP[i], t_kP[i])
                kk = k % 4
                S.op("act", lambda e, src=src, dst=dst, kk=kk: e.activation(
                    out=dst[:].rearrange("p (r m) -> p r m", r=16)[:, :, kk * 64:(kk + 1) * 64],
                    in_=src[:, kk * 1024:(kk + 1) * 1024].rearrange("p (m r) -> p r m", r=16), func=AF.Copy),
                    reads=[tsrc], writes=[tdst[kk]])

            load(0)
            for k in range(8):
                perm_piece(0, k)
            gblk = 0
            gpair = 0
            for hp in range(4):
                pi = hp % 2
                if hp + 1 < 4:
                    load(hp + 1)
                pblk = 0
                for hh in range(2):
                    hb = hh * 64
                    blocks = [(d, r, n) for d in DS for r in range(d) for n in range(32 // d)]
                    info = {}

                    def QK(bi):
                        nonlocal gblk, gpair
                        d, r, n = blocks[bi]
                        if d == 1:
                            qa = qN[pi][hb:hb + 64, n * 128:(n + 1) * 128]
                            kc = kN[pi][hb:hb + 64, n * 128:(n + 1) * 128]
                            kp = kN[pi][hb:hb + 64, (n - 1) * 128:n * 128] if n > 0 else None
                            rd = [t_qN[pi], t_kN[pi], self.t_cb]
                        elif d == 4:
                            qv = qN[pi][:].rearrange("p (m r) -> p r m", r=4)
                            kv = kN[pi][:].rearrange("p (m r) -> p r m", r=4)
                            qa = qv[hb:hb + 64, r, n * 128:(n + 1) * 128]
                            kc = kv[hb:hb + 64, r, n * 128:(n + 1) * 128]
                            kp = kv[hb:hb + 64, r, (n - 1) * 128:n * 128] if n > 0 else None
                            rd = [t_qN[pi], t_kN[pi], self.t_cb]
                        else:
                            J0 = r * 256 + n * 128
                            qa = qP[pi][hb:hb + 64, J0:J0 + 128]
                            kc = kP[pi][hb:hb + 64, J0:J0 + 128]
                            kp = kP[pi][hb:hb + 64, J0 - 128:J0] if n > 0 else None
                            rd = t_qP[pi] + t_kP[pi] + [self.t_cb]
                        bs = gblk % 4
                        pbuf = gblk % 6
                        gblk += 1
                        if n % 2 == 0:
                            gpair += 1
                        bo = 4 + gpair % 3
                        info[bi] = (bs, pbuf, bo)
                        lo = 0 if n > 0 else 128
                        pairs = []
                        if n > 0:
                            pairs.append((kp, qa, ps[bs][:, 0:128]))
                        pairs.append((kc, qa, ps[bs][:, 128:256]))
                        for ii, (l, rr, oo) in enumerate(pairs):
                            S.op("pe", lambda e, l=l, rr=rr, oo=oo: e.matmul(oo, lhsT=l, rhs=rr, start=True, stop=True),
                                 reads=rd, writes=[t_ps[bs]], inc=(ii == len(pairs) - 1))

                    def REST(bi):
                        d, r, n = blocks[bi]
                        nbk = 32 // d
                        bs, pbuf, bo = info[bi]
                        p_, tp_ = pt[pbuf], t_pt[pbuf]
                        v_ = vt[pi][d]
                        lo = 0 if n > 0 else 128
                        S.op("act", lambda e, p_=p_, bs=bs, lo=lo: e.activation(out=p_[:, lo:256], in_=ps[bs][:, lo:256], func=AF.Exp),
                             reads=[t_ps[bs]], writes=[tp_])
                        S.op("dve", lambda e, p_=p_, lo=lo: e.tensor_tensor(out=p_[:, lo:256], in0=p_[:, lo:256], in1=self.mask01[:, lo:256], op=ALU.mult),
                             reads=[tp_, self.t_cb], writes=[tp_])
                        vb_ = r * nbk + n
                        c0 = (n % 2) * 128
                        onum = ps[bo][0:64, c0:c0 + 128]
                        oden = ps[bo][0:64, 256 + c0:256 + c0 + 128]
                        pv = []
                        if n > 0:
                            pv.append((v_[:, vb_ - 1, hb:hb + 64], p_[:, 0:128], onum, True, False))
                        pv.append((v_[:, vb_, hb:hb + 64], p_[:, 128:256], onum, n == 0, True))
                        if n > 0:
                            pv.append((self.ones_b[:, 0:64], p_[:, 0:128], oden, True, False))
                        pv.append((self.ones_b[:, 0:64], p_[:, 128:256], oden, n == 0, True))
                        for ii, (l, rr, oo, st, sp_) in enumerate(pv):
                            S.op("pe", lambda e, l=l, rr=rr, oo=oo, st=st, sp_=sp_: e.matmul(
                                oo, lhsT=l, rhs=rr, start=st, stop=sp_, skip_group_check=True),
                                reads=[t_vt[pi][d], tp_, self.t_cb], writes=[t_ps[bo]], inc=(ii == len(pv) - 1))
                        if n % 2 == 1:
                            av = acc[:].rearrange("p a (m r) -> p a r m", r=d)[:, :, r, (n - 1) * 128:(n + 1) * 128]

                            def do_acc(av=av, bo=bo, d=d):
                                if d == 1:
                                    S.op("dve", lambda e: e.tensor_copy(
                                        out=av, in_=ps[bo][0:64, 0:512].rearrange("p (a m) -> p a m", a=2)),
                                        reads=[t_ps[bo]], writes=[t_acc])
                                else:
                                    S.op("dve", lambda e: e.tensor_tensor(
                                        out=av, in0=av, in1=ps[bo][0:64, 0:512].rearrange("p (a m) -> p a m", a=2), op=ALU.add),
                                        reads=[t_ps[bo], t_acc], writes=[t_acc])
                            while pend_acc:
                                pend_acc.pop(0)()
                            pend_acc.append(do_acc)

                    LA = 3
                    pend_acc = []
                    for bi in range(min(LA, len(blocks))):
                        QK(bi)
                    for bi in range(len(blocks)):
                        if bi + LA < len(blocks):
                            QK(bi + LA)
                        REST(bi)
                        pblk += 1
                        if hp + 1 < 4 and pblk % 20 == 0 and pblk // 20 <= 8:
                            perm_piece(hp + 1, pblk // 20 - 1)
                    while pend_acc:
                        pend_acc.pop(0)()
                    for i in range(NT):
                        t0 = i * TT
                        rc, trc = rec[i % 2], t_rec[i % 2]
                        S.op("act", lambda e, rc=rc, t0=t0: e.activation(out=rc[:], in_=acc[:, 1, t0:t0 + TT], func=AF.Ln), reads=[t_acc], writes=[trc])
                        S.op("act", lambda e, rc=rc: e.activation(out=rc[:], in_=rc[:], func=AF.Exp, scale=-1.0), reads=[trc], writes=[trc])
                        S.op("dve", lambda e, rc=rc, t0=t0: e.tensor_tensor(out=cout[:, t0:t0 + TT], in0=acc[:, 0, t0:t0 + TT], in1=rc[:], op=ALU.mult),
                             reads=[t_acc, trc], writes=[t_cout])
                    hg = hp * 2 + hh
                    S.dma("sp", self.MIXT[hg * 64:(hg + 1) * 64, :], cout[:], t_cout, reads=[t_cout], writes=[self.t_MIXT])
            S.barrier(release=t_qN + t_kN + [t_cout] + [t_vt[i][d] for i in range(2) for d in DS])

    def odd_ret_out(self, j):
        nc, S = self.nc, self.S
        NCH = 32
        with ExitStack() as ph:
            rq = self.sb(ph, "rq", [128, S_LEN], BF16)
            rk = self.sb(ph, "rk", [128, S_LEN], BF16)
            rqd = self.sb(ph, "rqd", [128, S_LEN], BF16)
            t_rq, t_rk = T("rq"), T("rk")
            t_rqd = [T(f"rqd{n}") for n in range(NCH)]
            rvt = self.sb(ph, "rvt", [128, NCH, 256], BF16)
            t_rvt = T("rvt")
            kdT = self.sb(ph, "kdT", [128, NCH, 128], BF16)
            t_kdT = [T(f"kdT{n}") for n in range(NCH)]
            yT = [self.sb(ph, f"yT{h}", [128, S_LEN], F32) for h in range(2)]
            t_yT = [[T(f"yT{h}_{i}") for i in range(NT)] for h in range(2)]
            Sf = self.sb(ph, "Sf", [128, 128], F32)
            Sb = self.sb(ph, "Sb", [128, 128], BF16)
            t_Sf = [T("Sf0"), T("Sf1")]
            t_Sb = [T("Sb0"), T("Sb1")]
            sm = [self.sb(ph, f"sm{i}", [128, 128], BF16) for i in range(3)]
            t_sm = [T(f"sm{i}") for i in range(3)]
            sq = [self.sb(ph, f"sq{i}", [128, TT], F32) for i in range(2)]
            t_sq = [T(f"sq{i}") for i in range(2)]
            mean2 = [self.sb(ph, f"mean{i}", [128, TT], F32) for i in range(2)]
            t_mean2 = [T(f"mean{i}") for i in range(2)]
            lrs2 = [self.sb(ph, f"lrs{i}", [128, TT], F32) for i in range(2)]
            t_lrs2 = [T(f"lrs{i}") for i in range(2)]
            sgt = [self.sb(ph, f"sgt{i}", [128, TT], F32) for i in range(2)]
            t_sgt = [T(f"sgt{i}") for i in range(2)]
            ro = [self.sb(ph, f"ro{i}", [128, TT], BF16) for i in range(2)]
            t_ro = [T(f"ro{i}") for i in range(2)]
            ps, t_ps = self.ps, self.t_ps
            for rp in range(2):
                S.dma("sp", rq[:], self.QT[512 + rp * 128:512 + (rp + 1) * 128, :], t_rq, reads=[self.t_QT], writes=[t_rq])
                S.dma("sp", rk[:], self.KT[512 + rp * 128:512 + (rp + 1) * 128, :], t_rk, reads=[self.t_KT], writes=[t_rk])
                S.dma("sp", rvt[:], self.RV[:, rp * 256:(rp + 1) * 256].rearrange("(n p) f -> p n f", p=128), t_rvt,
                      reads=[self.t_RV], writes=[t_rvt])
                S.op("pool", lambda e: e.memset(Sf[:], 0.0), writes=t_Sf)
                S.op("pool", lambda e: e.memset(Sb[:], 0.0), writes=t_Sb)
                QD = self.ctab[:, 512 + rp * 128:512 + (rp + 1) * 128]
                KD = self.ctab[:, 768 + rp * 128:768 + (rp + 1) * 128]
                for n in range(NCH):
                    cs = slice(n * 128, (n + 1) * 128)
                    S.op("pool", lambda e, cs=cs, QD=QD: e.tensor_tensor(out=rqd[:, cs], in0=rq[:, cs], in1=QD, op=ALU.mult),
                         reads=[t_rq, self.t_cb], writes=[t_rqd[n]])
                    S.op("pe", lambda e, cs=cs, n=n: e.transpose(out=self.psb[:, (n % 8) * 128:(n % 8 + 1) * 128], in_=rk[:, cs], identity=self.ident[:]),
                         reads=[t_rk, self.t_cb], writes=[self.t_psb])
                    S.op("dve", lambda e, n=n, KD=KD: e.tensor_tensor(out=kdT[:, n, :], in0=self.psb[:, (n % 8) * 128:(n % 8 + 1) * 128], in1=KD, op=ALU.mult),
                         reads=[self.t_psb, self.t_cb], writes=[t_kdT[n]])
                cd = [float(np.exp(128.0 * np.log1p(-(2.0 ** (-5.0 - (2 * rp + hh)))))) for hh in range(2)]
                bk = 0
                for n in range(NCH):
                    cs = slice(n * 128, (n + 1) * 128)
                    for hh in range(2):
                        hb = hh * 64
                        h = 2 * rp + hh
                        bs, by = bk % 3, 3 + bk % 3
                        s_, ts_ = sm[bk % 3], t_sm[bk % 3]
                        bk += 1
                        self.mm(t_ps[bs], ps[bs][:, 0:128], [(rk[hb:hb + 64, cs], rq[hb:hb + 64, cs])], reads=[t_rk, t_rq])
                        S.op("dve", lambda e, s_=s_, bs=bs, h=h: e.tensor_tensor(out=s_[:], in0=ps[bs][:, 0:128], in1=self.ctab[:, h * 128:(h + 1) * 128], op=ALU.mult),
                             reads=[t_ps[bs], self.t_cb], writes=[ts_])
                        pairs = [(rvt[:, n, hh * 128:(hh + 1) * 128], s_[:])]
                        if n > 0:
                            pairs.append((Sb[hb:hb + 64, :], rqd[hb:hb + 64, cs]))
                        self.mm(t_ps[by], ps[by][:, 0:128], pairs, reads=[t_rvt, ts_, t_Sb[hh], t_rqd[n]])
                        S.op("act", lambda e, hh=hh, cs=cs, by=by: e.activation(out=yT[hh][:, cs], in_=ps[by][:, 0:128], func=AF.Copy),
                             reads=[t_ps[by]], writes=[t_yT[hh][n // 4]])
                    if n < NCH - 1:
                        self.mm(t_ps[6], ps[6][:, 0:256], [(kdT[:, n, :], rvt[:, n, :])], reads=[t_kdT[n], t_rvt])
                        for hh in range(2):
                            hb = hh * 64
                            S.op("dve", lambda e, hb=hb, hh=hh, cd=cd: e.scalar_tensor_tensor(
                                out=Sf[hb:hb + 64, :], in0=Sf[hb:hb + 64, :], scalar=cd[hh], in1=ps[6][hb:hb + 64, hh * 128:(hh + 1) * 128],
                                op0=ALU.mult, op1=ALU.add), reads=[t_ps[6], t_Sf[hh]], writes=[t_Sf[hh]])
                            S.op("act", lambda e, hb=hb: e.activation(out=Sb[hb:hb + 64, :], in_=Sf[hb:hb + 64, :], func=AF.Copy),
                                 reads=[t_Sf[hh]], writes=[t_Sb[hh]])
                cnt = 0
                for hh in range(2):
                    h = 2 * rp + hh
                    for i in range(NT):
                        t0 = i * TT
                        y_ = yT[hh][:, t0:t0 + TT]
                        ty = t_yT[hh][i]
                        sgx, tsgx = sgt[cnt % 2], t_sgt[cnt % 2]
                        r_, tr_ = ro[cnt % 2], t_ro[cnt % 2]
                        sq_, tsq_ = sq[cnt % 2], t_sq[cnt % 2]
                        bm, bv = cnt % 2, 2 + cnt % 2
                        mean, t_mean = mean2[cnt % 2], t_mean2[cnt % 2]
                        lrs, t_lrs = lrs2[cnt % 2], t_lrs2[cnt % 2]
                        cnt += 1
                        S.dma("sp", sgx[:], self.SG[h * 128:(h + 1) * 128, t0:t0 + TT], tsgx, reads=[self.t_SG], writes=[tsgx])
                        S.op("pe", lambda e, y_=y_, bm=bm: e.matmul(ps[bm][:], lhsT=self.ones_f[:], rhs=y_, start=True, stop=True),
                             reads=[ty, self.t_const], writes=[t_ps[bm]])
                        S.op("act", lambda e, y_=y_, sq_=sq_: e.activation(out=sq_[:], in_=y_, func=AF.Square), reads=[ty], writes=[tsq_])
                        S.op("pe", lambda e, sq_=sq_, bv=bv: e.matmul(ps[bv][:], lhsT=self.ones_f[:], rhs=sq_[:], start=True, stop=True),
                             reads=[tsq_, self.t_const], writes=[t_ps[bv]])
                        S.op("act", lambda e, bm=bm, mean=mean: e.activation(out=mean[:], in_=ps[bm][:], func=AF.Identity, scale=1.0 / 128),
                             reads=[t_ps[bm]], writes=[t_mean])
                        S.op("act", lambda e, lrs=lrs, mean=mean: e.activation(out=lrs[:], in_=mean[:], func=AF.Square), reads=[t_mean], writes=[t_lrs])
                        S.op("dve", lambda e, bv=bv, lrs=lrs: e.scalar_tensor_tensor(out=lrs[:], in0=ps[bv][:], scalar=1.0 / 128, in1=lrs[:],
                                                                            op0=ALU.mult, op1=ALU.subtract),
                             reads=[t_ps[bv], t_lrs], writes=[t_lrs])
                        S.op("act", lambda e, lrs=lrs: e.activation(out=lrs[:], in_=lrs[:], func=AF.Ln, bias=self.eps_c[:], scale=1.0),
                             reads=[t_lrs, self.t_const], writes=[t_lrs])
                        S.op("act", lambda e, lrs=lrs: e.activation(out=lrs[:], in_=lrs[:], func=AF.Exp, scale=-0.5),
                             reads=[t_lrs], writes=[t_lrs])
                        S.op("dve", lambda e, y_=y_, mean=mean: e.tensor_tensor(out=y_, in0=y_, in1=mean[:], op=ALU.subtract), reads=[ty, t_mean], writes=[ty])
                        S.op("dve", lambda e, y_=y_, lrs=lrs: e.tensor_tensor(out=y_, in0=y_, in1=lrs[:], op=ALU.mult), reads=[ty, t_lrs], writes=[ty])
                        S.op("dve", lambda e, y_=y_, r_=r_, sgx=sgx: e.tensor_tensor(out=r_[:], in0=y_, in1=sgx[:], op=ALU.mult),
                             reads=[ty, tsgx], writes=[tr_])
                        S.dma("sp", self.MIXT[512 + h * 128:512 + (h + 1) * 128, t0:t0 + TT], r_[:], tr_, reads=[tr_], writes=[self.t_MIXT])
            S.barrier(release=[t_rq, t_rk, t_rvt] + t_sgt + t_ro)

    def odd_out(self, j):
        nc, S = self.nc, self.S
        with ExitStack() as ph:
            w_out, t_wout = self.load_w(ph, "od_wout", self.od_w_out[j], D, D)
            xt = [self.sb(ph, f"xt{i}", [128, 8, TT], F32) for i in range(2)]
            t_xt = [T(f"xt{i}") for i in range(2)]
            mx = [self.sb(ph, f"mx{i}", [128, 8, TT], BF16) for i in range(2)]
            t_mx = [T(f"mx{i}") for i in range(2)]
            ps, t_ps = self.ps, self.t_ps
            for i in range(NT):
                t0 = i * TT
                x, tx = xt[i % 2], t_xt[i % 2]
                m, tm = mx[i % 2], t_mx[i % 2]
                S.dma("sp", x[:], self.xview(self.XT, t0), tx, reads=[self.t_XT], writes=[tx])
                S.dma("sp", m[:], self.xview(self.MIXT, t0), tm, reads=[self.t_MIXT], writes=[tm])
                for oc in range(8):
                    b = oc % 6
                    self.mm(t_ps[b], ps[b][:], [(w_out[:, k, oc * 128:(oc + 1) * 128], m[:, k, :]) for k in range(8)], reads=[t_wout, tm])
                    S.op("dve", lambda e, oc=oc, b=b, x=x: e.tensor_tensor(out=x[:, oc, :], in0=ps[b][:], in1=x[:, oc, :], op=ALU.add),
                         reads=[t_ps[b], tx], writes=[tx])
                S.dma("sp", self.xview(self.XT, t0), x[:], tx, reads=[tx], writes=[self.t_XT])
            S.barrier(release=[t_wout] + t_xt + t_mx)


def const_tables():
    t = np.arange(S_LEN, dtype=np.float32)
    inv = (np.float32(10000.0) ** (-(np.arange(0, 64, 2, dtype=np.float32)) / np.float32(64))).astype(np.float32)
    ang = (t[:, None] * inv[None, :]).astype(np.float32)
    cos = np.cos(ang).astype(np.float32).T
    sin = np.sin(ang).astype(np.float32).T
    cos64 = np.concatenate([cos, cos], 0)
    sins64 = np.concatenate([-sin, sin], 0)
    cos128 = np.concatenate([cos64, cos64], 0)
    sin128 = np.concatenate([sins64, sins64], 0)
    rope = np.stack([cos128, sin128, cos128 * np.float32(0.125), sin128 * np.float32(0.125)]).astype(np.float32)
    return np.ascontiguousarray(rope)


def ret_tables():
    i = np.arange(128, dtype=np.float64)
    tab = np.zeros((128, 1024), np.float64)
    for h in range(4):
        lg = np.log1p(-(2.0 ** (-5.0 - h)))
        diff = i[None, :] - i[:, None]
        tab[:, h * 128:(h + 1) * 128] = np.where(diff >= 0, np.exp(np.maximum(diff, 0.0) * lg), 0.0)
        rp, hh = h // 2, h % 2
        qd = np.exp((i + 1.0) * lg)
        tab[hh * 64:(hh + 1) * 64, 512 + rp * 128:512 + (rp + 1) * 128] = qd[None, :]
        kd = np.exp((127.0 - i) * lg)
        tab[:, 768 + rp * 128 + hh * 64:768 + rp * 128 + (hh + 1) * 64] = kd[:, None]
    return np.ascontiguousarray(tab.astype(np.float32))


def mask_tables():
    ki = np.arange(128)[:, None]
    qi = np.arange(128)[None, :]
    m = np.zeros((128, 640), np.float32)
    m[:, 0:128] = np.where(ki >= qi, 0.0, -30000.0)
    m[:, 128:256] = np.where(ki <= qi, 0.0, -30000.0)
    m[:, 256:384] = np.eye(128, dtype=np.float32)
    m[:, 384:512] = np.where(ki >= qi, 1.0, 0.0)
    m[:, 512:640] = np.where(ki <= qi, 1.0, 0.0)
    return m


def kernel(**inp):
    inp = {k: np.asarray(v) for k, v in inp.items()}
    return run(inp, DEPTH)


def run(inp, n_layers, cores=8, trace=False):
    pc = param_layout(inp)
    prm = pc.build()
    b = Builder(n_layers, pc.off, pc.n)
    nc = b.build()
    x = inp["x"].astype(np.float32)
    f32 = lambda a: np.ascontiguousarray(np.asarray(a, np.float32))
    swp = []
    for (a0, a1) in ((0, 512), (512, 1024), (1536, 1792), (1792, 2048)):
        blk = inp["od_w_in"][:, :, a0:a1]
        nh = (a1 - a0) // 64
        blk = blk.reshape(2, D, nh, 2, 32)[:, :, :, ::-1, :].reshape(2, D, a1 - a0)
        swp.append(blk)
    od_w_sw = f32(np.concatenate(swp, axis=2))
    shared = {
        "prm": prm, "ev_w_in": f32(inp["ev_w_in"]), "ev_w_out": f32(inp["ev_w_out"]),
        "od_w_in": f32(inp["od_w_in"]), "od_w_sw": od_w_sw, "od_w_out": f32(inp["od_w_out"]),
        "ffn_w_up": f32(inp["ffn_w_up"]), "ffn_w_down": f32(inp["ffn_w_down"]),
        "rope": const_tables(), "ctab": ret_tables(), "maskb": mask_tables(),
    }
    in_maps = []
    for c in range(cores):
        m = dict(shared)
        m["xT"] = np.ascontiguousarray(x[c].T)
        in_maps.append(m)
    res = run_bass_kernel_spmd(nc, in_maps, core_ids=list(range(cores)), trace=trace)
    out = np.stack([np.ascontiguousarray(res.results[c]["yT"].T) for c in range(cores)], axis=0)
    if trace:
        return out.astype(np.float32), res
    return out.astype(np.float32)
```

```python
import numpy as np
from contextlib import ExitStack
import concourse.bass as bass
import concourse.mybir as mybir
from concourse.bass_utils import run_bass_kernel_spmd

F32 = mybir.dt.float32
BF16 = mybir.dt.bfloat16
AF = mybir.ActivationFunctionType
ALU = mybir.AluOpType

D = 1024
S_LEN = 4096
TT = 512
NT = S_LEN // TT
DEPTH = 4
DFF = 2816
EPS = 1e-6
SEM_LIMIT = 30000


class T:
    def __init__(self, name="", excl=False):
        self.name = name
        self.w = None
        self.r = {}
        self.excl = excl
        self.dsem = None
        self.dcount = 0


class Sched:
    def __init__(self, nc, ctx):
        self.nc = nc
        self.ctx = ctx
        self.eng = {"pe": nc.tensor, "act": nc.scalar, "dve": nc.vector,
                    "pool": nc.gpsimd, "sp": nc.sync}
        self.sems = {}
        self.semeng = {}
        self.semcnt = {}
        self.nsem = 0
        self.cursem = {}
        self.cnt = {}
        self.pending = {}
        self.waited = {e: {} for e in self.eng}
        self.dma_pool = []
        for e in self.eng:
            self.cursem[e] = self._newsem(e)
            self.cnt[e] = 0
            self.pending[e] = False
        self.ninstr = 0

    def _newsem(self, e):
        key = self.nsem
        self.nsem += 1
        self.sems[key] = self.ctx.enter_context(self.nc.semaphore(f"s{key}"))
        self.semeng[key] = e
        self.semcnt[key] = 0
        return key

    def _deps(self, e, reads, writes):
        deps = {}

        def add(tok):
            if tok is None:
                return
            k, v = tok
            if deps.get(k, 0) < v:
                deps[k] = v

        for t in reads:
            add(t.w)
            if t.excl:
                for k, v in t.r.items():
                    if self.semeng[k] != e:
                        add((k, v))
        for t in writes:
            add(t.w)
            for k, v in t.r.items():
                add((k, v))
        out = []
        for k, v in deps.items():
            if e == "pe" and self.semeng[k] == "pe":
                continue
            if self.waited[e].get(k, 0) >= v:
                continue
            self.waited[e][k] = v
            out.append((k, v))
        return out

    def _emit(self, e, fn, waits, inc):
        eng = self.eng[e]
        for (k, v) in waits[1:]:
            eng.wait_ge(self.sems[k], v)
            self.ninstr += 1
        ins = fn(eng)
        if waits:
            k, v = waits[0]
            ins._wait_ge(self.sems[k], v)
        if inc is not None:
            ins.then_inc(self.sems[inc[0]], inc[1])
            self.semcnt[inc[0]] += inc[1]
        self.ninstr += 1
        return ins

    def op(self, e, fn, reads=(), writes=(), inc=True):
        waits = self._deps(e, reads, writes)
        if inc and not self.pending[e] and self.cnt[e] >= SEM_LIMIT:
            self.cursem[e] = self._newsem(e)
            self.cnt[e] = 0
        sk = self.cursem[e]
        if inc:
            self.cnt[e] += 1
            tok = (sk, self.cnt[e])
            self.pending[e] = False
        else:
            tok = (sk, self.cnt[e] + 1)
            self.pending[e] = True
        self._emit(e, fn, waits, (sk, 1) if inc else None)
        for t in reads:
            if t.r.get(tok[0], 0) < tok[1]:
                t.r[tok[0]] = tok[1]
        for t in writes:
            t.w = tok
            t.r = {}
        return tok

    def dma(self, q, out_ap, in_ap, semtile, reads=(), writes=(), group=False):
        if semtile.dsem is None:
            if self.dma_pool:
                semtile.dsem = self.dma_pool.pop()
                semtile.dcount = self.semcnt[semtile.dsem]
            else:
                semtile.dsem = self._newsem(None)
        sk = semtile.dsem
        saved = []
        if group:
            saved = [(t, t.w) for t in writes if t.w is not None and t.w[0] == sk]
            for t, _ in saved:
                t.w = None
        waits = self._deps(q, reads, writes)
        for t, w in saved:
            t.w = w
        semtile.dcount += 16
        tok = (sk, semtile.dcount)
        self._emit(q, lambda eng: eng.dma_start(out=out_ap, in_=in_ap), waits, (sk, 16))
        for t in reads:
            if t.r.get(sk, 0) < tok[1]:
                t.r[sk] = tok[1]
        for t in writes:
            t.w = tok
            t.r = {}
        return tok

    def barrier(self, release=()):
        toks = []
        for f in self.eng:
            assert not self.pending[f]
            if self.cnt[f] > 0:
                toks.append((self.cursem[f], self.cnt[f]))
        for k, e in self.semeng.items():
            if e is None and self.semcnt[k] > 0:
                toks.append((k, self.semcnt[k]))
        for e in self.eng:
            for (k, v) in toks:
                if self.waited[e].get(k, 0) >= v:
                    continue
                self.waited[e][k] = v
                self.eng[e].wait_ge(self.sems[k], v)
                self.ninstr += 1
        for t in release:
            if t.dsem is not None:
                self.dma_pool.append(t.dsem)
                t.dsem = None


def _cols(v):
    v = np.asarray(v, np.float32)
    return np.ascontiguousarray(v.reshape(-1, 128).T)


def _conv_cols(w):
    w = np.asarray(w, np.float32)
    K, C = w.shape
    return np.ascontiguousarray(w.T.reshape(C // 128, 128, K).transpose(1, 0, 2).reshape(128, -1))


class PCols:
    def __init__(self):
        self.blocks = []
        self.off = {}
        self.n = 0

    def add(self, name, arr):
        self.off[name] = self.n
        self.blocks.append(arr)
        self.n += arr.shape[1]

    def build(self):
        return np.ascontiguousarray(np.concatenate(self.blocks, axis=1))


def param_layout(inp):
    pc = PCols()
    for j in range(2):
        pc.add(f"ev_norm{j}", _cols(inp["ev_norm"][j]))
        pc.add(f"ev_aconv{j}", _conv_cols(inp["ev_a_conv"][j]))
        pc.add(f"ev_aconvb{j}", _cols(inp["ev_a_conv_b"][j]))
        pc.add(f"ev_lng{j}", _cols(inp["ev_a_ln_g"][j]))
        pc.add(f"ev_lnb{j}", _cols(inp["ev_a_ln_b"][j]))
        pc.add(f"ev_bconv{j}", _conv_cols(inp["ev_b_conv"][j]))
        pc.add(f"od_norm{j}", _cols(inp["od_norm"][j]))
    for l in range(DEPTH):
        pc.add(f"ffn_norm{l}", _cols(inp["ffn_norm"][l]))
        pc.add(f"ffn_conv{l}", _conv_cols(inp["ffn_conv"][l]))
        pc.add(f"ffn_convb{l}", _cols(inp["ffn_conv_b"][l]))
    pc.add("final_norm", _cols(inp["final_norm"]))
    return pc


class Builder:
    def __init__(self, n_layers, pc_off, pc_n):
        self.n_layers = n_layers
        self.off = pc_off
        nc = self.nc = bass.Bass("TRN2", target_bir_lowering=False)
        dt = nc.dram_tensor
        self.xin = dt("xT", [D, S_LEN], F32, kind="ExternalInput").ap()
        self.prm_d = dt("prm", [128, pc_n], F32, kind="ExternalInput").ap()
        self.ev_w_in = dt("ev_w_in", [2, D, 2560], F32, kind="ExternalInput").ap()
        self.ev_w_out = dt("ev_w_out", [2, D, D], F32, kind="ExternalInput").ap()
        self.od_w_in = dt("od_w_in", [2, D, 3072], F32, kind="ExternalInput").ap()
        self.od_w_sw = dt("od_w_sw", [2, D, 1536], F32, kind="ExternalInput").ap()
        self.od_w_out = dt("od_w_out", [2, D, D], F32, kind="ExternalInput").ap()
        self.w_up = dt("ffn_w_up", [DEPTH, D, 2 * DFF], F32, kind="ExternalInput").ap()
        self.w_down = dt("ffn_w_down", [DEPTH, DFF, D], F32, kind="ExternalInput").ap()
        self.rope_d = dt("rope", [4, 128, S_LEN], F32, kind="ExternalInput").ap()
        self.ctab_d = dt("ctab", [128, 1024], F32, kind="ExternalInput").ap()
        self.maskb_d = dt("maskb", [128, 640], F32, kind="ExternalInput").ap()
        self.yout = dt("yT", [D, S_LEN], F32, kind="ExternalOutput").ap()
        self.XT = dt("XT_s", [D, S_LEN], F32, kind="Internal").ap()
        self.HT = dt("HT_s", [D, S_LEN], BF16, kind="Internal").ap()
        self.QT = dt("QT_s", [768, S_LEN], BF16, kind="Internal").ap()
        self.KT = dt("KT_s", [768, S_LEN], BF16, kind="Internal").ap()
        self.V = dt("V_s", [S_LEN, 512], BF16, kind="Internal").ap()
        self.RV = dt("RV_s", [S_LEN, 512], BF16, kind="Internal").ap()
        self.SG = dt("SG_s", [512, S_LEN], F32, kind="Internal").ap()
        self.MIXT = dt("MIXT_s", [D, S_LEN], BF16, kind="Internal").ap()
        self.t_XT, self.t_HT, self.t_QT, self.t_KT = T("XT"), T("HT"), T("QT"), T("KT")
        self.t_V, self.t_RV, self.t_SG, self.t_MIXT, self.t_Y = T("V"), T("RV"), T("SG"), T("MIXT"), T("Y")

    def sb(self, ph, name, shape, dtype):
        self._uid = getattr(self, "_uid", 0) + 1
        return ph.enter_context(self.nc.sbuf_tensor(f"{name}_u{self._uid}", shape, dtype))

    def mm(self, bank_t, out_ap, pairs, reads, start=True, stop=True, skip=False):
        n = len(pairs)
        for i, (l, r) in enumerate(pairs):
            st = start and i == 0
            sp_ = stop and i == n - 1
            kw = dict(skip_group_check=True) if skip else {}
            self.S.op("pe", lambda e, l=l, r=r, st=st, sp_=sp_, kw=kw: e.matmul(out_ap, lhsT=l, rhs=r, start=st, stop=sp_, **kw),
                      reads=reads, writes=[bank_t], inc=(i == n - 1))

    def pcol(self, name, c):
        o = self.off[name] + c
        return self.prm[:, o:o + 1]

    def xview(self, dram, t0):
        return dram.rearrange("(c p) t -> p c t", p=128)[:, :, t0:t0 + TT]

    def load_w(self, ph, name, src2d, rows, cols, q="pool"):
        kc = rows // 128
        w = self.sb(ph, name, [128, kc, cols], BF16)
        t = T(name)
        v = src2d.rearrange("(kc p) n -> p kc n", p=128)
        for k in range(kc):
            self.S.dma(q, w[:, k, :], v[:, k, :], t, writes=[t], group=True)
        return w, t

    def load_wg(self, ph, name, src2d, rows, cols, groups, w=None, dst0=0, q="pool"):
        kc = rows // 128
        if w is None:
            w = self.sb(ph, name, [128, kc, cols], BF16)
        v = src2d.rearrange("(kc p) n -> p kc n", p=128)
        tiles = []
        for gi, (c0, c1) in enumerate(groups):
            t = T(f"{name}_g{gi}")
            for k in range(kc):
                self.S.dma(q, w[:, k, dst0 + c0:dst0 + c1], v[:, k, c0:c1], t, writes=[t], group=True)
            tiles.append((c0, c1, t))

        def tile_for(col):
            for (c0, c1, t) in tiles:
                if c0 <= col < c1:
                    return t
            raise KeyError(col)
        return w, tile_for, [t for (_, _, t) in tiles]

    def rms_hT(self, xt, t_xt, hT, t_hT, gname, sq, t_sq, rstd, t_rstd, bank, t_bank):
        S = self.S
        for c in range(8):
            b = c % 2
            S.op("act", lambda e, c=c, b=b: e.activation(out=sq[b][:], in_=xt[:, c, :], func=AF.Square),
                 reads=[t_xt], writes=[t_sq[b]])
            S.op("pe", lambda e, c=c, b=b: e.matmul(bank[:], lhsT=self.ones_f[:], rhs=sq[b][:], start=(c == 0), stop=(c == 7)),
                 reads=[t_sq[b], self.t_const], writes=[t_bank], inc=True)
        S.op("act", lambda e: e.activation(out=rstd[:], in_=bank[:], func=AF.Sqrt, bias=self.eps_c[:], scale=1.0 / D),
             reads=[t_bank, self.t_const], writes=[t_rstd])
        S.op("dve", lambda e: e.reciprocal(out=rstd[:], in_=rstd[:]), reads=[t_rstd], writes=[t_rstd])
        for c in range(8):
            S.op("dve", lambda e, c=c: e.scalar_tensor_tensor(out=hT[:, c, :], in0=xt[:, c, :], scalar=self.pcol(gname, c),
                                                              in1=rstd[:], op0=ALU.mult, op1=ALU.mult),
                 reads=[t_xt, t_rstd, self.t_const], writes=[t_hT[c]])

    def build(self):
        nc = self.nc
        with ExitStack() as ctx:
            S = self.S = Sched(nc, ctx)
            self.prm = nc.alloc_sbuf_tensor("prm_sb", [128, self.prm_d.shape[1]], F32)
            self.ones_f = nc.alloc_sbuf_tensor("ones_f", [128, 128], F32)
            self.eps_c = nc.alloc_sbuf_tensor("eps_c", [128, 1], F32)
            self.t_const = T("const")
            S.dma("sp", self.prm[:], self.prm_d, self.t_const, writes=[self.t_const])
            S.op("pool", lambda e: e.memset(self.ones_f[:], 1.0), writes=[self.t_const])
            S.op("pool", lambda e: e.memset(self.eps_c[:], EPS), writes=[self.t_const])
            self.ctab = nc.alloc_sbuf_tensor("ctab_sb", [128, 1024], F32)
            self.maskb = nc.alloc_sbuf_tensor("maskb_sb", [128, 256], BF16)
            self.ident = nc.alloc_sbuf_tensor("ident_sb", [128, 128], BF16)
            self.ones_b = nc.alloc_sbuf_tensor("ones_b", [128, 64], BF16)
            self.t_cb = T("constb")
            S.dma("sp", self.ctab[:], self.ctab_d, self.t_cb, writes=[self.t_cb])
            S.barrier()
            S.dma("pool", self.maskb[:], self.maskb_d[:, 0:256], self.t_cb, writes=[self.t_cb])
            S.barrier()
            S.dma("pool", self.ident[:], self.maskb_d[:, 256:384], self.t_cb, writes=[self.t_cb])
            S.barrier()
            self.mask01 = nc.alloc_sbuf_tensor("mask01_sb", [128, 256], BF16)
            S.dma("pool", self.mask01[:], self.maskb_d[:, 384:640], self.t_cb, writes=[self.t_cb])
            S.op("pool", lambda e: e.memset(self.ones_b[:], 1.0), writes=[self.t_cb])
            self.ps = [nc.alloc_psum_tensor(f"ps{i}", [128, 512], F32) for i in range(7)]
            self.t_ps = [T(f"ps{i}", excl=True) for i in range(7)]
            self.psb = nc.alloc_psum_tensor("psb", [128, 1024], BF16)
            self.t_psb = T("psb", excl=True)
            S.barrier()

            src = self.xin
            for layer in range(self.n_layers):
                j = layer // 2
                if layer % 2 == 0:
                    self.phase_even(j, src)
                else:
                    self.phase_odd(j)
                src = self.XT
                self.phase_ffn(layer, 0, fuse_j=(j if layer % 2 == 1 else None))
                self.phase_ffn(layer, 1, final=(layer == self.n_layers - 1))
            print("ninstr", S.ninstr, "nsem", S.nsem, flush=True)
        return nc

    def phase_even(self, j, src):
        nc, S = self.nc, self.S
        with ExitStack() as ph:
            w_in, win_t, win_tl = self.load_wg(ph, "ev_win", self.ev_w_in[j], D, 2560,
                                               [(512, 1024), (0, 512), (1536, 2048), (2048, 2560), (1024, 1536)])
            w_out, t_wout = self.load_w(ph, "ev_wout", self.ev_w_out[j], D, D)
            xt = [self.sb(ph, f"xt{i}", [128, 8, TT], F32) for i in range(2)]
            t_xt = [T(f"xt{i}") for i in range(2)]
            hTb = [self.sb(ph, f"hT{i}", [128, 8, TT], BF16) for i in range(2)]
            t_hTb = [[T(f"hT{i}_{c}") for c in range(8)] for i in range(2)]
            sq = [self.sb(ph, f"sq{i}", [128, TT], F32) for i in range(2)]
            t_sq = [T(f"sq{i}") for i in range(2)]
            rstd = self.sb(ph, "rstd", [128, TT], F32)
            t_rstd = T("rstd")
            abuf = [self.sb(ph, f"abuf{c}", [128, 30 + TT], BF16) for c in range(4)]
            dg = self.sb(ph, "dg", [128, 124, 128], BF16)
            t_dg = T("dg")
            t_abuf = [T(f"abuf{c}") for c in range(4)]
            acv = [self.sb(ph, f"acv{c}", [128, TT], F32) for c in range(4)]
            t_acv = [T(f"acv{c}") for c in range(4)]
            acp = [self.sb(ph, f"acp{c}", [128, TT], F32) for c in range(2)]
            t_acp = [T(f"acp{c}") for c in range(2)]
            sig = [self.sb(ph, f"sig{i}", [128, TT], F32) for i in range(2)]
            t_sig = [T(f"sig{i}") for i in range(2)]
            ctmp = [self.sb(ph, f"ctmp{i}", [128, TT], F32) for i in range(3)]
            t_ctmp = [T(f"ctmp{i}") for i in range(3)]
            ktmp = 0
            bbuf = [self.sb(ph, f"bbuf{c}", [128, 2 + TT], F32) for c in range(4)]
            t_bbuf = [T(f"bbuf{c}") for c in range(4)]
            bacc = [self.sb(ph, f"bacc{i}", [128, TT], F32) for i in range(2)]
            t_bacc = [T(f"bacc{i}") for i in range(2)]
            mix = self.sb(ph, "mix", [128, 8, TT], BF16)
            t_mix = [T(f"mix{c}") for c in range(8)]
            mean = self.sb(ph, "mean", [128, TT], F32)
            t_mean = T("mean")
            lrs = self.sb(ph, "lrs", [128, TT], F32)
            t_lrs = T("lrs")
            ps, t_ps = self.ps, self.t_ps
            for c in range(4):
                S.op("pool", lambda e, c=c: e.memset(abuf[c][:, 0:30], 0.0), writes=[t_abuf[c]])
                S.op("pool", lambda e, c=c: e.memset(bbuf[c][:, 0:2], 0.0), writes=[t_bbuf[c]])
            gname = f"ev_norm{j}"
            for c in range(4):
                for k in range(31):
                    wcol = self.off[f"ev_aconv{j}"] + c * 31 + k
                    S.op("dve", lambda e, c=c, k=k, wcol=wcol: e.tensor_scalar(out=dg[:, c * 31 + k, :], in0=self.ident[:],
                                                                              scalar1=self.prm[:, wcol:wcol + 1], scalar2=None, op0=ALU.mult),
                         reads=[self.t_cb, self.t_const], writes=[t_dg])
            bk = 0

            def nb():
                nonlocal bk
                b = 1 + (bk % 6)
                bk += 1
                return b

            def prologue(i):
                S.dma("sp", xt[i % 2][:], self.xview(src, i * TT), t_xt[i % 2], reads=[self.t_XT], writes=[t_xt[i % 2]])
                self.rms_hT(xt[i % 2], t_xt[i % 2], hTb[i % 2], t_hTb[i % 2], gname, sq, t_sq, rstd, t_rstd, ps[0], t_ps[0])

            prologue(0)
            for i in range(NT):
                t0 = i * TT
                x, tx = xt[i % 2], t_xt[i % 2]
                hT, t_hT = hTb[i % 2], t_hTb[i % 2]

                def proj(col0, b):
                    self.mm(t_ps[b], ps[b][:], [(w_in[:, k, col0:col0 + 128], hT[:, k, :]) for k in range(8)],
                            reads=[win_t(col0)] + t_hT)

                for c in range(4):
                    bg, bv = nb(), nb()
                    proj(512 + c * 128, bg)
                    proj(c * 128, bv)
                    sg, tsg = sig[c % 2], t_sig[c % 2]
                    S.op("act", lambda e, bg=bg, sg=sg: e.activation(out=sg[:], in_=ps[bg][:], func=AF.Sigmoid),
                         reads=[t_ps[bg]], writes=[tsg])
                    S.op("dve", lambda e, bv=bv, sg=sg, c=c: e.tensor_tensor(out=abuf[c][:, 30:30 + TT], in0=ps[bv][:], in1=sg[:], op=ALU.mult),
                         reads=[t_ps[bv], tsg], writes=[t_abuf[c]])
                    wc = self.off[f"ev_aconv{j}"] + c * 31
                    bcv = nb()
                    self.mm(t_ps[bcv], ps[bcv][:], [(dg[:, c * 31 + k, :], abuf[c][:, k:k + TT]) for k in range(31)],
                            reads=[t_dg, t_abuf[c]])
                    S.op("act", lambda e, c=c, bcv=bcv: e.activation(out=acv[c][:], in_=ps[bcv][:], func=AF.Identity,
                                                                     bias=self.pcol(f"ev_aconvb{j}", c), scale=1.0),
                         reads=[t_ps[bcv], self.t_const], writes=[t_acv[c]])
                    S.op("pool", lambda e, c=c: e.tensor_copy(out=abuf[c][:, 0:30], in_=abuf[c][:, TT:TT + 30]),
                         reads=[t_abuf[c]], writes=[t_abuf[c]])
                bm, bv2 = nb(), nb()
                for c in range(4):
                    S.op("pe", lambda e, c=c, bm=bm: e.matmul(ps[bm][:], lhsT=self.ones_f[:], rhs=acv[c][:], start=(c == 0), stop=(c == 3)),
                         reads=[t_acv[c], self.t_const], writes=[t_ps[bm]])
                for c in range(4):
                    b = c % 2
                    S.op("act", lambda e, c=c, b=b: e.activation(out=sq[b][:], in_=acv[c][:], func=AF.Square),
                         reads=[t_acv[c]], writes=[t_sq[b]])
                    S.op("pe", lambda e, c=c, b=b, bv2=bv2: e.matmul(ps[bv2][:], lhsT=self.ones_f[:], rhs=sq[b][:], start=(c == 0), stop=(c == 3)),
                         reads=[t_sq[b], self.t_const], writes=[t_ps[bv2]])
                S.op("act", lambda e, bm=bm: e.activation(out=mean[:], in_=ps[bm][:], func=AF.Identity, scale=1.0 / 512),
                     reads=[t_ps[bm]], writes=[t_mean])
                S.op("act", lambda e: e.activation(out=lrs[:], in_=mean[:], func=AF.Square), reads=[t_mean], writes=[t_lrs])
                S.op("dve", lambda e, bv2=bv2: e.scalar_tensor_tensor(out=lrs[:], in0=ps[bv2][:], scalar=1.0 / 512, in1=lrs[:],
                                                                      op0=ALU.mult, op1=ALU.subtract),
                     reads=[t_ps[bv2], t_lrs], writes=[t_lrs])
                S.op("act", lambda e: e.activation(out=lrs[:], in_=lrs[:], func=AF.Sqrt, bias=self.eps_c[:], scale=1.0),
                     reads=[t_lrs, self.t_const], writes=[t_lrs])
                S.op("dve", lambda e: e.reciprocal(out=lrs[:], in_=lrs[:]), reads=[t_lrs], writes=[t_lrs])
                for c in range(4):
                    S.op("dve", lambda e, c=c: e.tensor_tensor(out=acv[c][:], in0=acv[c][:], in1=mean[:], op=ALU.subtract),
                         reads=[t_acv[c], t_mean], writes=[t_acv[c]])
                    S.op("dve", lambda e, c=c: e.tensor_tensor(out=acv[c][:], in0=acv[c][:], in1=lrs[:], op=ALU.mult),
                         reads=[t_acv[c], t_lrs], writes=[t_acv[c]])
                    S.op("act", lambda e, c=c: e.activation(out=mix[:, c, :], in_=acv[c][:], func=AF.Silu,
                                                            bias=self.pcol(f"ev_lnb{j}", c), scale=self.pcol(f"ev_lng{j}", c)),
                         reads=[t_acv[c], self.t_const], writes=[t_mix[c]])
                for c in range(4):
                    bc_, bh_, bb_ = nb(), nb(), nb()
                    proj(1536 + c * 128, bc_)
                    proj(2048 + c * 128, bh_)
                    proj(1024 + c * 128, bb_)
                    sg, tsg = sig[c % 2], t_sig[c % 2]
                    S.op("act", lambda e, bh_=bh_, sg=sg: e.activation(out=sg[:], in_=ps[bh_][:], func=AF.Copy),
                         reads=[t_ps[bh_]], writes=[tsg])
                    S.op("dve", lambda e, bc_=bc_, sg=sg, c=c: e.tensor_tensor(out=bbuf[c][:, 2:2 + TT], in0=ps[bc_][:], in1=sg[:], op=ALU.mult),
                         reads=[t_ps[bc_], tsg], writes=[t_bbuf[c]])
                    wc = self.off[f"ev_bconv{j}"] + c * 3
                    ba, tba = bacc[c % 2], t_bacc[c % 2]
                    S.op("act", lambda e, c=c, wc=wc, ba=ba: e.activation(out=ba[:], in_=bbuf[c][:, 2:2 + TT], func=AF.Identity,
                                                                          scale=self.prm[:, wc + 2:wc + 3]),
                         reads=[t_bbuf[c], self.t_const], writes=[tba])
                    for k in range(2):
                        S.op("dve", lambda e, c=c, wc=wc, k=k, ba=ba: e.scalar_tensor_tensor(
                            out=ba[:], in0=bbuf[c][:, k:k + TT], scalar=self.prm[:, wc + k:wc + k + 1], in1=ba[:],
                            op0=ALU.mult, op1=ALU.add), reads=[t_bbuf[c], tba, self.t_const], writes=[tba])
                    S.op("dve", lambda e, c=c, bb_=bb_, ba=ba: e.tensor_tensor(out=mix[:, 4 + c, :], in0=ps[bb_][:], in1=ba[:], op=ALU.mult),
                         reads=[t_ps[bb_], tba], writes=[t_mix[4 + c]])
                    S.op("pool", lambda e, c=c: e.tensor_copy(out=bbuf[c][:, 0:2], in_=bbuf[c][:, TT:TT + 2]),
                         reads=[t_bbuf[c]], writes=[t_bbuf[c]])
                if i + 1 < NT:
                    prologue(i + 1)
                for oc in range(8):
                    b = nb()
                    self.mm(t_ps[b], ps[b][:], [(w_out[:, k, oc * 128:(oc + 1) * 128], mix[:, k, :]) for k in range(8)],
                            reads=[t_wout] + t_mix)
                    S.op("dve", lambda e, oc=oc, b=b, x=x: e.tensor_tensor(out=x[:, oc, :], in0=ps[b][:], in1=x[:, oc, :], op=ALU.add),
                         reads=[t_ps[b], tx], writes=[tx])
                S.dma("sp", self.xview(self.XT, t0), x[:], tx, reads=[tx], writes=[self.t_XT])
            S.barrier(release=win_tl + [t_wout] + t_xt)

    def phase_ffn(self, layer, hf, final=False, fuse_j=None):
        nc, S = self.nc, self.S
        HC = 11
        with ExitStack() as ph:
            wu = self.sb(ph, "wu", [128, 8, 2 * HC * 128], BF16)
            upv = self.w_up[layer].rearrange("(kc p) n -> p kc n", p=128)
            GP = ((0, 2), (2, 5), (5, 8), (8, 11))
            t_wug = [T(f"wu_g{gi}") for gi in range(len(GP))]
            for gi, (p0, p1) in enumerate(GP):
                for k in range(8):
                    S.dma("pool", wu[:, k, p0 * 128:p1 * 128], upv[:, k, (hf * HC + p0) * 128:(hf * HC + p1) * 128], t_wug[gi],
                          writes=[t_wug[gi]], group=True)
                    S.dma("pool", wu[:, k, (HC + p0) * 128:(HC + p1) * 128], upv[:, k, DFF + (hf * HC + p0) * 128:DFF + (hf * HC + p1) * 128],
                          t_wug[gi], writes=[t_wug[gi]], group=True)

            def t_wu_for(pc_):
                for gi, (p0, p1) in enumerate(GP):
                    if p0 <= pc_ < p1:
                        return t_wug[gi]
            wd, t_wd = self.load_w(ph, "wd", self.w_down[layer][hf * HC * 128:(hf + 1) * HC * 128, :], HC * 128, D)
            xt = [self.sb(ph, f"xt{i}", [128, 8, TT], F32) for i in range(2)]
            t_xt = [T(f"xt{i}") for i in range(2)]
            hTb = [self.sb(ph, f"hT{i}", [128, 8, TT], BF16) for i in range(2)]
            t_hTb = [[T(f"hT{i}_{c}") for c in range(8)] for i in range(2)]
            t_hTall = [T(f"hTall{i}") for i in range(2)]
            sq8 = self.sb(ph, "sq8", [128, 8, TT], F32)
            t_sq8 = [T(f"sq8_{c}") for c in range(8)]
            rstd = self.sb(ph, "rstd", [128, TT], F32)
            t_rstd = T("rstd")
            NU = 4
            ug = [self.sb(ph, f"ug{i}", [128, TT], F32) for i in range(NU)]
            t_ug = [T(f"ug{i}") for i in range(NU)]
            uu = [self.sb(ph, f"uu{i}", [128, TT], F32) for i in range(NU)]
            t_uu = [T(f"uu{i}") for i in range(NU)]
            Bg = [self.sb(ph, f"Bg{i}", [128, TT], F32) for i in range(3)]
            t_Bg = [T(f"Bg{i}") for i in range(3)]
            hal = self.sb(ph, "hal", [128, 2, 2 * HC, 2], F32)
            t_hal = [[T(f"hal{p}_{c}") for c in range(2 * HC)] for p in range(2)]
            g = self.sb(ph, "g", [128, HC, TT], BF16)
            t_g = [T(f"g{c}") for c in range(HC)]
            if fuse_j is not None:
                w_out, t_wout = self.load_w(ph, "od_wout", self.od_w_out[fuse_j], D, D)
                mx = self.sb(ph, "mx", [128, 8, TT], BF16)
                t_mx = T("mx")
            if final:
                yo = self.sb(ph, "yo", [128, 8, TT], F32)
                t_yo = [T(f"yo{c}") for c in range(8)]
                t_yoall = T("yoall")
                rstd_f = self.sb(ph, "rstd_f", [128, TT], F32)
                t_rstd_f = T("rstd_f")
            ps, t_ps = self.ps, self.t_ps
            S.op("pool", lambda e: e.memset(hal[:], 0.0), writes=t_hal[0] + t_hal[1])
            gname = f"ffn_norm{layer}"
            cw = self.off[f"ffn_conv{layer}"]
            cb = self.off[f"ffn_convb{layer}"]
            bk = 0

            def nb():
                nonlocal bk
                b = 1 + (bk % 6)
                bk += 1
                return b

            def pro_load(i):
                t0 = i * TT
                x, tx = xt[i % 2], t_xt[i % 2]
                S.dma("sp", x[:], self.xview(self.XT, t0), tx, reads=[self.t_XT], writes=[tx])
                if hf == 1:
                    S.dma("sp", hTb[i % 2][:], self.xview(self.HT, t0), t_hTall[i % 2], reads=[self.t_HT], writes=[t_hTall[i % 2]])
                if fuse_j is not None:
                    S.dma("sp", mx[:], self.xview(self.MIXT, t0), t_mx, reads=[self.t_MIXT], writes=[t_mx])

            def pro_out(i):
                if fuse_j is None:
                    return
                x, tx = xt[i % 2], t_xt[i % 2]
                for oc in range(8):
                    b = nb()
                    self.mm(t_ps[b], ps[b][:], [(w_out[:, k, oc * 128:(oc + 1) * 128], mx[:, k, :]) for k in range(8)], reads=[t_wout, t_mx])
                    S.op("dve", lambda e, oc=oc, b=b, x=x: e.tensor_tensor(out=x[:, oc, :], in0=ps[b][:], in1=x[:, oc, :], op=ALU.add),
                         reads=[t_ps[b], tx], writes=[tx])

            def pro_sq(i):
                if hf == 1:
                    return
                x, tx = xt[i % 2], t_xt[i % 2]
                for c in range(8):
                    S.op("pool", lambda e, c=c, x=x: e.tensor_tensor(out=sq8[:, c, :], in0=x[:, c, :], in1=x[:, c, :], op=ALU.mult),
                         reads=[tx], writes=[t_sq8[c]])

            def pro_stat(i):
                if hf == 1:
                    return
                for c in range(8):
                    S.op("pe", lambda e, c=c: e.matmul(ps[0][:], lhsT=self.ones_f[:], rhs=sq8[:, c, :], start=(c == 0), stop=(c == 7)),
                         reads=[t_sq8[c], self.t_const], writes=[t_ps[0]], inc=(c == 7))
                S.op("act", lambda e: e.activation(out=rstd[:], in_=ps[0][:], func=AF.Sqrt, bias=self.eps_c[:], scale=1.0 / D),
                     reads=[t_ps[0], self.t_const], writes=[t_rstd])

            def pro_norm(i):
                if hf == 1:
                    return
                t0 = i * TT
                x, tx = xt[i % 2], t_xt[i % 2]
                h_, th_ = hTb[i % 2], t_hTb[i % 2]
                S.op("dve", lambda e: e.reciprocal(out=rstd[:], in_=rstd[:]), reads=[t_rstd], writes=[t_rstd])
                for c in range(8):
                    S.op("dve", lambda e, c=c, x=x, h_=h_: e.scalar_tensor_tensor(out=h_[:, c, :], in0=x[:, c, :], scalar=self.pcol(gname, c),
                                                                                  in1=rstd[:], op0=ALU.mult, op1=ALU.mult),
                         reads=[tx, t_rstd, self.t_const], writes=[th_[c]])
                S.dma("sp", self.xview(self.HT, t0), h_[:], t_hTall[i % 2], reads=th_, writes=[self.t_HT])

            def evac_back(i, pc_):
                u_g, tg_ = ug[pc_ % NU], t_ug[pc_ % NU]
                u_u, tu_ = uu[pc_ % NU], t_uu[pc_ % NU]
                S.op("act", lambda e, u_g=u_g: e.activation(out=u_g[:], in_=u_g[:], func=AF.Silu), reads=[tg_], writes=[tg_])
                S.op("pool", lambda e, u_g=u_g, u_u=u_u, pc_=pc_: e.tensor_tensor(out=g[:, pc_, :], in0=u_g[:], in1=u_u[:], op=ALU.mult),
                     reads=[tg_, tu_], writes=[t_g[pc_]])

            def up_pair(i, pc_):
                hT = hTb[i % 2]
                hreads = t_hTb[i % 2] if hf == 0 else [t_hTall[i % 2]]
                hp_, hn_ = i % 2, (i + 1) % 2
                banks = []
                for role in range(2):
                    b = nb()
                    ch = role * HC + pc_
                    self.mm(t_ps[b], ps[b][:], [(wu[:, k, ch * 128:(ch + 1) * 128], hT[:, k, :]) for k in range(8)],
                            reads=[t_wu_for(pc_)] + hreads)
                    banks.append(b)
                for role in range(2):
                    b = banks[role]
                    ch = role * HC + pc_
                    gch = role * 22 + hf * HC + pc_
                    u = (ug if role == 0 else uu)[pc_ % NU]
                    tu = (t_ug if role == 0 else t_uu)[pc_ % NU]
                    w0 = self.prm[:, cw + gch * 3:cw + gch * 3 + 1]
                    w1 = self.prm[:, cw + gch * 3 + 1:cw + gch * 3 + 2]
                    w2 = self.prm[:, cw + gch * 3 + 2:cw + gch * 3 + 3]
                    bias = self.prm[:, cb + gch:cb + gch + 1]
                    S.op("act", lambda e, b=b, ch=ch: e.activation(out=hal[:, hn_, ch, :], in_=ps[b][:, TT - 2:TT], func=AF.Copy),
                         reads=[t_ps[b]], writes=[t_hal[hn_][ch]])
                    S.op("act", lambda e, u=u, b=b, w2=w2, bias=bias: e.activation(out=u[:], in_=ps[b][:], func=AF.Identity, bias=bias, scale=w2),
                         reads=[t_ps[b], self.t_const], writes=[tu])
                    if role == 0:
                        bg_, tbg_ = Bg[pc_ % 3], t_Bg[pc_ % 3]
                        S.op("act", lambda e, bg_=bg_, b=b, w1=w1: e.activation(out=bg_[:], in_=ps[b][:], func=AF.Identity, scale=w1),
                             reads=[t_ps[b], self.t_const], writes=[tbg_])
                    else:
                        S.op("dve", lambda e, u=u, b=b, w1=w1: e.scalar_tensor_tensor(out=u[:, 1:TT], in0=ps[b][:, 0:TT - 1], scalar=w1, in1=u[:, 1:TT],
                                                                                     op0=ALU.mult, op1=ALU.add),
                             reads=[t_ps[b], tu, self.t_const], writes=[tu])
                    S.op("dve", lambda e, u=u, b=b, w0=w0: e.scalar_tensor_tensor(out=u[:, 2:TT], in0=ps[b][:, 0:TT - 2], scalar=w0, in1=u[:, 2:TT],
                                                                                 op0=ALU.mult, op1=ALU.add),
                         reads=[t_ps[b], tu, self.t_const], writes=[tu])
                    S.op("dve", lambda e, u=u, ch=ch, w0=w0: e.scalar_tensor_tensor(out=u[:, 0:2], in0=hal[:, hp_, ch, 0:2], scalar=w0, in1=u[:, 0:2],
                                                                                   op0=ALU.mult, op1=ALU.add),
                         reads=[t_hal[hp_][ch], tu, self.t_const], writes=[tu])
                    S.op("dve", lambda e, u=u, ch=ch, w1=w1: e.scalar_tensor_tensor(out=u[:, 0:1], in0=hal[:, hp_, ch, 1:2], scalar=w1, in1=u[:, 0:1],
                                                                                   op0=ALU.mult, op1=ALU.add),
                         reads=[t_hal[hp_][ch], tu, self.t_const], writes=[tu])
                    if role == 0:
                        S.op("pool", lambda e, u=u, bg_=bg_: e.tensor_tensor(out=u[:, 1:TT], in0=u[:, 1:TT], in1=bg_[:, 0:TT - 1], op=ALU.add),
                             reads=[tu, tbg_], writes=[tu])
                if pc_ >= 2:
                    evac_back(i, pc_ - 2)
                if i + 1 < NT:
                    if pc_ == 2:
                        pro_load(i + 1)
                    elif pc_ == 3:
                        pro_out(i + 1)
                    elif pc_ == 5:
                        pro_sq(i + 1)

            LEAD = 0 if hf == 0 else 2
            pro_load(0)
            pro_out(0)
            pro_sq(0)
            pro_stat(0)
            pro_norm(0)
            for i in range(NT):
                t0 = i * TT
                x, tx = xt[i % 2], t_xt[i % 2]
                for pc_ in range(LEAD if i > 0 else 0, HC):
                    up_pair(i, pc_)
                if i + 1 < NT:
                    pro_stat(i + 1)
                evac_back(i, HC - 2)
                evac_back(i, HC - 1)
                if i + 1 < NT:
                    pro_norm(i + 1)
                    for pc_ in range(LEAD):
                        up_pair(i + 1, pc_)
                for oc in range(8):
                    b = nb()
                    self.mm(t_ps[b], ps[b][:], [(wd[:, k, oc * 128:(oc + 1) * 128], g[:, k, :]) for k in range(HC)],
                            reads=[t_wd] + t_g)
                    S.op("dve", lambda e, oc=oc, b=b, x=x: e.tensor_tensor(out=x[:, oc, :], in0=ps[b][:], in1=x[:, oc, :], op=ALU.add),
                         reads=[t_ps[b], tx], writes=[tx])
                if not final:
                    S.dma("sp", self.xview(self.XT, t0), x[:], tx, reads=[tx], writes=[self.t_XT])
                else:
                    for c in range(8):
                        S.op("pool", lambda e, c=c, x=x: e.tensor_tensor(out=sq8[:, c, :], in0=x[:, c, :], in1=x[:, c, :], op=ALU.mult),
                             reads=[tx], writes=[t_sq8[c]])
                    for c in range(8):
                        S.op("pe", lambda e, c=c: e.matmul(ps[0][:], lhsT=self.ones_f[:], rhs=sq8[:, c, :], start=(c == 0), stop=(c == 7)),
                             reads=[t_sq8[c], self.t_const], writes=[t_ps[0]], inc=(c == 7))
                    S.op("act", lambda e: e.activation(out=rstd_f[:], in_=ps[0][:], func=AF.Sqrt, bias=self.eps_c[:], scale=1.0 / D),
                         reads=[t_ps[0], self.t_const], writes=[t_rstd_f])
                    S.op("dve", lambda e: e.reciprocal(out=rstd_f[:], in_=rstd_f[:]), reads=[t_rstd_f], writes=[t_rstd_f])
                    for c in range(8):
                        S.op("dve", lambda e, c=c, x=x: e.scalar_tensor_tensor(out=yo[:, c, :], in0=x[:, c, :], scalar=self.pcol("final_norm", c),
                                                                              in1=rstd_f[:], op0=ALU.mult, op1=ALU.mult),
                             reads=[tx, t_rstd_f, self.t_const], writes=[t_yo[c]])
                    S.dma("sp", self.xview(self.yout, t0), yo[:], t_yoall, reads=t_yo, writes=[self.t_Y])
            S.barrier(release=t_wug + [t_wd] + t_hTall + t_xt + ([t_yoall] if final else []) + ([t_wout, t_mx] if fuse_j is not None else []))

    def phase_final(self, src):
        nc, S = self.nc, self.S
        with ExitStack() as ph:
            xt = [self.sb(ph, f"xt{i}", [128, 8, TT], F32) for i in range(2)]
            t_xt = [T(f"xt{i}") for i in range(2)]
            yo = [self.sb(ph, f"yo{i}", [128, 8, TT], F32) for i in range(2)]
            t_yo = [[T(f"yo{i}_{c}") for c in range(8)] for i in range(2)]
            t_yoall = [T(f"yoall{i}") for i in range(2)]
            sq = [self.sb(ph, f"sq{i}", [128, TT], F32) for i in range(2)]
            t_sq = [T(f"sq{i}") for i in range(2)]
            rstd = self.sb(ph, "rstd", [128, TT], F32)
            t_rstd = T("rstd")
            for i in range(NT):
                t0 = i * TT
                x, tx = xt[i % 2], t_xt[i % 2]
                S.dma("sp", x[:], self.xview(src, t0), tx, reads=[self.t_XT], writes=[tx])
                self.rms_hT(x, tx, yo[i % 2], t_yo[i % 2], "final_norm", sq, t_sq, rstd, t_rstd, self.ps[0], self.t_ps[0])
                S.dma("sp", self.xview(self.yout, t0), yo[i % 2][:], t_yoall[i % 2], reads=t_yo[i % 2], writes=[self.t_Y])
            S.barrier(release=t_xt + t_yoall)

    def phase_odd(self, j):
        self.odd_proj(j)
        self.odd_attn_all()
        self.odd_ret_out(j)

    def odd_proj(self, j):
        nc, S = self.nc, self.S
        with ExitStack() as ph:
            w_in = self.sb(ph, "od_win", [128, 8, 3072], BF16)
            w_sw = self.sb(ph, "od_wsw", [128, 8, 1536], BF16)
            _, win_a, tl_a = self.load_wg(ph, "od_win_a", self.od_w_in[j], D, 3072, [(0, 512)], w=w_in)
            _, wsw_a, tl_b = self.load_wg(ph, "od_wsw_a", self.od_w_sw[j], D, 1536, [(0, 512)], w=w_sw)
            _, win_b, tl_c = self.load_wg(ph, "od_win_b", self.od_w_in[j], D, 3072, [(512, 1024)], w=w_in)
            _, wsw_b, tl_d = self.load_wg(ph, "od_wsw_b", self.od_w_sw[j], D, 1536, [(512, 1024)], w=w_sw)
            _, win_c, tl_e = self.load_wg(ph, "od_win_c", self.od_w_in[j], D, 3072, [(1536, 2048)], w=w_in)
            _, wsw_c, tl_f = self.load_wg(ph, "od_wsw_c", self.od_w_sw[j], D, 1536, [(1024, 1536)], w=w_sw)
            _, win_d, tl_g = self.load_wg(ph, "od_win_d", self.od_w_in[j], D, 3072, [(1024, 1536), (2048, 2560), (2560, 3072)], w=w_in)
            wtl = tl_a + tl_b + tl_c + tl_d + tl_e + tl_f + tl_g

            def t_win(col):
                for f in (win_a, win_b, win_c, win_d):
                    try:
                        return f(col)
                    except KeyError:
                        pass

            def t_wsw(col):
                for f in (wsw_a, wsw_b, wsw_c):
                    try:
                        return f(col)
                    except KeyError:
                        pass
            xt = [self.sb(ph, f"xt{i}", [128, 8, TT], F32) for i in range(2)]
            t_xt = [T(f"xt{i}") for i in range(2)]
            hTb = [self.sb(ph, f"hT{i}", [128, 8, TT], BF16) for i in range(2)]
            t_hTb = [[T(f"hT{i}_{c}") for c in range(8)] for i in range(2)]
            sq = [self.sb(ph, f"sq{i}", [128, TT], F32) for i in range(2)]
            t_sq = [T(f"sq{i}") for i in range(2)]
            rstd = self.sb(ph, "rstd", [128, TT], F32)
            t_rstd = T("rstd")
            rp_ = [self.sb(ph, f"rope{i}", [128, 4, TT], F32) for i in range(2)]
            t_rp = [T(f"rope{i}") for i in range(2)]
            t1 = [self.sb(ph, f"t1_{i}", [128, TT], F32) for i in range(2)]
            t_t1 = [T(f"t1_{i}") for i in range(2)]
            t2 = [self.sb(ph, f"t2_{i}", [128, TT], F32) for i in range(2)]
            t_t2 = [T(f"t2_{i}") for i in range(2)]
            ob = [self.sb(ph, f"ob{i}", [128, TT], BF16) for i in range(4)]
            t_ob = [T(f"ob{i}") for i in range(4)]
            vb = [self.sb(ph, f"vb{i}", [128, 4, 512], BF16) for i in range(2)]
            t_vb = [T(f"vb{i}") for i in range(2)]
            sgo = [self.sb(ph, f"sgo{i}", [128, TT], F32) for i in range(2)]
            t_sgo = [T(f"sgo{i}") for i in range(2)]
            ps, t_ps = self.ps, self.t_ps
            gname = f"od_norm{j}"
            bk = 0
            cnt = 0

            def nb():
                nonlocal bk
                b = 1 + (bk % 6)
                bk += 1
                return b

            qk = []
            for c in range(4):
                qk.append((c * 128, c * 128, self.QT, self.t_QT, c * 128, True))
            for c in range(4):
                qk.append((512 + c * 128, 512 + c * 128, self.KT, self.t_KT, c * 128, False))
            for c in range(2):
                qk.append((1536 + c * 128, 1024 + c * 128, self.QT, self.t_QT, 512 + c * 128, False))
            for c in range(2):
                qk.append((1792 + c * 128, 1280 + c * 128, self.KT, self.t_KT, 512 + c * 128, True))

            def prologue(i):
                S.dma("sp", xt[i % 2][:], self.xview(self.XT, i * TT), t_xt[i % 2], reads=[self.t_XT], writes=[t_xt[i % 2]])
                S.dma("sp", rp_[i % 2][:], self.rope_d.rearrange("a p t -> p a t")[:, :, i * TT:(i + 1) * TT], t_rp[i % 2], writes=[t_rp[i % 2]])
                self.rms_hT(xt[i % 2], t_xt[i % 2], hTb[i % 2], t_hTb[i % 2], gname, sq, t_sq, rstd, t_rstd, ps[0], t_ps[0])

            prologue(0)
            for i in range(NT):
                t0 = i * TT
                x, tx = xt[i % 2], t_xt[i % 2]
                rt, trt = rp_[i % 2], t_rp[i % 2]
                hT, t_hT = hTb[i % 2], t_hTb[i % 2]
                for (mc, sc, dst, tdst, row0, scaled) in qk:
                    ba, bb = nb(), nb()
                    self.mm(t_ps[ba], ps[ba][:], [(w_in[:, k, mc:mc + 128], hT[:, k, :]) for k in range(8)], reads=[t_win(mc)] + t_hT)
                    self.mm(t_ps[bb], ps[bb][:], [(w_sw[:, k, sc:sc + 128], hT[:, k, :]) for k in range(8)], reads=[t_wsw(sc)] + t_hT)
                    ci, si = (2, 3) if scaled else (0, 1)
                    a1, ta1 = t1[cnt % 2], t_t1[cnt % 2]
                    a2, ta2 = t2[cnt % 2], t_t2[cnt % 2]
                    o, to = ob[cnt % 4], t_ob[cnt % 4]
                    cnt += 1
                    S.op("dve", lambda e, a1=a1, ba=ba, ci=ci, rt=rt: e.tensor_tensor(out=a1[:], in0=ps[ba][:], in1=rt[:, ci, :], op=ALU.mult),
                         reads=[t_ps[ba], trt], writes=[ta1])
                    S.op("dve", lambda e, a2=a2, bb=bb, si=si, rt=rt: e.tensor_tensor(out=a2[:], in0=ps[bb][:], in1=rt[:, si, :], op=ALU.mult),
                         reads=[t_ps[bb], trt], writes=[ta2])
                    S.op("pool", lambda e, a1=a1, a2=a2, o=o: e.tensor_tensor(out=o[:], in0=a1[:], in1=a2[:], op=ALU.add),
                         reads=[ta1, ta2], writes=[to])
                    S.dma("sp", dst[row0:row0 + 128, t0:t0 + TT], o[:], to, reads=[to], writes=[tdst])
                if i + 1 < NT:
                    prologue(i + 1)
                for vi, (c0, dst, tdst) in enumerate(((1024, self.V, self.t_V), (2048, self.RV, self.t_RV))):
                    v, tv = vb[vi], t_vb[vi]
                    for ts in range(4):
                        b = nb()
                        self.mm(t_ps[b], ps[b][:], [(hT[:, k, ts * 128:(ts + 1) * 128], w_in[:, k, c0:c0 + 512]) for k in range(8)],
                                reads=[t_win(c0)] + t_hT)
                        S.op("act", lambda e, v=v, ts=ts, b=b: e.activation(out=v[:, ts, :], in_=ps[b][:], func=AF.Copy),
                             reads=[t_ps[b]], writes=[tv])
                    S.dma("sp", dst[t0:t0 + TT, :].rearrange("(ts p) f -> p ts f", p=128), v[:], tv, reads=[tv], writes=[tdst])
                for c in range(4):
                    b = nb()
                    self.mm(t_ps[b], ps[b][:], [(w_in[:, k, 2560 + c * 128:2560 + (c + 1) * 128], hT[:, k, :]) for k in range(8)],
                            reads=[t_win(2560)] + t_hT)
                    sg, tsg = sgo[c % 2], t_sgo[c % 2]
                    S.op("act", lambda e, sg=sg, b=b: e.activation(out=sg[:], in_=ps[b][:], func=AF.Silu), reads=[t_ps[b]], writes=[tsg])
                    S.dma("sp", self.SG[c * 128:(c + 1) * 128, t0:t0 + TT], sg[:], tsg, reads=[tsg], writes=[self.t_SG])
            S.barrier(release=wtl + t_xt + t_rp + t_ob + t_vb + t_sgo)

    def odd_attn_all(self):
        nc, S = self.nc, self.S
        DS = (1, 4, 16)
        with ExitStack() as ph:
            qN = [self.sb(ph, f"qN{i}", [128, S_LEN], BF16) for i in range(2)]
            kN = [self.sb(ph, f"kN{i}", [128, S_LEN], BF16) for i in range(2)]
            t_qN = [T(f"qN{i}") for i in range(2)]
            t_kN = [T(f"kN{i}") for i in range(2)]
            qP = [self.sb(ph, f"qP16_{i}", [128, S_LEN], BF16) for i in range(2)]
            kP = [self.sb(ph, f"kP16_{i}", [128, S_LEN], BF16) for i in range(2)]
            t_qP = [[T(f"qP16_{i}_{k}") for k in range(4)] for i in range(2)]
            t_kP = [[T(f"kP16_{i}_{k}") for k in range(4)] for i in range(2)]
            vt = [{d: self.sb(ph, f"vt{i}_{d}", [128, 32, 128], BF16) for d in DS} for i in range(2)]
            t_vt = [{d: T(f"vt{i}_{d}") for d in DS} for i in range(2)]
            acc = self.sb(ph, "accnd", [64, 2, S_LEN], F32)
            t_acc = T("acc")
            pt = [self.sb(ph, f"pt{i}", [128, 256], BF16) for i in range(6)]
            t_pt = [T(f"pt{i}") for i in range(6)]
            cout = self.sb(ph, "cout", [64, S_LEN], BF16)
            t_cout = T("cout")
            rec = [self.sb(ph, f"rec{i}", [64, TT], F32) for i in range(2)]
            t_rec = [T(f"rec{i}") for i in range(2)]
            ps, t_ps = self.ps, self.t_ps

            def load(hp):
                i = hp % 2
                S.dma("sp", qN[i][:], self.QT[hp * 128:(hp + 1) * 128, :], t_qN[i], reads=[self.t_QT], writes=[t_qN[i]])
                S.dma("sp", kN[i][:], self.KT[hp * 128:(hp + 1) * 128, :], t_kN[i], reads=[self.t_KT], writes=[t_kN[i]])
                for d in DS:
                    nbk = 32 // d
                    vsrc = self.V[:, hp * 128:(hp + 1) * 128].rearrange("(n p r) f -> r p n f", p=128, r=d)
                    for r in range(d):
                        S.dma("sp", vt[i][d][:, r * nbk:(r + 1) * nbk, :], vsrc[r], t_vt[i][d], reads=[self.t_V], writes=[t_vt[i][d]], group=True)

            def perm_piece(hp, k):
                i = hp % 2
                src, tsrc, dst, tdst = (qN[i], t_qN[i], qP[i], t_qP[i]) if k < 4 else (kN[i], t_kN[i], kP[i], t_kP[i])
                kk = k % 4
                S.op("act", lambda e, src=src, dst=dst, kk=kk: e.activation(
                    out=dst[:].rearrange("p (r m) -> p r m", r=16)[:, :, kk * 64:(kk + 1) * 64],
                    in_=src[:, kk * 1024:(kk + 1) * 1024].rearrange("p (m r) -> p r m", r=16), func=AF.Copy),
                    reads=[tsrc], writes=[tdst[kk]])

            load(0)
            for k in range(8):
                perm_piece(0, k)
            gblk = 0
            gpair = 0
            for hp in range(4):
                pi = hp % 2
                if hp + 1 < 4:
                    load(hp + 1)
                pblk = 0
                for hh in range(2):
                    hb = hh * 64
                    blocks = [(d, r, n) for d in DS for r in range(d) for n in range(32 // d)]
                    info = {}

                    def QK(bi):
                        nonlocal gblk, gpair
                        d, r, n = blocks[bi]
                        if d == 1:
                            qa = qN[pi][hb:hb + 64, n * 128:(n + 1) * 128]
                            kc = kN[pi][hb:hb + 64, n * 128:(n + 1) * 128]
                            kp = kN[pi][hb:hb + 64, (n - 1) * 128:n * 128] if n > 0 else None
                            rd = [t_qN[pi], t_kN[pi], self.t_cb]
                        elif d == 4:
                            qv = qN[pi][:].rearrange("p (m r) -> p r m", r=4)
                            kv = kN[pi][:].rearrange("p (m r) -> p r m", r=4)
                            qa = qv[hb:hb + 64, r, n * 128:(n + 1) * 128]
                            kc = kv[hb:hb + 64, r, n * 128:(n + 1) * 128]
                            kp = kv[hb:hb + 64, r, (n - 1) * 128:n * 128] if n > 0 else None
                            rd = [t_qN[pi], t_kN[pi], self.t_cb]
                        else:
                            J0 = r * 256 + n * 128
                            qa = qP[pi][hb:hb + 64, J0:J0 + 128]
                            kc = kP[pi][hb:hb + 64, J0:J0 + 128]
                            kp = kP[pi][hb:hb + 64, J0 - 128:J0] if n > 0 else None
                            rd = t_qP[pi] + t_kP[pi] + [self.t_cb]
                        bs = gblk % 4
                        pbuf = gblk % 6
                        gblk += 1
                        if n % 2 == 0:
                            gpair += 1
                        bo = 4 + gpair % 3
                        info[bi] = (bs, pbuf, bo)
                        lo = 0 if n > 0 else 128
                        pairs = []
                        if n > 0:
                            pairs.append((kp, qa, ps[bs][:, 0:128]))
                        pairs.append((kc, qa, ps[bs][:, 128:256]))
                        for ii, (l, rr, oo) in enumerate(pairs):
                            S.op("pe", lambda e, l=l, rr=rr, oo=oo: e.matmul(oo, lhsT=l, rhs=rr, start=True, stop=True),
                                 reads=rd, writes=[t_ps[bs]], inc=(ii == len(pairs) - 1))

                    def REST(bi):
                        d, r, n = blocks[bi]
                        nbk = 32 // d
                        bs, pbuf, bo = info[bi]
                        p_, tp_ = pt[pbuf], t_pt[pbuf]
                        v_ = vt[pi][d]
                        lo = 0 if n > 0 else 128
                        S.op("act", lambda e, p_=p_, bs=bs, lo=lo: e.activation(out=p_[:, lo:256], in_=ps[bs][:, lo:256], func=AF.Exp),
                             reads=[t_ps[bs]], writes=[tp_])
                        S.op("dve", lambda e, p_=p_, lo=lo: e.tensor_tensor(out=p_[:, lo:256], in0=p_[:, lo:256], in1=self.mask01[:, lo:256], op=ALU.mult),
                             reads=[tp_, self.t_cb], writes=[tp_])
                        vb_ = r * nbk + n
                        c0 = (n % 2) * 128
                        onum = ps[bo][0:64, c0:c0 + 128]
                        oden = ps[bo][0:64, 256 + c0:256 + c0 + 128]
                        pv = []
                        if n > 0:
                            pv.append((v_[:, vb_ - 1, hb:hb + 64], p_[:, 0:128], onum, True, False))
                        pv.append((v_[:, vb_, hb:hb + 64], p_[:, 128:256], onum, n == 0, True))
                        if n > 0:
                            pv.append((self.ones_b[:, 0:64], p_[:, 0:128], oden, True, False))
                        pv.append((self.ones_b[:, 0:64], p_[:, 128:256], oden, n == 0, True))
                        for ii, (l, rr, oo, st, sp_) in enumerate(pv):
                            S.op("pe", lambda e, l=l, rr=rr, oo=oo, st=st, sp_=sp_: e.matmul(
                                oo, lhsT=l, rhs=rr, start=st, stop=sp_, skip_group_check=True),
                                reads=[t_vt[pi][d], tp_, self.t_cb], writes=[t_ps[bo]], inc=(ii == len(pv) - 1))
                        if n % 2 == 1:
                            av = acc[:].rearrange("p a (m r) -> p a r m", r=d)[:, :, r, (n - 1) * 128:(n + 1) * 128]

                            def do_acc(av=av, bo=bo, d=d):
                                if d == 1:
                                    S.op("dve", lambda e: e.tensor_copy(
                                        out=av, in_=ps[bo][0:64, 0:512].rearrange("p (a m) -> p a m", a=2)),
                                        reads=[t_ps[bo]], writes=[t_acc])
                                else:
                                    S.op("dve", lambda e: e.tensor_tensor(
                                        out=av, in0=av, in1=ps[bo][0:64, 0:512].rearrange("p (a m) -> p a m", a=2), op=ALU.add),
                                        reads=[t_ps[bo], t_acc], writes=[t_acc])
                            while pend_acc:
                                pend_acc.pop(0)()
                            pend_acc.append(do_acc)

                    LA = 3
                    pend_acc = []
                    for bi in range(min(LA, len(blocks))):
                        QK(bi)
                    for bi in range(len(blocks)):
                        if bi + LA < len(blocks):
                            QK(bi + LA)
                        REST(bi)
                        pblk += 1
                        if hp + 1 < 4 and pblk % 20 == 0 and pblk // 20 <= 8:
                            perm_piece(hp + 1, pblk // 20 - 1)
                    while pend_acc:
                        pend_acc.pop(0)()
                    for i in range(NT):
                        t0 = i * TT
                        rc, trc = rec[i % 2], t_rec[i % 2]
                        S.op("act", lambda e, rc=rc, t0=t0: e.activation(out=rc[:], in_=acc[:, 1, t0:t0 + TT], func=AF.Ln), reads=[t_acc], writes=[trc])
                        S.op("act", lambda e, rc=rc: e.activation(out=rc[:], in_=rc[:], func=AF.Exp, scale=-1.0), reads=[trc], writes=[trc])
                        S.op("dve", lambda e, rc=rc, t0=t0: e.tensor_tensor(out=cout[:, t0:t0 + TT], in0=acc[:, 0, t0:t0 + TT], in1=rc[:], op=ALU.mult),
                             reads=[t_acc, trc], writes=[t_cout])
                    hg = hp * 2 + hh
                    S.dma("sp", self.MIXT[hg * 64:(hg + 1) * 64, :], cout[:], t_cout, reads=[t_cout], writes=[self.t_MIXT])
            S.barrier(release=t_qN + t_kN + [t_cout] + [t_vt[i][d] for i in range(2) for d in DS])

    def odd_ret_out(self, j):
        nc, S = self.nc, self.S
        NCH = 32
        with ExitStack() as ph:
            rq = self.sb(ph, "rq", [128, S_LEN], BF16)
            rk = self.sb(ph, "rk", [128, S_LEN], BF16)
            rqd = self.sb(ph, "rqd", [128, S_LEN], BF16)
            t_rq, t_rk = T("rq"), T("rk")
            t_rqd = [T(f"rqd{n}") for n in range(NCH)]
            rvt = self.sb(ph, "rvt", [128, NCH, 256], BF16)
            t_rvt = T("rvt")
            kdT = self.sb(ph, "kdT", [128, NCH, 128], BF16)
            t_kdT = [T(f"kdT{n}") for n in range(NCH)]
            yT = [[self.sb(ph, f"yT{r}_{h}", [128, S_LEN], F32) for h in range(2)] for r in range(2)]
            t_yT = [[[T(f"yT{r}_{h}_{i}") for i in range(NT)] for h in range(2)] for r in range(2)]
            Sf = self.sb(ph, "Sf", [128, 128], F32)
            Sb = self.sb(ph, "Sb", [128, 128], BF16)
            t_Sf = [T("Sf0"), T("Sf1")]
            t_Sb = [T("Sb0"), T("Sb1")]
            sm = [self.sb(ph, f"sm{i}", [128, 128], BF16) for i in range(3)]
            t_sm = [T(f"sm{i}") for i in range(3)]
            sq = [self.sb(ph, f"sq{i}", [128, TT], F32) for i in range(2)]
            t_sq = [T(f"sq{i}") for i in range(2)]
            mean2 = [self.sb(ph, f"mean{i}", [128, TT], F32) for i in range(2)]
            t_mean2 = [T(f"mean{i}") for i in range(2)]
            lrs2 = [self.sb(ph, f"lrs{i}", [128, TT], F32) for i in range(2)]
            t_lrs2 = [T(f"lrs{i}") for i in range(2)]
            sgt = [self.sb(ph, f"sgt{i}", [128, TT], F32) for i in range(2)]
            t_sgt = [T(f"sgt{i}") for i in range(2)]
            ro = [self.sb(ph, f"ro{i}", [128, TT], BF16) for i in range(2)]
            t_ro = [T(f"ro{i}") for i in range(2)]
            ps, t_ps = self.ps, self.t_ps
            st = {"bk": 0, "cnt": 0}

            def load_pre(rp):
                S.dma("sp", rq[:], self.QT[512 + rp * 128:512 + (rp + 1) * 128, :], t_rq, reads=[self.t_QT], writes=[t_rq])
                S.dma("sp", rk[:], self.KT[512 + rp * 128:512 + (rp + 1) * 128, :], t_rk, reads=[self.t_KT], writes=[t_rk])
                S.dma("sp", rvt[:], self.RV[:, rp * 256:(rp + 1) * 256].rearrange("(n p) f -> p n f", p=128), t_rvt,
                      reads=[self.t_RV], writes=[t_rvt])
                S.op("pool", lambda e: e.memset(Sf[:], 0.0), writes=t_Sf)
                S.op("pool", lambda e: e.memset(Sb[:], 0.0), writes=t_Sb)
                QD = self.ctab[:, 512 + rp * 128:512 + (rp + 1) * 128]
                KD = self.ctab[:, 768 + rp * 128:768 + (rp + 1) * 128]
                for n in range(NCH):
                    cs = slice(n * 128, (n + 1) * 128)
                    S.op("pool", lambda e, cs=cs, QD=QD: e.tensor_tensor(out=rqd[:, cs], in0=rq[:, cs], in1=QD, op=ALU.mult),
                         reads=[t_rq, self.t_cb], writes=[t_rqd[n]])
                    S.op("pe", lambda e, cs=cs, n=n: e.transpose(out=self.psb[:, (n % 8) * 128:(n % 8 + 1) * 128], in_=rk[:, cs], identity=self.ident[:]),
                         reads=[t_rk, self.t_cb], writes=[self.t_psb])
                    S.op("dve", lambda e, n=n, KD=KD: e.tensor_tensor(out=kdT[:, n, :], in0=self.psb[:, (n % 8) * 128:(n % 8 + 1) * 128], in1=KD, op=ALU.mult),
                         reads=[self.t_psb, self.t_cb], writes=[t_kdT[n]])

            def chunk(rp, n):
                cd = [float(np.exp(128.0 * np.log1p(-(2.0 ** (-5.0 - (2 * rp + hh)))))) for hh in range(2)]
                cs = slice(n * 128, (n + 1) * 128)
                for hh in range(2):
                    hb = hh * 64
                    h = 2 * rp + hh
                    bk = st["bk"]
                    st["bk"] += 1
                    bs, by = bk % 2, 2 + bk % 2
                    s_, ts_ = sm[bk % 3], t_sm[bk % 3]
                    self.mm(t_ps[bs], ps[bs][:, 0:128], [(rk[hb:hb + 64, cs], rq[hb:hb + 64, cs])], reads=[t_rk, t_rq])
                    S.op("dve", lambda e, s_=s_, bs=bs, h=h: e.tensor_tensor(out=s_[:], in0=ps[bs][:, 0:128], in1=self.ctab[:, h * 128:(h + 1) * 128], op=ALU.mult),
                         reads=[t_ps[bs], self.t_cb], writes=[ts_])
                    pairs = [(rvt[:, n, hh * 128:(hh + 1) * 128], s_[:])]
                    if n > 0:
                        pairs.append((Sb[hb:hb + 64, :], rqd[hb:hb + 64, cs]))
                    self.mm(t_ps[by], ps[by][:, 0:128], pairs, reads=[t_rvt, ts_, t_Sb[hh], t_rqd[n]])
                    S.op("act", lambda e, hh=hh, cs=cs, by=by: e.activation(out=yT[rp][hh][:, cs], in_=ps[by][:, 0:128], func=AF.Copy),
                         reads=[t_ps[by]], writes=[t_yT[rp][hh][n // 4]])
                if n < NCH - 1:
                    self.mm(t_ps[6], ps[6][:, 0:256], [(kdT[:, n, :], rvt[:, n, :])], reads=[t_kdT[n], t_rvt])
                    for hh in range(2):
                        hb = hh * 64
                        S.op("dve", lambda e, hb=hb, hh=hh: e.scalar_tensor_tensor(
                            out=Sf[hb:hb + 64, :], in0=Sf[hb:hb + 64, :], scalar=cd[hh], in1=ps[6][hb:hb + 64, hh * 128:(hh + 1) * 128],
                            op0=ALU.mult, op1=ALU.add), reads=[t_ps[6], t_Sf[hh]], writes=[t_Sf[hh]])
                        S.op("act", lambda e, hb=hb: e.activation(out=Sb[hb:hb + 64, :], in_=Sf[hb:hb + 64, :], func=AF.Copy),
                             reads=[t_Sf[hh]], writes=[t_Sb[hh]])

            def hn_tile(rp, hh, i):
                h = 2 * rp + hh
                t0 = i * TT
                y_ = yT[rp][hh][:, t0:t0 + TT]
                ty = t_yT[rp][hh][i]
                cnt = st["cnt"]
                st["cnt"] += 1
                sgx, tsgx = sgt[cnt % 2], t_sgt[cnt % 2]
                r_, tr_ = ro[cnt % 2], t_ro[cnt % 2]
                sq_, tsq_ = sq[cnt % 2], t_sq[cnt % 2]
                bm, bv = 4, 5
                mean, t_mean = mean2[cnt % 2], t_mean2[cnt % 2]
                lrs, t_lrs = lrs2[cnt % 2], t_lrs2[cnt % 2]
                S.dma("sp", sgx[:], self.SG[h * 128:(h + 1) * 128, t0:t0 + TT], tsgx, reads=[self.t_SG], writes=[tsgx])
                S.op("pe", lambda e: e.matmul(ps[bm][:], lhsT=self.ones_f[:], rhs=y_, start=True, stop=True),
                     reads=[ty, self.t_const], writes=[t_ps[bm]])
                S.op("act", lambda e: e.activation(out=sq_[:], in_=y_, func=AF.Square), reads=[ty], writes=[tsq_])
                S.op("pe", lambda e: e.matmul(ps[bv][:], lhsT=self.ones_f[:], rhs=sq_[:], start=True, stop=True),
                     reads=[tsq_, self.t_const], writes=[t_ps[bv]])
                S.op("act", lambda e: e.activation(out=mean[:], in_=ps[bm][:], func=AF.Identity, scale=1.0 / 128),
                     reads=[t_ps[bm]], writes=[t_mean])
                S.op("act", lambda e: e.activation(out=lrs[:], in_=mean[:], func=AF.Square), reads=[t_mean], writes=[t_lrs])
                S.op("dve", lambda e: e.scalar_tensor_tensor(out=lrs[:], in0=ps[bv][:], scalar=1.0 / 128, in1=lrs[:],
                                                             op0=ALU.mult, op1=ALU.subtract),
                     reads=[t_ps[bv], t_lrs], writes=[t_lrs])
                S.op("act", lambda e: e.activation(out=lrs[:], in_=lrs[:], func=AF.Ln, bias=self.eps_c[:], scale=1.0),
                     reads=[t_lrs, self.t_const], writes=[t_lrs])
                S.op("act", lambda e: e.activation(out=lrs[:], in_=lrs[:], func=AF.Exp, scale=-0.5), reads=[t_lrs], writes=[t_lrs])
                S.op("dve", lambda e: e.tensor_tensor(out=y_, in0=y_, in1=mean[:], op=ALU.subtract), reads=[ty, t_mean], writes=[ty])
                S.op("dve", lambda e: e.tensor_tensor(out=y_, in0=y_, in1=lrs[:], op=ALU.mult), reads=[ty, t_lrs], writes=[ty])
                S.op("dve", lambda e: e.tensor_tensor(out=r_[:], in0=y_, in1=sgx[:], op=ALU.mult), reads=[ty, tsgx], writes=[tr_])
                S.dma("sp", self.MIXT[512 + h * 128:512 + (h + 1) * 128, t0:t0 + TT], r_[:], tr_, reads=[tr_], writes=[self.t_MIXT])

            load_pre(0)
            for n in range(NCH):
                chunk(0, n)
            load_pre(1)
            hn_list = [(0, hh, i) for hh in range(2) for i in range(NT)]
            for n in range(NCH):
                chunk(1, n)
                if n % 2 == 1 and hn_list:
                    hn_tile(*hn_list.pop(0))
            while hn_list:
                hn_tile(*hn_list.pop(0))
            for hh in range(2):
                for i in range(NT):
                    hn_tile(1, hh, i)
            S.barrier(release=[t_rq, t_rk, t_rvt] + t_sgt + t_ro)

    def odd_out(self, j):
        nc, S = self.nc, self.S
        with ExitStack() as ph:
            w_out, t_wout = self.load_w(ph, "od_wout", self.od_w_out[j], D, D)
            xt = [self.sb(ph, f"xt{i}", [128, 8, TT], F32) for i in range(2)]
            t_xt = [T(f"xt{i}") for i in range(2)]
            mx = [self.sb(ph, f"mx{i}", [128, 8, TT], BF16) for i in range(2)]
            t_mx = [T(f"mx{i}") for i in range(2)]
            ps, t_ps = self.ps, self.t_ps
            for i in range(NT):
                t0 = i * TT
                x, tx = xt[i % 2], t_xt[i % 2]
                m, tm = mx[i % 2], t_mx[i % 2]
                S.dma("sp", x[:], self.xview(self.XT, t0), tx, reads=[self.t_XT], writes=[tx])
                S.dma("sp", m[:], self.xview(self.MIXT, t0), tm, reads=[self.t_MIXT], writes=[tm])
                for oc in range(8):
                    b = oc % 6
                    self.mm(t_ps[b], ps[b][:], [(w_out[:, k, oc * 128:(oc + 1) * 128], m[:, k, :]) for k in range(8)], reads=[t_wout, tm])
                    S.op("dve", lambda e, oc=oc, b=b, x=x: e.tensor_tensor(out=x[:, oc, :], in0=ps[b][:], in1=x[:, oc, :], op=ALU.add),
                         reads=[t_ps[b], tx], writes=[tx])
                S.dma("sp", self.xview(self.XT, t0), x[:], tx, reads=[tx], writes=[self.t_XT])
            S.barrier(release=[t_wout] + t_xt + t_mx)


def const_tables():
    t = np.arange(S_LEN, dtype=np.float32)
    inv = (np.float32(10000.0) ** (-(np.arange(0, 64, 2, dtype=np.float32)) / np.float32(64))).astype(np.float32)
    ang = (t[:, None] * inv[None, :]).astype(np.float32)
    cos = np.cos(ang).astype(np.float32).T
    sin = np.sin(ang).astype(np.float32).T
    cos64 = np.concatenate([cos, cos], 0)
    sins64 = np.concatenate([-sin, sin], 0)
    cos128 = np.concatenate([cos64, cos64], 0)
    sin128 = np.concatenate([sins64, sins64], 0)
    rope = np.stack([cos128, sin128, cos128 * np.float32(0.125), sin128 * np.float32(0.125)]).astype(np.float32)
    return np.ascontiguousarray(rope)


def ret_tables():
    i = np.arange(128, dtype=np.float64)
    tab = np.zeros((128, 1024), np.float64)
    for h in range(4):
        lg = np.log1p(-(2.0 ** (-5.0 - h)))
        diff = i[None, :] - i[:, None]
        tab[:, h * 128:(h + 1) * 128] = np.where(diff >= 0, np.exp(np.maximum(diff, 0.0) * lg), 0.0)
        rp, hh = h // 2, h % 2
        qd = np.exp((i + 1.0) * lg)
        tab[hh * 64:(hh + 1) * 64, 512 + rp * 128:512 + (rp + 1) * 128] = qd[None, :]
        kd = np.exp((127.0 - i) * lg)
        tab[:, 768 + rp * 128 + hh * 64:768 + rp * 128 + (hh + 1) * 64] = kd[:, None]
    return np.ascontiguousarray(tab.astype(np.float32))


def mask_tables():
    ki = np.arange(128)[:, None]
    qi = np.arange(128)[None, :]
    m = np.zeros((128, 640), np.float32)
    m[:, 0:128] = np.where(ki >= qi, 0.0, -30000.0)
    m[:, 128:256] = np.where(ki <= qi, 0.0, -30000.0)
    m[:, 256:384] = np.eye(128, dtype=np.float32)
    m[:, 384:512] = np.where(ki >= qi, 1.0, 0.0)
    m[:, 512:640] = np.where(ki <= qi, 1.0, 0.0)
    return m


def kernel(**inp):
    inp = {k: np.asarray(v) for k, v in inp.items()}
    return run(inp, DEPTH)


def run(inp, n_layers, cores=8, trace=False):
    pc = param_layout(inp)
    prm = pc.build()
    b = Builder(n_layers, pc.off, pc.n)
    nc = b.build()
    x = inp["x"].astype(np.float32)
    f32 = lambda a: np.ascontiguousarray(np.asarray(a, np.float32))
    swp = []
    for (a0, a1) in ((0, 512), (512, 1024), (1536, 1792), (1792, 2048)):
        blk = inp["od_w_in"][:, :, a0:a1]
        nh = (a1 - a0) // 64
        blk = blk.reshape(2, D, nh, 2, 32)[:, :, :, ::-1, :].reshape(2, D, a1 - a0)
        swp.append(blk)
    od_w_sw = f32(np.concatenate(swp, axis=2))
    shared = {
        "prm": prm, "ev_w_in": f32(inp["ev_w_in"]), "ev_w_out": f32(inp["ev_w_out"]),
        "od_w_in": f32(inp["od_w_in"]), "od_w_sw": od_w_sw, "od_w_out": f32(inp["od_w_out"]),
        "ffn_w_up": f32(inp["ffn_w_up"]), "ffn_w_down": f32(inp["ffn_w_down"]),
        "rope": const_tables(), "ctab": ret_tables(), "maskb": mask_tables(),
    }
    in_maps = []
    for c in range(cores):
        m = dict(shared)
        m["xT"] = np.ascontiguousarray(x[c].T)
        in_maps.append(m)
    res = run_bass_kernel_spmd(nc, in_maps, core_ids=list(range(cores)), trace=trace)
    out = np.stack([np.ascontiguousarray(res.results[c]["yT"].T) for c in range(cores)], axis=0)
    if trace:
        return out.astype(np.float32), res
    return out.astype(np.float32)
```

```python
import numpy as np
from contextlib import ExitStack
import concourse.bass as bass
import concourse.mybir as mybir
from concourse.bass_utils import run_bass_kernel_spmd

F32 = mybir.dt.float32
BF16 = mybir.dt.bfloat16
AF = mybir.ActivationFunctionType
ALU = mybir.AluOpType

D = 1024
S_LEN = 4096
TT = 512
NT = S_LEN // TT
DEPTH = 4
DFF = 2816
EPS = 1e-6
SEM_LIMIT = 30000


class T:
    def __init__(self, name="", excl=False):
        self.name = name
        self.w = None
        self.r = {}
        self.excl = excl
        self.dsem = None
        self.dcount = 0


class Sched:
    def __init__(self, nc, ctx):
        self.nc = nc
        self.ctx = ctx
        self.eng = {"pe": nc.tensor, "act": nc.scalar, "dve": nc.vector,
                    "pool": nc.gpsimd, "sp": nc.sync}
        self.sems = {}
        self.semeng = {}
        self.semcnt = {}
        self.nsem = 0
        self.cursem = {}
        self.cnt = {}
        self.pending = {}
        self.waited = {e: {} for e in self.eng}
        self.dma_pool = []
        for e in self.eng:
            self.cursem[e] = self._newsem(e)
            self.cnt[e] = 0
            self.pending[e] = False
        self.ninstr = 0

    def _newsem(self, e):
        key = self.nsem
        self.nsem += 1
        self.sems[key] = self.ctx.enter_context(self.nc.semaphore(f"s{key}"))
        self.semeng[key] = e
        self.semcnt[key] = 0
        return key

    def _deps(self, e, reads, writes):
        deps = {}

        def add(tok):
            if tok is None:
                return
            k, v = tok
            if deps.get(k, 0) < v:
                deps[k] = v

        for t in reads:
            add(t.w)
            if t.excl:
                for k, v in t.r.items():
                    if self.semeng[k] != e:
                        add((k, v))
        for t in writes:
            add(t.w)
            for k, v in t.r.items():
                add((k, v))
        out = []
        for k, v in deps.items():
            if e == "pe" and self.semeng[k] == "pe":
                continue
            if self.waited[e].get(k, 0) >= v:
                continue
            self.waited[e][k] = v
            out.append((k, v))
        return out

    def _emit(self, e, fn, waits, inc):
        eng = self.eng[e]
        for (k, v) in waits[1:]:
            eng.wait_ge(self.sems[k], v)
            self.ninstr += 1
        ins = fn(eng)
        if waits:
            k, v = waits[0]
            ins._wait_ge(self.sems[k], v)
        if inc is not None:
            ins.then_inc(self.sems[inc[0]], inc[1])
            self.semcnt[inc[0]] += inc[1]
        self.ninstr += 1
        return ins

    def op(self, e, fn, reads=(), writes=(), inc=True):
        waits = self._deps(e, reads, writes)
        if inc and not self.pending[e] and self.cnt[e] >= SEM_LIMIT:
            self.cursem[e] = self._newsem(e)
            self.cnt[e] = 0
        sk = self.cursem[e]
        if inc:
            self.cnt[e] += 1
            tok = (sk, self.cnt[e])
            self.pending[e] = False
        else:
            tok = (sk, self.cnt[e] + 1)
            self.pending[e] = True
        self._emit(e, fn, waits, (sk, 1) if inc else None)
        for t in reads:
            if t.r.get(tok[0], 0) < tok[1]:
                t.r[tok[0]] = tok[1]
        for t in writes:
            t.w = tok
            t.r = {}
        return tok

    def dma(self, q, out_ap, in_ap, semtile, reads=(), writes=(), group=False):
        if semtile.dsem is None:
            if self.dma_pool:
                semtile.dsem = self.dma_pool.pop()
                semtile.dcount = self.semcnt[semtile.dsem]
            else:
                semtile.dsem = self._newsem(None)
        sk = semtile.dsem
        saved = []
        if group:
            saved = [(t, t.w) for t in writes if t.w is not None and t.w[0] == sk]
            for t, _ in saved:
                t.w = None
        waits = self._deps(q, reads, writes)
        for t, w in saved:
            t.w = w
        semtile.dcount += 16
        tok = (sk, semtile.dcount)
        self._emit(q, lambda eng: eng.dma_start(out=out_ap, in_=in_ap), waits, (sk, 16))
        for t in reads:
            if t.r.get(sk, 0) < tok[1]:
                t.r[sk] = tok[1]
        for t in writes:
            t.w = tok
            t.r = {}
        return tok

    def barrier(self, release=()):
        toks = []
        for f in self.eng:
            assert not self.pending[f]
            if self.cnt[f] > 0:
                toks.append((self.cursem[f], self.cnt[f]))
        for k, e in self.semeng.items():
            if e is None and self.semcnt[k] > 0:
                toks.append((k, self.semcnt[k]))
        for e in self.eng:
            for (k, v) in toks:
                if self.waited[e].get(k, 0) >= v:
                    continue
                self.waited[e][k] = v
                self.eng[e].wait_ge(self.sems[k], v)
                self.ninstr += 1
        for t in release:
            if t.dsem is not None:
                self.dma_pool.append(t.dsem)
                t.dsem = None


def _cols(v):
    v = np.asarray(v, np.float32)
    return np.ascontiguousarray(v.reshape(-1, 128).T)


def _conv_cols(w):
    w = np.asarray(w, np.float32)
    K, C = w.shape
    return np.ascontiguousarray(w.T.reshape(C // 128, 128, K).transpose(1, 0, 2).reshape(128, -1))


class PCols:
    def __init__(self):
        self.blocks = []
        self.off = {}
        self.n = 0

    def add(self, name, arr):
        self.off[name] = self.n
        self.blocks.append(arr)
        self.n += arr.shape[1]

    def build(self):
        return np.ascontiguousarray(np.concatenate(self.blocks, axis=1))


def param_layout(inp):
    pc = PCols()
    for j in range(2):
        pc.add(f"ev_norm{j}", _cols(inp["ev_norm"][j]))
        pc.add(f"ev_aconv{j}", _conv_cols(inp["ev_a_conv"][j]))
        pc.add(f"ev_aconvb{j}", _cols(inp["ev_a_conv_b"][j]))
        pc.add(f"ev_lng{j}", _cols(inp["ev_a_ln_g"][j]))
        pc.add(f"ev_lnb{j}", _cols(inp["ev_a_ln_b"][j]))
        pc.add(f"ev_bconv{j}", _conv_cols(inp["ev_b_conv"][j]))
        pc.add(f"od_norm{j}", _cols(inp["od_norm"][j]))
    for l in range(DEPTH):
        pc.add(f"ffn_norm{l}", _cols(inp["ffn_norm"][l]))
        pc.add(f"ffn_conv{l}", _conv_cols(inp["ffn_conv"][l]))
        pc.add(f"ffn_convb{l}", _cols(inp["ffn_conv_b"][l]))
    pc.add("final_norm", _cols(inp["final_norm"]))
    return pc


class Builder:
    def __init__(self, n_layers, pc_off, pc_n):
        self.n_layers = n_layers
        self.off = pc_off
        nc = self.nc = bass.Bass("TRN2", target_bir_lowering=False)
        dt = nc.dram_tensor
        self.xin = dt("xT", [D, S_LEN], F32, kind="ExternalInput").ap()
        self.prm_d = dt("prm", [128, pc_n], F32, kind="ExternalInput").ap()
        self.ev_w_in = dt("ev_w_in", [2, D, 2560], F32, kind="ExternalInput").ap()
        self.ev_w_out = dt("ev_w_out", [2, D, D], F32, kind="ExternalInput").ap()
        self.od_w_in = dt("od_w_in", [2, D, 3072], F32, kind="ExternalInput").ap()
        self.od_w_sw = dt("od_w_sw", [2, D, 1536], F32, kind="ExternalInput").ap()
        self.od_w_out = dt("od_w_out", [2, D, D], F32, kind="ExternalInput").ap()
        self.w_up = dt("ffn_w_up", [DEPTH, D, 2 * DFF], F32, kind="ExternalInput").ap()
        self.w_down = dt("ffn_w_down", [DEPTH, DFF, D], F32, kind="ExternalInput").ap()
        self.rope_d = dt("rope", [4, 128, S_LEN], F32, kind="ExternalInput").ap()
        self.ctab_d = dt("ctab", [128, 1024], F32, kind="ExternalInput").ap()
        self.maskb_d = dt("maskb", [128, 640], F32, kind="ExternalInput").ap()
        self.yout = dt("yT", [D, S_LEN], F32, kind="ExternalOutput").ap()
        self.XT = dt("XT_s", [D, S_LEN], F32, kind="Internal").ap()
        self.HT = dt("HT_s", [D, S_LEN], BF16, kind="Internal").ap()
        self.QT = dt("QT_s", [768, S_LEN], BF16, kind="Internal").ap()
        self.KT = dt("KT_s", [768, S_LEN], BF16, kind="Internal").ap()
        self.V = dt("V_s", [S_LEN, 512], BF16, kind="Internal").ap()
        self.RV = dt("RV_s", [S_LEN, 512], BF16, kind="Internal").ap()
        self.SG = dt("SG_s", [512, S_LEN], F32, kind="Internal").ap()
        self.MIXT = dt("MIXT_s", [D, S_LEN], BF16, kind="Internal").ap()
        self.t_XT, self.t_HT, self.t_QT, self.t_KT = T("XT"), T("HT"), T("QT"), T("KT")
        self.t_V, self.t_RV, self.t_SG, self.t_MIXT, self.t_Y = T("V"), T("RV"), T("SG"), T("MIXT"), T("Y")

    def sb(self, ph, name, shape, dtype):
        self._uid = getattr(self, "_uid", 0) + 1
        return ph.enter_context(self.nc.sbuf_tensor(f"{name}_u{self._uid}", shape, dtype))

    def mm(self, bank_t, out_ap, pairs, reads, start=True, stop=True, skip=False):
        n = len(pairs)
        for i, (l, r) in enumerate(pairs):
            st = start and i == 0
            sp_ = stop and i == n - 1
            kw = dict(skip_group_check=True) if skip else {}
            self.S.op("pe", lambda e, l=l, r=r, st=st, sp_=sp_, kw=kw: e.matmul(out_ap, lhsT=l, rhs=r, start=st, stop=sp_, **kw),
                      reads=reads, writes=[bank_t], inc=(i == n - 1))

    def pcol(self, name, c):
        o = self.off[name] + c
        return self.prm[:, o:o + 1]

    def xview(self, dram, t0):
        return dram.rearrange("(c p) t -> p c t", p=128)[:, :, t0:t0 + TT]

    def load_w(self, ph, name, src2d, rows, cols, q="pool"):
        kc = rows // 128
        w = self.sb(ph, name, [128, kc, cols], BF16)
        t = T(name)
        v = src2d.rearrange("(kc p) n -> p kc n", p=128)
        for k in range(kc):
            self.S.dma(q, w[:, k, :], v[:, k, :], t, writes=[t], group=True)
        return w, t

    def load_wg(self, ph, name, src2d, rows, cols, groups, w=None, dst0=0, q="pool"):
        kc = rows // 128
        if w is None:
            w = self.sb(ph, name, [128, kc, cols], BF16)
        v = src2d.rearrange("(kc p) n -> p kc n", p=128)
        tiles = []
        for gi, (c0, c1) in enumerate(groups):
            t = T(f"{name}_g{gi}")
            for k in range(kc):
                self.S.dma(q, w[:, k, dst0 + c0:dst0 + c1], v[:, k, c0:c1], t, writes=[t], group=True)
            tiles.append((c0, c1, t))

        def tile_for(col):
            for (c0, c1, t) in tiles:
                if c0 <= col < c1:
                    return t
            raise KeyError(col)
        return w, tile_for, [t for (_, _, t) in tiles]

    def rms_hT(self, xt, t_xt, hT, t_hT, gname, sq, t_sq, rstd, t_rstd, bank, t_bank):
        S = self.S
        for c in range(8):
            b = c % 2
            S.op("act", lambda e, c=c, b=b: e.activation(out=sq[b][:], in_=xt[:, c, :], func=AF.Square),
                 reads=[t_xt], writes=[t_sq[b]])
            S.op("pe", lambda e, c=c, b=b: e.matmul(bank[:], lhsT=self.ones_f[:], rhs=sq[b][:], start=(c == 0), stop=(c == 7)),
                 reads=[t_sq[b], self.t_const], writes=[t_bank], inc=True)
        S.op("act", lambda e: e.activation(out=rstd[:], in_=bank[:], func=AF.Ln, bias=self.eps_c[:], scale=1.0 / D),
             reads=[t_bank, self.t_const], writes=[t_rstd])
        S.op("act", lambda e: e.activation(out=rstd[:], in_=rstd[:], func=AF.Exp, scale=-0.5), reads=[t_rstd], writes=[t_rstd])
        for c in range(8):
            S.op("dve", lambda e, c=c: e.scalar_tensor_tensor(out=hT[:, c, :], in0=xt[:, c, :], scalar=self.pcol(gname, c),
                                                              in1=rstd[:], op0=ALU.mult, op1=ALU.mult),
                 reads=[t_xt, t_rstd, self.t_const], writes=[t_hT[c]])

    def build(self):
        nc = self.nc
        with ExitStack() as ctx:
            S = self.S = Sched(nc, ctx)
            self.prm = nc.alloc_sbuf_tensor("prm_sb", [128, self.prm_d.shape[1]], F32)
            self.ones_f = nc.alloc_sbuf_tensor("ones_f", [128, 128], F32)
            self.eps_c = nc.alloc_sbuf_tensor("eps_c", [128, 1], F32)
            self.t_const = T("const")
            S.dma("sp", self.prm[:], self.prm_d, self.t_const, writes=[self.t_const])
            S.op("pool", lambda e: e.memset(self.ones_f[:], 1.0), writes=[self.t_const])
            S.op("pool", lambda e: e.memset(self.eps_c[:], EPS), writes=[self.t_const])
            self.ctab = nc.alloc_sbuf_tensor("ctab_sb", [128, 1024], F32)
            self.maskb = nc.alloc_sbuf_tensor("maskb_sb", [128, 256], BF16)
            self.ident = nc.alloc_sbuf_tensor("ident_sb", [128, 128], BF16)
            self.ones_b = nc.alloc_sbuf_tensor("ones_b", [128, 64], BF16)
            self.t_cb = T("constb")
            S.dma("sp", self.ctab[:], self.ctab_d, self.t_cb, writes=[self.t_cb])
            S.barrier()
            S.dma("pool", self.maskb[:], self.maskb_d[:, 0:256], self.t_cb, writes=[self.t_cb])
            S.barrier()
            S.dma("pool", self.ident[:], self.maskb_d[:, 256:384], self.t_cb, writes=[self.t_cb])
            S.barrier()
            self.mask01 = nc.alloc_sbuf_tensor("mask01_sb", [128, 256], BF16)
            S.dma("pool", self.mask01[:], self.maskb_d[:, 384:640], self.t_cb, writes=[self.t_cb])
            S.op("pool", lambda e: e.memset(self.ones_b[:], 1.0), writes=[self.t_cb])
            self.ps = [nc.alloc_psum_tensor(f"ps{i}", [128, 512], F32) for i in range(7)]
            self.t_ps = [T(f"ps{i}", excl=True) for i in range(7)]
            self.psb = nc.alloc_psum_tensor("psb", [128, 1024], BF16)
            self.t_psb = T("psb", excl=True)
            S.barrier()

            src = self.xin
            for layer in range(self.n_layers):
                j = layer // 2
                if layer % 2 == 0:
                    self.phase_even(j, src)
                else:
                    self.phase_odd(j)
                src = self.XT
                self.phase_ffn(layer, 0, fuse_j=(j if layer % 2 == 1 else None))
                self.phase_ffn(layer, 1, final=(layer == self.n_layers - 1))
            print("ninstr", S.ninstr, "nsem", S.nsem, flush=True)
        return nc

    def phase_even(self, j, src):
        nc, S = self.nc, self.S
        with ExitStack() as ph:
            w_in, win_t, win_tl = self.load_wg(ph, "ev_win", self.ev_w_in[j], D, 2560,
                                               [(512, 1024), (0, 512), (1536, 2048), (2048, 2560), (1024, 1536)])
            w_out, t_wout = self.load_w(ph, "ev_wout", self.ev_w_out[j], D, D)
            xt = [self.sb(ph, f"xt{i}", [128, 8, TT], F32) for i in range(2)]
            t_xt = [T(f"xt{i}") for i in range(2)]
            hTb = [self.sb(ph, f"hT{i}", [128, 8, TT], BF16) for i in range(2)]
            t_hTb = [[T(f"hT{i}_{c}") for c in range(8)] for i in range(2)]
            sq = [self.sb(ph, f"sq{i}", [128, TT], F32) for i in range(2)]
            t_sq = [T(f"sq{i}") for i in range(2)]
            rstd = self.sb(ph, "rstd", [128, TT], F32)
            t_rstd = T("rstd")
            abuf = [self.sb(ph, f"abuf{c}", [128, 30 + TT], BF16) for c in range(4)]
            dg = self.sb(ph, "dg", [128, 124, 128], BF16)
            t_dg = T("dg")
            t_abuf = [T(f"abuf{c}") for c in range(4)]
            acv = [self.sb(ph, f"acv{c}", [128, TT], F32) for c in range(4)]
            t_acv = [T(f"acv{c}") for c in range(4)]
            acp = [self.sb(ph, f"acp{c}", [128, TT], F32) for c in range(2)]
            t_acp = [T(f"acp{c}") for c in range(2)]
            sig = [self.sb(ph, f"sig{i}", [128, TT], F32) for i in range(2)]
            t_sig = [T(f"sig{i}") for i in range(2)]
            ctmp = [self.sb(ph, f"ctmp{i}", [128, TT], F32) for i in range(3)]
            t_ctmp = [T(f"ctmp{i}") for i in range(3)]
            ktmp = 0
            bbuf = [self.sb(ph, f"bbuf{c}", [128, 2 + TT], F32) for c in range(4)]
            t_bbuf = [T(f"bbuf{c}") for c in range(4)]
            bacc = [self.sb(ph, f"bacc{i}", [128, TT], F32) for i in range(2)]
            t_bacc = [T(f"bacc{i}") for i in range(2)]
            mix = self.sb(ph, "mix", [128, 8, TT], BF16)
            t_mix = [T(f"mix{c}") for c in range(8)]
            mean = self.sb(ph, "mean", [128, TT], F32)
            t_mean = T("mean")
            lrs = self.sb(ph, "lrs", [128, TT], F32)
            t_lrs = T("lrs")
            ps, t_ps = self.ps, self.t_ps
            for c in range(4):
                S.op("pool", lambda e, c=c: e.memset(abuf[c][:, 0:30], 0.0), writes=[t_abuf[c]])
                S.op("pool", lambda e, c=c: e.memset(bbuf[c][:, 0:2], 0.0), writes=[t_bbuf[c]])
            gname = f"ev_norm{j}"
            for c in range(4):
                for k in range(31):
                    wcol = self.off[f"ev_aconv{j}"] + c * 31 + k
                    S.op("dve", lambda e, c=c, k=k, wcol=wcol: e.tensor_scalar(out=dg[:, c * 31 + k, :], in0=self.ident[:],
                                                                              scalar1=self.prm[:, wcol:wcol + 1], scalar2=None, op0=ALU.mult),
                         reads=[self.t_cb, self.t_const], writes=[t_dg])
            bk = 0

            def nb():
                nonlocal bk
                b = 1 + (bk % 6)
                bk += 1
                return b

            def prologue(i):
                S.dma("sp", xt[i % 2][:], self.xview(src, i * TT), t_xt[i % 2], reads=[self.t_XT], writes=[t_xt[i % 2]])
                self.rms_hT(xt[i % 2], t_xt[i % 2], hTb[i % 2], t_hTb[i % 2], gname, sq, t_sq, rstd, t_rstd, ps[0], t_ps[0])

            prologue(0)
            for i in range(NT):
                t0 = i * TT
                x, tx = xt[i % 2], t_xt[i % 2]
                hT, t_hT = hTb[i % 2], t_hTb[i % 2]

                def proj(col0, b):
                    self.mm(t_ps[b], ps[b][:], [(w_in[:, k, col0:col0 + 128], hT[:, k, :]) for k in range(8)],
                            reads=[win_t(col0)] + t_hT)

                for c in range(4):
                    bg, bv = nb(), nb()
                    proj(512 + c * 128, bg)
                    proj(c * 128, bv)
                    sg, tsg = sig[c % 2], t_sig[c % 2]
                    S.op("act", lambda e, bg=bg, sg=sg: e.activation(out=sg[:], in_=ps[bg][:], func=AF.Sigmoid),
                         reads=[t_ps[bg]], writes=[tsg])
                    S.op("dve", lambda e, bv=bv, sg=sg, c=c: e.tensor_tensor(out=abuf[c][:, 30:30 + TT], in0=ps[bv][:], in1=sg[:], op=ALU.mult),
                         reads=[t_ps[bv], tsg], writes=[t_abuf[c]])
                    wc = self.off[f"ev_aconv{j}"] + c * 31
                    bcv = nb()
                    self.mm(t_ps[bcv], ps[bcv][:], [(dg[:, c * 31 + k, :], abuf[c][:, k:k + TT]) for k in range(31)],
                            reads=[t_dg, t_abuf[c]])
                    S.op("act", lambda e, c=c, bcv=bcv: e.activation(out=acv[c][:], in_=ps[bcv][:], func=AF.Identity,
                                                                     bias=self.pcol(f"ev_aconvb{j}", c), scale=1.0),
                         reads=[t_ps[bcv], self.t_const], writes=[t_acv[c]])
                    S.op("pool", lambda e, c=c: e.tensor_copy(out=abuf[c][:, 0:30], in_=abuf[c][:, TT:TT + 30]),
                         reads=[t_abuf[c]], writes=[t_abuf[c]])
                bm, bv2 = nb(), nb()
                for c in range(4):
                    S.op("pe", lambda e, c=c, bm=bm: e.matmul(ps[bm][:], lhsT=self.ones_f[:], rhs=acv[c][:], start=(c == 0), stop=(c == 3)),
                         reads=[t_acv[c], self.t_const], writes=[t_ps[bm]])
                for c in range(4):
                    b = c % 2
                    S.op("act", lambda e, c=c, b=b: e.activation(out=sq[b][:], in_=acv[c][:], func=AF.Square),
                         reads=[t_acv[c]], writes=[t_sq[b]])
                    S.op("pe", lambda e, c=c, b=b, bv2=bv2: e.matmul(ps[bv2][:], lhsT=self.ones_f[:], rhs=sq[b][:], start=(c == 0), stop=(c == 3)),
                         reads=[t_sq[b], self.t_const], writes=[t_ps[bv2]])
                S.op("act", lambda e, bm=bm: e.activation(out=mean[:], in_=ps[bm][:], func=AF.Identity, scale=1.0 / 512),
                     reads=[t_ps[bm]], writes=[t_mean])
                S.op("act", lambda e: e.activation(out=lrs[:], in_=mean[:], func=AF.Square), reads=[t_mean], writes=[t_lrs])
                S.op("dve", lambda e, bv2=bv2: e.scalar_tensor_tensor(out=lrs[:], in0=ps[bv2][:], scalar=1.0 / 512, in1=lrs[:],
                                                                      op0=ALU.mult, op1=ALU.subtract),
                     reads=[t_ps[bv2], t_lrs], writes=[t_lrs])
                S.op("act", lambda e: e.activation(out=lrs[:], in_=lrs[:], func=AF.Ln, bias=self.eps_c[:], scale=1.0),
                     reads=[t_lrs, self.t_const], writes=[t_lrs])
                S.op("act", lambda e: e.activation(out=lrs[:], in_=lrs[:], func=AF.Exp, scale=-0.5), reads=[t_lrs], writes=[t_lrs])
                for c in range(4):
                    S.op("dve", lambda e, c=c: e.tensor_tensor(out=acv[c][:], in0=acv[c][:], in1=mean[:], op=ALU.subtract),
                         reads=[t_acv[c], t_mean], writes=[t_acv[c]])
                    S.op("dve", lambda e, c=c: e.tensor_tensor(out=acv[c][:], in0=acv[c][:], in1=lrs[:], op=ALU.mult),
                         reads=[t_acv[c], t_lrs], writes=[t_acv[c]])
                    S.op("act", lambda e, c=c: e.activation(out=mix[:, c, :], in_=acv[c][:], func=AF.Silu,
                                                            bias=self.pcol(f"ev_lnb{j}", c), scale=self.pcol(f"ev_lng{j}", c)),
                         reads=[t_acv[c], self.t_const], writes=[t_mix[c]])
                for c in range(4):
                    bc_, bh_, bb_ = nb(), nb(), nb()
                    proj(1536 + c * 128, bc_)
                    proj(2048 + c * 128, bh_)
                    proj(1024 + c * 128, bb_)
                    sg, tsg = sig[c % 2], t_sig[c % 2]
                    S.op("act", lambda e, bh_=bh_, sg=sg: e.activation(out=sg[:], in_=ps[bh_][:], func=AF.Copy),
                         reads=[t_ps[bh_]], writes=[tsg])
                    S.op("dve", lambda e, bc_=bc_, sg=sg, c=c: e.tensor_tensor(out=bbuf[c][:, 2:2 + TT], in0=ps[bc_][:], in1=sg[:], op=ALU.mult),
                         reads=[t_ps[bc_], tsg], writes=[t_bbuf[c]])
                    wc = self.off[f"ev_bconv{j}"] + c * 3
                    ba, tba = bacc[c % 2], t_bacc[c % 2]
                    S.op("act", lambda e, c=c, wc=wc, ba=ba: e.activation(out=ba[:], in_=bbuf[c][:, 2:2 + TT], func=AF.Identity,
                                                                          scale=self.prm[:, wc + 2:wc + 3]),
                         reads=[t_bbuf[c], self.t_const], writes=[tba])
                    for k in range(2):
                        S.op("dve", lambda e, c=c, wc=wc, k=k, ba=ba: e.scalar_tensor_tensor(
                            out=ba[:], in0=bbuf[c][:, k:k + TT], scalar=self.prm[:, wc + k:wc + k + 1], in1=ba[:],
                            op0=ALU.mult, op1=ALU.add), reads=[t_bbuf[c], tba, self.t_const], writes=[tba])
                    S.op("dve", lambda e, c=c, bb_=bb_, ba=ba: e.tensor_tensor(out=mix[:, 4 + c, :], in0=ps[bb_][:], in1=ba[:], op=ALU.mult),
                         reads=[t_ps[bb_], tba], writes=[t_mix[4 + c]])
                    S.op("pool", lambda e, c=c: e.tensor_copy(out=bbuf[c][:, 0:2], in_=bbuf[c][:, TT:TT + 2]),
                         reads=[t_bbuf[c]], writes=[t_bbuf[c]])
                if i + 1 < NT:
                    prologue(i + 1)
                for oc in range(8):
                    b = nb()
                    self.mm(t_ps[b], ps[b][:], [(w_out[:, k, oc * 128:(oc + 1) * 128], mix[:, k, :]) for k in range(8)],
                            reads=[t_wout] + t_mix)
                    S.op("dve", lambda e, oc=oc, b=b, x=x: e.tensor_tensor(out=x[:, oc, :], in0=ps[b][:], in1=x[:, oc, :], op=ALU.add),
                         reads=[t_ps[b], tx], writes=[tx])
                S.dma("sp", self.xview(self.XT, t0), x[:], tx, reads=[tx], writes=[self.t_XT])
            S.barrier(release=win_tl + [t_wout] + t_xt)

    def phase_ffn(self, layer, hf, final=False, fuse_j=None):
        nc, S = self.nc, self.S
        HC = 11
        with ExitStack() as ph:
            wu = self.sb(ph, "wu", [128, 8, 2 * HC * 128], BF16)
            upv = self.w_up[layer].rearrange("(kc p) n -> p kc n", p=128)
            GP = ((0, 2), (2, 5), (5, 8), (8, 11))
            t_wug = [T(f"wu_g{gi}") for gi in range(len(GP))]
            for gi, (p0, p1) in enumerate(GP):
                for k in range(8):
                    S.dma("pool", wu[:, k, p0 * 128:p1 * 128], upv[:, k, (hf * HC + p0) * 128:(hf * HC + p1) * 128], t_wug[gi],
                          writes=[t_wug[gi]], group=True)
                    S.dma("pool", wu[:, k, (HC + p0) * 128:(HC + p1) * 128], upv[:, k, DFF + (hf * HC + p0) * 128:DFF + (hf * HC + p1) * 128],
                          t_wug[gi], writes=[t_wug[gi]], group=True)

            def t_wu_for(pc_):
                for gi, (p0, p1) in enumerate(GP):
                    if p0 <= pc_ < p1:
                        return t_wug[gi]
            wd, t_wd = self.load_w(ph, "wd", self.w_down[layer][hf * HC * 128:(hf + 1) * HC * 128, :], HC * 128, D)
            xt = [self.sb(ph, f"xt{i}", [128, 8, TT], F32) for i in range(2)]
            t_xt = [T(f"xt{i}") for i in range(2)]
            hTb = [self.sb(ph, f"hT{i}", [128, 8, TT], BF16) for i in range(2)]
            t_hTb = [[T(f"hT{i}_{c}") for c in range(8)] for i in range(2)]
            t_hTall = [T(f"hTall{i}") for i in range(2)]
            sq8 = self.sb(ph, "sq8", [128, 8, TT], F32)
            t_sq8 = [T(f"sq8_{c}") for c in range(8)]
            rstd = self.sb(ph, "rstd", [128, TT], F32)
            t_rstd = T("rstd")
            NU = 4
            ug = [self.sb(ph, f"ug{i}", [128, TT], F32) for i in range(NU)]
            t_ug = [T(f"ug{i}") for i in range(NU)]
            uu = [self.sb(ph, f"uu{i}", [128, TT], F32) for i in range(NU)]
            t_uu = [T(f"uu{i}") for i in range(NU)]
            Bg = [self.sb(ph, f"Bg{i}", [128, TT], F32) for i in range(3)]
            t_Bg = [T(f"Bg{i}") for i in range(3)]
            hal = self.sb(ph, "hal", [128, 2, 2 * HC, 2], F32)
            t_hal = [[T(f"hal{p}_{c}") for c in range(2 * HC)] for p in range(2)]
            g = self.sb(ph, "g", [128, HC, TT], BF16)
            t_g = [T(f"g{c}") for c in range(HC)]
            if fuse_j is not None:
                w_out, t_wout = self.load_w(ph, "od_wout", self.od_w_out[fuse_j], D, D)
                mx = self.sb(ph, "mx", [128, 8, TT], BF16)
                t_mx = T("mx")
            if final:
                yo = self.sb(ph, "yo", [128, 8, TT], F32)
                t_yo = [T(f"yo{c}") for c in range(8)]
                t_yoall = T("yoall")
                rstd_f = self.sb(ph, "rstd_f", [128, TT], F32)
                t_rstd_f = T("rstd_f")
            ps, t_ps = self.ps, self.t_ps
            S.op("pool", lambda e: e.memset(hal[:], 0.0), writes=t_hal[0] + t_hal[1])
            gname = f"ffn_norm{layer}"
            cw = self.off[f"ffn_conv{layer}"]
            cb = self.off[f"ffn_convb{layer}"]
            bk = 0

            def nb():
                nonlocal bk
                b = 1 + (bk % 6)
                bk += 1
                return b

            def pro_load(i):
                t0 = i * TT
                x, tx = xt[i % 2], t_xt[i % 2]
                S.dma("sp", x[:], self.xview(self.XT, t0), tx, reads=[self.t_XT], writes=[tx])
                if hf == 1:
                    S.dma("sp", hTb[i % 2][:], self.xview(self.HT, t0), t_hTall[i % 2], reads=[self.t_HT], writes=[t_hTall[i % 2]])
                if fuse_j is not None:
                    S.dma("sp", mx[:], self.xview(self.MIXT, t0), t_mx, reads=[self.t_MIXT], writes=[t_mx])

            def pro_out(i):
                if fuse_j is None:
                    return
                x, tx = xt[i % 2], t_xt[i % 2]
                for oc in range(8):
                    b = nb()
                    self.mm(t_ps[b], ps[b][:], [(w_out[:, k, oc * 128:(oc + 1) * 128], mx[:, k, :]) for k in range(8)], reads=[t_wout, t_mx])
                    S.op("dve", lambda e, oc=oc, b=b, x=x: e.tensor_tensor(out=x[:, oc, :], in0=ps[b][:], in1=x[:, oc, :], op=ALU.add),
                         reads=[t_ps[b], tx], writes=[tx])

            def pro_sq(i):
                if hf == 1:
                    return
                x, tx = xt[i % 2], t_xt[i % 2]
                for c in range(8):
                    S.op("pool", lambda e, c=c, x=x: e.tensor_tensor(out=sq8[:, c, :], in0=x[:, c, :], in1=x[:, c, :], op=ALU.mult),
                         reads=[tx], writes=[t_sq8[c]])

            def pro_stat(i):
                if hf == 1:
                    return
                for c in range(8):
                    S.op("pe", lambda e, c=c: e.matmul(ps[0][:], lhsT=self.ones_f[:], rhs=sq8[:, c, :], start=(c == 0), stop=(c == 7)),
                         reads=[t_sq8[c], self.t_const], writes=[t_ps[0]], inc=(c == 7))
                S.op("act", lambda e: e.activation(out=rstd[:], in_=ps[0][:], func=AF.Ln, bias=self.eps_c[:], scale=1.0 / D),
                     reads=[t_ps[0], self.t_const], writes=[t_rstd])
                S.op("act", lambda e: e.activation(out=rstd[:], in_=rstd[:], func=AF.Exp, scale=-0.5), reads=[t_rstd], writes=[t_rstd])

            def pro_norm(i):
                if hf == 1:
                    return
                t0 = i * TT
                x, tx = xt[i % 2], t_xt[i % 2]
                h_, th_ = hTb[i % 2], t_hTb[i % 2]
                for c in range(8):
                    S.op("dve", lambda e, c=c, x=x, h_=h_: e.scalar_tensor_tensor(out=h_[:, c, :], in0=x[:, c, :], scalar=self.pcol(gname, c),
                                                                                  in1=rstd[:], op0=ALU.mult, op1=ALU.mult),
                         reads=[tx, t_rstd, self.t_const], writes=[th_[c]])
                S.dma("sp", self.xview(self.HT, t0), h_[:], t_hTall[i % 2], reads=th_, writes=[self.t_HT])

            def evac_back(i, pc_):
                u_g, tg_ = ug[pc_ % NU], t_ug[pc_ % NU]
                u_u, tu_ = uu[pc_ % NU], t_uu[pc_ % NU]
                S.op("act", lambda e, u_g=u_g: e.activation(out=u_g[:], in_=u_g[:], func=AF.Silu), reads=[tg_], writes=[tg_])
                S.op("pool", lambda e, u_g=u_g, u_u=u_u, pc_=pc_: e.tensor_tensor(out=g[:, pc_, :], in0=u_g[:], in1=u_u[:], op=ALU.mult),
                     reads=[tg_, tu_], writes=[t_g[pc_]])

            def up_pair(i, pc_):
                hT = hTb[i % 2]
                hreads = t_hTb[i % 2] if hf == 0 else [t_hTall[i % 2]]
                hp_, hn_ = i % 2, (i + 1) % 2
                banks = []
                for role in range(2):
                    b = nb()
                    ch = role * HC + pc_
                    self.mm(t_ps[b], ps[b][:], [(wu[:, k, ch * 128:(ch + 1) * 128], hT[:, k, :]) for k in range(8)],
                            reads=[t_wu_for(pc_)] + hreads)
                    banks.append(b)
                for role in range(2):
                    b = banks[role]
                    ch = role * HC + pc_
                    gch = role * 22 + hf * HC + pc_
                    u = (ug if role == 0 else uu)[pc_ % NU]
                    tu = (t_ug if role == 0 else t_uu)[pc_ % NU]
                    w0 = self.prm[:, cw + gch * 3:cw + gch * 3 + 1]
                    w1 = self.prm[:, cw + gch * 3 + 1:cw + gch * 3 + 2]
                    w2 = self.prm[:, cw + gch * 3 + 2:cw + gch * 3 + 3]
                    bias = self.prm[:, cb + gch:cb + gch + 1]
                    S.op("act", lambda e, b=b, ch=ch: e.activation(out=hal[:, hn_, ch, :], in_=ps[b][:, TT - 2:TT], func=AF.Copy),
                         reads=[t_ps[b]], writes=[t_hal[hn_][ch]])
                    S.op("act", lambda e, u=u, b=b, w2=w2, bias=bias: e.activation(out=u[:], in_=ps[b][:], func=AF.Identity, bias=bias, scale=w2),
                         reads=[t_ps[b], self.t_const], writes=[tu])
                    if role == 0:
                        bg_, tbg_ = Bg[pc_ % 3], t_Bg[pc_ % 3]
                        S.op("act", lambda e, bg_=bg_, b=b, w1=w1: e.activation(out=bg_[:], in_=ps[b][:], func=AF.Identity, scale=w1),
                             reads=[t_ps[b], self.t_const], writes=[tbg_])
                    else:
                        S.op("dve", lambda e, u=u, b=b, w1=w1: e.scalar_tensor_tensor(out=u[:, 1:TT], in0=ps[b][:, 0:TT - 1], scalar=w1, in1=u[:, 1:TT],
                                                                                     op0=ALU.mult, op1=ALU.add),
                             reads=[t_ps[b], tu, self.t_const], writes=[tu])
                    S.op("dve", lambda e, u=u, b=b, w0=w0: e.scalar_tensor_tensor(out=u[:, 2:TT], in0=ps[b][:, 0:TT - 2], scalar=w0, in1=u[:, 2:TT],
                                                                                 op0=ALU.mult, op1=ALU.add),
                         reads=[t_ps[b], tu, self.t_const], writes=[tu])
                    S.op("dve", lambda e, u=u, ch=ch, w0=w0: e.scalar_tensor_tensor(out=u[:, 0:2], in0=hal[:, hp_, ch, 0:2], scalar=w0, in1=u[:, 0:2],
                                                                                   op0=ALU.mult, op1=ALU.add),
                         reads=[t_hal[hp_][ch], tu, self.t_const], writes=[tu])
                    S.op("dve", lambda e, u=u, ch=ch, w1=w1: e.scalar_tensor_tensor(out=u[:, 0:1], in0=hal[:, hp_, ch, 1:2], scalar=w1, in1=u[:, 0:1],
                                                                                   op0=ALU.mult, op1=ALU.add),
                         reads=[t_hal[hp_][ch], tu, self.t_const], writes=[tu])
                    if role == 0:
                        S.op("pool", lambda e, u=u, bg_=bg_: e.tensor_tensor(out=u[:, 1:TT], in0=u[:, 1:TT], in1=bg_[:, 0:TT - 1], op=ALU.add),
                             reads=[tu, tbg_], writes=[tu])
                if pc_ >= 2:
                    evac_back(i, pc_ - 2)
                if i + 1 < NT:
                    if pc_ == 2:
                        pro_load(i + 1)
                    elif pc_ == 3:
                        pro_out(i + 1)
                    elif pc_ == 5:
                        pro_sq(i + 1)

            LEAD = 0 if hf == 0 else 2
            pro_load(0)
            pro_out(0)
            pro_sq(0)
            pro_stat(0)
            pro_norm(0)
            for i in range(NT):
                t0 = i * TT
                x, tx = xt[i % 2], t_xt[i % 2]
                for pc_ in range(LEAD if i > 0 else 0, HC):
                    up_pair(i, pc_)
                if i + 1 < NT:
                    pro_stat(i + 1)
                evac_back(i, HC - 2)
                evac_back(i, HC - 1)
                if i + 1 < NT:
                    pro_norm(i + 1)
                    for pc_ in range(LEAD):
                        up_pair(i + 1, pc_)
                for oc in range(8):
                    b = nb()
                    self.mm(t_ps[b], ps[b][:], [(wd[:, k, oc * 128:(oc + 1) * 128], g[:, k, :]) for k in range(HC)],
                            reads=[t_wd] + t_g)
                    S.op("dve", lambda e, oc=oc, b=b, x=x: e.tensor_tensor(out=x[:, oc, :], in0=ps[b][:], in1=x[:, oc, :], op=ALU.add),
                         reads=[t_ps[b], tx], writes=[tx])
                if not final:
                    S.dma("sp", self.xview(self.XT, t0), x[:], tx, reads=[tx], writes=[self.t_XT])
                else:
                    for c in range(8):
                        S.op("pool", lambda e, c=c, x=x: e.tensor_tensor(out=sq8[:, c, :], in0=x[:, c, :], in1=x[:, c, :], op=ALU.mult),
                             reads=[tx], writes=[t_sq8[c]])
                    for c in range(8):
                        S.op("pe", lambda e, c=c: e.matmul(ps[0][:], lhsT=self.ones_f[:], rhs=sq8[:, c, :], start=(c == 0), stop=(c == 7)),
                             reads=[t_sq8[c], self.t_const], writes=[t_ps[0]], inc=(c == 7))
                    S.op("act", lambda e: e.activation(out=rstd_f[:], in_=ps[0][:], func=AF.Ln, bias=self.eps_c[:], scale=1.0 / D),
                         reads=[t_ps[0], self.t_const], writes=[t_rstd_f])
                    S.op("act", lambda e: e.activation(out=rstd_f[:], in_=rstd_f[:], func=AF.Exp, scale=-0.5), reads=[t_rstd_f], writes=[t_rstd_f])
                    for c in range(8):
                        S.op("dve", lambda e, c=c, x=x: e.scalar_tensor_tensor(out=yo[:, c, :], in0=x[:, c, :], scalar=self.pcol("final_norm", c),
                                                                              in1=rstd_f[:], op0=ALU.mult, op1=ALU.mult),
                             reads=[tx, t_rstd_f, self.t_const], writes=[t_yo[c]])
                    S.dma("sp", self.xview(self.yout, t0), yo[:], t_yoall, reads=t_yo, writes=[self.t_Y])
            S.barrier(release=t_wug + [t_wd] + t_hTall + t_xt + ([t_yoall] if final else []) + ([t_wout, t_mx] if fuse_j is not None else []))

    def phase_final(self, src):
        nc, S = self.nc, self.S
        with ExitStack() as ph:
            xt = [self.sb(ph, f"xt{i}", [128, 8, TT], F32) for i in range(2)]
            t_xt = [T(f"xt{i}") for i in range(2)]
            yo = [self.sb(ph, f"yo{i}", [128, 8, TT], F32) for i in range(2)]
            t_yo = [[T(f"yo{i}_{c}") for c in range(8)] for i in range(2)]
            t_yoall = [T(f"yoall{i}") for i in range(2)]
            sq = [self.sb(ph, f"sq{i}", [128, TT], F32) for i in range(2)]
            t_sq = [T(f"sq{i}") for i in range(2)]
            rstd = self.sb(ph, "rstd", [128, TT], F32)
            t_rstd = T("rstd")
            for i in range(NT):
                t0 = i * TT
                x, tx = xt[i % 2], t_xt[i % 2]
                S.dma("sp", x[:], self.xview(src, t0), tx, reads=[self.t_XT], writes=[tx])
                self.rms_hT(x, tx, yo[i % 2], t_yo[i % 2], "final_norm", sq, t_sq, rstd, t_rstd, self.ps[0], self.t_ps[0])
                S.dma("sp", self.xview(self.yout, t0), yo[i % 2][:], t_yoall[i % 2], reads=t_yo[i % 2], writes=[self.t_Y])
            S.barrier(release=t_xt + t_yoall)

    def phase_odd(self, j):
        self.odd_proj(j)
        self.odd_attn_all()
        self.odd_ret_out(j)

    def odd_proj(self, j):
        nc, S = self.nc, self.S
        with ExitStack() as ph:
            w_in = self.sb(ph, "od_win", [128, 8, 3072], BF16)
            w_sw = self.sb(ph, "od_wsw", [128, 8, 1536], BF16)
            _, win_a, tl_a = self.load_wg(ph, "od_win_a", self.od_w_in[j], D, 3072, [(0, 512)], w=w_in)
            _, wsw_a, tl_b = self.load_wg(ph, "od_wsw_a", self.od_w_sw[j], D, 1536, [(0, 512)], w=w_sw)
            _, win_b, tl_c = self.load_wg(ph, "od_win_b", self.od_w_in[j], D, 3072, [(512, 1024)], w=w_in)
            _, wsw_b, tl_d = self.load_wg(ph, "od_wsw_b", self.od_w_sw[j], D, 1536, [(512, 1024)], w=w_sw)
            _, win_c, tl_e = self.load_wg(ph, "od_win_c", self.od_w_in[j], D, 3072, [(1536, 2048)], w=w_in)
            _, wsw_c, tl_f = self.load_wg(ph, "od_wsw_c", self.od_w_sw[j], D, 1536, [(1024, 1536)], w=w_sw)
            _, win_d, tl_g = self.load_wg(ph, "od_win_d", self.od_w_in[j], D, 3072, [(1024, 1536), (2048, 2560), (2560, 3072)], w=w_in)
            wtl = tl_a + tl_b + tl_c + tl_d + tl_e + tl_f + tl_g

            def t_win(col):
                for f in (win_a, win_b, win_c, win_d):
                    try:
                        return f(col)
                    except KeyError:
                        pass

            def t_wsw(col):
                for f in (wsw_a, wsw_b, wsw_c):
                    try:
                        return f(col)
                    except KeyError:
                        pass
            xt = [self.sb(ph, f"xt{i}", [128, 8, TT], F32) for i in range(2)]
            t_xt = [T(f"xt{i}") for i in range(2)]
            hTb = [self.sb(ph, f"hT{i}", [128, 8, TT], BF16) for i in range(2)]
            t_hTb = [[T(f"hT{i}_{c}") for c in range(8)] for i in range(2)]
            sq = [self.sb(ph, f"sq{i}", [128, TT], F32) for i in range(2)]
            t_sq = [T(f"sq{i}") for i in range(2)]
            rstd = self.sb(ph, "rstd", [128, TT], F32)
            t_rstd = T("rstd")
            rp_ = [self.sb(ph, f"rope{i}", [128, 4, TT], F32) for i in range(2)]
            t_rp = [T(f"rope{i}") for i in range(2)]
            t1 = [self.sb(ph, f"t1_{i}", [128, TT], F32) for i in range(2)]
            t_t1 = [T(f"t1_{i}") for i in range(2)]
            t2 = [self.sb(ph, f"t2_{i}", [128, TT], F32) for i in range(2)]
            t_t2 = [T(f"t2_{i}") for i in range(2)]
            ob = [self.sb(ph, f"ob{i}", [128, TT], BF16) for i in range(4)]
            t_ob = [T(f"ob{i}") for i in range(4)]
            vb = [self.sb(ph, f"vb{i}", [128, 4, 512], BF16) for i in range(2)]
            t_vb = [T(f"vb{i}") for i in range(2)]
            sgo = [self.sb(ph, f"sgo{i}", [128, TT], F32) for i in range(2)]
            t_sgo = [T(f"sgo{i}") for i in range(2)]
            ps, t_ps = self.ps, self.t_ps
            gname = f"od_norm{j}"
            bk = 0
            cnt = 0

            def nb():
                nonlocal bk
                b = 1 + (bk % 6)
                bk += 1
                return b

            qk = []
            for c in range(4):
                qk.append((c * 128, c * 128, self.QT, self.t_QT, c * 128, True))
            for c in range(4):
                qk.append((512 + c * 128, 512 + c * 128, self.KT, self.t_KT, c * 128, False))
            for c in range(2):
                qk.append((1536 + c * 128, 1024 + c * 128, self.QT, self.t_QT, 512 + c * 128, False))
            for c in range(2):
                qk.append((1792 + c * 128, 1280 + c * 128, self.KT, self.t_KT, 512 + c * 128, True))

            def prologue(i):
                S.dma("sp", xt[i % 2][:], self.xview(self.XT, i * TT), t_xt[i % 2], reads=[self.t_XT], writes=[t_xt[i % 2]])
                S.dma("sp", rp_[i % 2][:], self.rope_d.rearrange("a p t -> p a t")[:, :, i * TT:(i + 1) * TT], t_rp[i % 2], writes=[t_rp[i % 2]])
                self.rms_hT(xt[i % 2], t_xt[i % 2], hTb[i % 2], t_hTb[i % 2], gname, sq, t_sq, rstd, t_rstd, ps[0], t_ps[0])

            prologue(0)
            for i in range(NT):
                t0 = i * TT
                x, tx = xt[i % 2], t_xt[i % 2]
                rt, trt = rp_[i % 2], t_rp[i % 2]
                hT, t_hT = hTb[i % 2], t_hTb[i % 2]
                for (mc, sc, dst, tdst, row0, scaled) in qk:
                    ba, bb = nb(), nb()
                    self.mm(t_ps[ba], ps[ba][:], [(w_in[:, k, mc:mc + 128], hT[:, k, :]) for k in range(8)], reads=[t_win(mc)] + t_hT)
                    self.mm(t_ps[bb], ps[bb][:], [(w_sw[:, k, sc:sc + 128], hT[:, k, :]) for k in range(8)], reads=[t_wsw(sc)] + t_hT)
                    ci, si = (2, 3) if scaled else (0, 1)
                    a1, ta1 = t1[cnt % 2], t_t1[cnt % 2]
                    a2, ta2 = t2[cnt % 2], t_t2[cnt % 2]
                    o, to = ob[cnt % 4], t_ob[cnt % 4]
                    cnt += 1
                    S.op("dve", lambda e, a1=a1, ba=ba, ci=ci, rt=rt: e.tensor_tensor(out=a1[:], in0=ps[ba][:], in1=rt[:, ci, :], op=ALU.mult),
                         reads=[t_ps[ba], trt], writes=[ta1])
                    S.op("dve", lambda e, a2=a2, bb=bb, si=si, rt=rt: e.tensor_tensor(out=a2[:], in0=ps[bb][:], in1=rt[:, si, :], op=ALU.mult),
                         reads=[t_ps[bb], trt], writes=[ta2])
                    S.op("pool", lambda e, a1=a1, a2=a2, o=o: e.tensor_tensor(out=o[:], in0=a1[:], in1=a2[:], op=ALU.add),
                         reads=[ta1, ta2], writes=[to])
                    S.dma("sp", dst[row0:row0 + 128, t0:t0 + TT], o[:], to, reads=[to], writes=[tdst])
                if i + 1 < NT:
                    prologue(i + 1)
                for vi, (c0, dst, tdst) in enumerate(((1024, self.V, self.t_V), (2048, self.RV, self.t_RV))):
                    v, tv = vb[vi], t_vb[vi]
                    for ts in range(4):
                        b = nb()
                        self.mm(t_ps[b], ps[b][:], [(hT[:, k, ts * 128:(ts + 1) * 128], w_in[:, k, c0:c0 + 512]) for k in range(8)],
                                reads=[t_win(c0)] + t_hT)
                        S.op("act", lambda e, v=v, ts=ts, b=b: e.activation(out=v[:, ts, :], in_=ps[b][:], func=AF.Copy),
                             reads=[t_ps[b]], writes=[tv])
                    S.dma("sp", dst[t0:t0 + TT, :].rearrange("(ts p) f -> p ts f", p=128), v[:], tv, reads=[tv], writes=[tdst])
                for c in range(4):
                    b = nb()
                    self.mm(t_ps[b], ps[b][:], [(w_in[:, k, 2560 + c * 128:2560 + (c + 1) * 128], hT[:, k, :]) for k in range(8)],
                            reads=[t_win(2560)] + t_hT)
                    sg, tsg = sgo[c % 2], t_sgo[c % 2]
                    S.op("act", lambda e, sg=sg, b=b: e.activation(out=sg[:], in_=ps[b][:], func=AF.Silu), reads=[t_ps[b]], writes=[tsg])
                    S.dma("sp", self.SG[c * 128:(c + 1) * 128, t0:t0 + TT], sg[:], tsg, reads=[tsg], writes=[self.t_SG])
            S.barrier(release=wtl + t_xt + t_rp + t_ob + t_vb + t_sgo)

    def odd_attn_all(self):
        nc, S = self.nc, self.S
        DS = (1, 4, 16)
        with ExitStack() as ph:
            qN = [self.sb(ph, f"qN{i}", [128, S_LEN], BF16) for i in range(2)]
            kN = [self.sb(ph, f"kN{i}", [128, S_LEN], BF16) for i in range(2)]
            t_qN = [T(f"qN{i}") for i in range(2)]
            t_kN = [T(f"kN{i}") for i in range(2)]
            qP = [self.sb(ph, f"qP16_{i}", [128, S_LEN], BF16) for i in range(2)]
            kP = [self.sb(ph, f"kP16_{i}", [128, S_LEN], BF16) for i in range(2)]
            t_qP = [[T(f"qP16_{i}_{k}") for k in range(4)] for i in range(2)]
            t_kP = [[T(f"kP16_{i}_{k}") for k in range(4)] for i in range(2)]
            vt = [{d: self.sb(ph, f"vt{i}_{d}", [128, 32, 128], BF16) for d in DS} for i in range(2)]
            t_vt = [{d: T(f"vt{i}_{d}") for d in DS} for i in range(2)]
            acc = self.sb(ph, "accnd", [64, 2, S_LEN], F32)
            t_acc = T("acc")
            pt = [self.sb(ph, f"pt{i}", [128, 256], BF16) for i in range(6)]
            t_pt = [T(f"pt{i}") for i in range(6)]
            cout = self.sb(ph, "cout", [64, S_LEN], BF16)
            t_cout = T("cout")
            rec = [self.sb(ph, f"rec{i}", [64, TT], F32) for i in range(2)]
            t_rec = [T(f"rec{i}") for i in range(2)]
            ps, t_ps = self.ps, self.t_ps

            def load(hp):
                i = hp % 2
                S.dma("sp", qN[i][:], self.QT[hp * 128:(hp + 1) * 128, :], t_qN[i], reads=[self.t_QT], writes=[t_qN[i]])
                S.dma("sp", kN[i][:], self.KT[hp * 128:(hp + 1) * 128, :], t_kN[i], reads=[self.t_KT], writes=[t_kN[i]])
                for d in DS:
                    nbk = 32 // d
                    vsrc = self.V[:, hp * 128:(hp + 1) * 128].rearrange("(n p r) f -> r p n f", p=128, r=d)
                    for r in range(d):
                        S.dma("sp", vt[i][d][:, r * nbk:(r + 1) * nbk, :], vsrc[r], t_vt[i][d], reads=[self.t_V], writes=[t_vt[i][d]], group=True)

            def perm_piece(hp, k):
                i = hp % 2
                src, tsrc, dst, tdst = (qN[i], t_qN[i], qP[i], t_qP[i]) if k < 4 else (kN[i], t_kN[i], kP[i], t_kP[i])
                kk = k % 4
                S.op("act", lambda e, src=src, dst=dst, kk=kk: e.activation(
                    out=dst[:].rearrange("p (r m) -> p r m", r=16)[:, :, kk * 64:(kk + 1) * 64],
                    in_=src[:, kk * 1024:(kk + 1) * 1024].rearrange("p (m r) -> p r m", r=16), func=AF.Copy),
                    reads=[tsrc], writes=[tdst[kk]])

            load(0)
            for k in range(8):
                perm_piece(0, k)
            gblk = 0
            gpair = 0
            for hp in range(4):
                pi = hp % 2
                if hp + 1 < 4:
                    load(hp + 1)
                pblk = 0
                for hh in range(2):
                    hb = hh * 64
                    blocks = [(d, r, n) for d in DS for r in range(d) for n in range(32 // d)]
                    info = {}

                    def QK(bi):
                        nonlocal gblk, gpair
                        d, r, n = blocks[bi]
                        if d == 1:
                            qa = qN[pi][hb:hb + 64, n * 128:(n + 1) * 128]
                            kc = kN[pi][hb:hb + 64, n * 128:(n + 1) * 128]
                            kp = kN[pi][hb:hb + 64, (n - 1) * 128:n * 128] if n > 0 else None
                            rd = [t_qN[pi], t_kN[pi], self.t_cb]
                        elif d == 4:
                            qv = qN[pi][:].rearrange("p (m r) -> p r m", r=4)
                            kv = kN[pi][:].rearrange("p (m r) -> p r m", r=4)
                            qa = qv[hb:hb + 64, r, n * 128:(n + 1) * 128]
                            kc = kv[hb:hb + 64, r, n * 128:(n + 1) * 128]
                            kp = kv[hb:hb + 64, r, (n - 1) * 128:n * 128] if n > 0 else None
                            rd = [t_qN[pi], t_kN[pi], self.t_cb]
                        else:
                            J0 = r * 256 + n * 128
                            qa = qP[pi][hb:hb + 64, J0:J0 + 128]
                            kc = kP[pi][hb:hb + 64, J0:J0 + 128]
                            kp = kP[pi][hb:hb + 64, J0 - 128:J0] if n > 0 else None
                            rd = t_qP[pi] + t_kP[pi] + [self.t_cb]
                        bs = gblk % 4
                        pbuf = gblk % 6
                        gblk += 1
                        if n % 2 == 0:
                            gpair += 1
                        bo = 4 + gpair % 3
                        info[bi] = (bs, pbuf, bo)
                        lo = 0 if n > 0 else 128
                        pairs = []
                        if n > 0:
                            pairs.append((kp, qa, ps[bs][:, 0:128]))
                        pairs.append((kc, qa, ps[bs][:, 128:256]))
                        for ii, (l, rr, oo) in enumerate(pairs):
                            S.op("pe", lambda e, l=l, rr=rr, oo=oo: e.matmul(oo, lhsT=l, rhs=rr, start=True, stop=True),
                                 reads=rd, writes=[t_ps[bs]], inc=(ii == len(pairs) - 1))

                    def REST(bi):
                        d, r, n = blocks[bi]
                        nbk = 32 // d
                        bs, pbuf, bo = info[bi]
                        p_, tp_ = pt[pbuf], t_pt[pbuf]
                        v_ = vt[pi][d]
                        lo = 0 if n > 0 else 128
                        S.op("act", lambda e, p_=p_, bs=bs, lo=lo: e.activation(out=p_[:, lo:256], in_=ps[bs][:, lo:256], func=AF.Exp),
                             reads=[t_ps[bs]], writes=[tp_])
                        S.op("dve", lambda e, p_=p_, lo=lo: e.tensor_tensor(out=p_[:, lo:256], in0=p_[:, lo:256], in1=self.mask01[:, lo:256], op=ALU.mult),
                             reads=[tp_, self.t_cb], writes=[tp_])
                        vb_ = r * nbk + n
                        c0 = (n % 2) * 128
                        onum = ps[bo][0:64, c0:c0 + 128]
                        oden = ps[bo][0:64, 256 + c0:256 + c0 + 128]
                        pv = []
                        if n > 0:
                            pv.append((v_[:, vb_ - 1, hb:hb + 64], p_[:, 0:128], onum, True, False))
                        pv.append((v_[:, vb_, hb:hb + 64], p_[:, 128:256], onum, n == 0, True))
                        if n > 0:
                            pv.append((self.ones_b[:, 0:64], p_[:, 0:128], oden, True, False))
                        pv.append((self.ones_b[:, 0:64], p_[:, 128:256], oden, n == 0, True))
                        for ii, (l, rr, oo, st, sp_) in enumerate(pv):
                            S.op("pe", lambda e, l=l, rr=rr, oo=oo, st=st, sp_=sp_: e.matmul(
                                oo, lhsT=l, rhs=rr, start=st, stop=sp_, skip_group_check=True),
                                reads=[t_vt[pi][d], tp_, self.t_cb], writes=[t_ps[bo]], inc=(ii == len(pv) - 1))
                        if n % 2 == 1:
                            av = acc[:].rearrange("p a (m r) -> p a r m", r=d)[:, :, r, (n - 1) * 128:(n + 1) * 128]

                            def do_acc(av=av, bo=bo, d=d):
                                if d == 1:
                                    S.op("dve", lambda e: e.tensor_copy(
                                        out=av, in_=ps[bo][0:64, 0:512].rearrange("p (a m) -> p a m", a=2)),
                                        reads=[t_ps[bo]], writes=[t_acc])
                                else:
                                    S.op("dve", lambda e: e.tensor_tensor(
                                        out=av, in0=av, in1=ps[bo][0:64, 0:512].rearrange("p (a m) -> p a m", a=2), op=ALU.add),
                                        reads=[t_ps[bo], t_acc], writes=[t_acc])
                            while pend_acc:
                                pend_acc.pop(0)()
                            pend_acc.append(do_acc)

                    LA = 3
                    pend_acc = []
                    for bi in range(min(LA, len(blocks))):
                        QK(bi)
                    for bi in range(len(blocks)):
                        if bi + LA < len(blocks):
                            QK(bi + LA)
                        REST(bi)
                        pblk += 1
                        if hp + 1 < 4 and pblk % 20 == 0 and pblk // 20 <= 8:
                            perm_piece(hp + 1, pblk // 20 - 1)
                    while pend_acc:
                        pend_acc.pop(0)()
                    for i in range(NT):
                        t0 = i * TT
                        rc, trc = rec[i % 2], t_rec[i % 2]
                        S.op("act", lambda e, rc=rc, t0=t0: e.activation(out=rc[:], in_=acc[:, 1, t0:t0 + TT], func=AF.Ln), reads=[t_acc], writes=[trc])
                        S.op("act", lambda e, rc=rc: e.activation(out=rc[:], in_=rc[:], func=AF.Exp, scale=-1.0), reads=[trc], writes=[trc])
                        S.op("dve", lambda e, rc=rc, t0=t0: e.tensor_tensor(out=cout[:, t0:t0 + TT], in0=acc[:, 0, t0:t0 + TT], in1=rc[:], op=ALU.mult),
                             reads=[t_acc, trc], writes=[t_cout])
                    hg = hp * 2 + hh
                    S.dma("sp", self.MIXT[hg * 64:(hg + 1) * 64, :], cout[:], t_cout, reads=[t_cout], writes=[self.t_MIXT])
            S.barrier(release=t_qN + t_kN + [t_cout] + [t_vt[i][d] for i in range(2) for d in DS])

    def odd_ret_out(self, j):
        nc, S = self.nc, self.S
        NCH = 32
        with ExitStack() as ph:
            rq = self.sb(ph, "rq", [128, S_LEN], BF16)
            rk = self.sb(ph, "rk", [128, S_LEN], BF16)
            rqd = self.sb(ph, "rqd", [128, S_LEN], BF16)
            t_rq, t_rk = T("rq"), T("rk")
            t_rqd = [T(f"rqd{n}") for n in range(NCH)]
            rvt = self.sb(ph, "rvt", [128, NCH, 256], BF16)
            t_rvt = T("rvt")
            kdT = self.sb(ph, "kdT", [128, NCH, 128], BF16)
            t_kdT = [T(f"kdT{n}") for n in range(NCH)]
            yT = [[self.sb(ph, f"yT{r}_{h}", [128, S_LEN], F32) for h in range(2)] for r in range(2)]
            t_yT = [[[T(f"yT{r}_{h}_{i}") for i in range(NT)] for h in range(2)] for r in range(2)]
            Sf = self.sb(ph, "Sf", [128, 128], F32)
            Sb = self.sb(ph, "Sb", [128, 128], BF16)
            t_Sf = [T("Sf0"), T("Sf1")]
            t_Sb = [T("Sb0"), T("Sb1")]
            sm = [self.sb(ph, f"sm{i}", [128, 128], BF16) for i in range(3)]
            t_sm = [T(f"sm{i}") for i in range(3)]
            sq = [self.sb(ph, f"sq{i}", [128, TT], F32) for i in range(2)]
            t_sq = [T(f"sq{i}") for i in range(2)]
            mean2 = [self.sb(ph, f"mean{i}", [128, TT], F32) for i in range(2)]
            t_mean2 = [T(f"mean{i}") for i in range(2)]
            lrs2 = [self.sb(ph, f"lrs{i}", [128, TT], F32) for i in range(2)]
            t_lrs2 = [T(f"lrs{i}") for i in range(2)]
            sgt = [self.sb(ph, f"sgt{i}", [128, TT], F32) for i in range(2)]
            t_sgt = [T(f"sgt{i}") for i in range(2)]
            ro = [self.sb(ph, f"ro{i}", [128, TT], BF16) for i in range(2)]
            t_ro = [T(f"ro{i}") for i in range(2)]
            ps, t_ps = self.ps, self.t_ps
            st = {"bk": 0, "cnt": 0}

            def load_pre(rp):
                S.dma("sp", rq[:], self.QT[512 + rp * 128:512 + (rp + 1) * 128, :], t_rq, reads=[self.t_QT], writes=[t_rq])
                S.dma("sp", rk[:], self.KT[512 + rp * 128:512 + (rp + 1) * 128, :], t_rk, reads=[self.t_KT], writes=[t_rk])
                S.dma("sp", rvt[:], self.RV[:, rp * 256:(rp + 1) * 256].rearrange("(n p) f -> p n f", p=128), t_rvt,
                      reads=[self.t_RV], writes=[t_rvt])
                S.op("pool", lambda e: e.memset(Sf[:], 0.0), writes=t_Sf)
                S.op("pool", lambda e: e.memset(Sb[:], 0.0), writes=t_Sb)
                QD = self.ctab[:, 512 + rp * 128:512 + (rp + 1) * 128]
                KD = self.ctab[:, 768 + rp * 128:768 + (rp + 1) * 128]
                for n in range(NCH):
                    cs = slice(n * 128, (n + 1) * 128)
                    S.op("pool", lambda e, cs=cs, QD=QD: e.tensor_tensor(out=rqd[:, cs], in0=rq[:, cs], in1=QD, op=ALU.mult),
                         reads=[t_rq, self.t_cb], writes=[t_rqd[n]])
                    S.op("pe", lambda e, cs=cs, n=n: e.transpose(out=self.psb[:, (n % 8) * 128:(n % 8 + 1) * 128], in_=rk[:, cs], identity=self.ident[:]),
                         reads=[t_rk, self.t_cb], writes=[self.t_psb])
                    S.op("dve", lambda e, n=n, KD=KD: e.tensor_tensor(out=kdT[:, n, :], in0=self.psb[:, (n % 8) * 128:(n % 8 + 1) * 128], in1=KD, op=ALU.mult),
                         reads=[self.t_psb, self.t_cb], writes=[t_kdT[n]])

            def chunk(rp, n):
                cd = [float(np.exp(128.0 * np.log1p(-(2.0 ** (-5.0 - (2 * rp + hh)))))) for hh in range(2)]
                cs = slice(n * 128, (n + 1) * 128)
                for hh in range(2):
                    hb = hh * 64
                    h = 2 * rp + hh
                    bk = st["bk"]
                    st["bk"] += 1
                    bs, by = bk % 2, 2 + bk % 2
                    s_, ts_ = sm[bk % 3], t_sm[bk % 3]
                    self.mm(t_ps[bs], ps[bs][:, 0:128], [(rk[hb:hb + 64, cs], rq[hb:hb + 64, cs])], reads=[t_rk, t_rq])
                    S.op("dve", lambda e, s_=s_, bs=bs, h=h: e.tensor_tensor(out=s_[:], in0=ps[bs][:, 0:128], in1=self.ctab[:, h * 128:(h + 1) * 128], op=ALU.mult),
                         reads=[t_ps[bs], self.t_cb], writes=[ts_])
                    pairs = [(rvt[:, n, hh * 128:(hh + 1) * 128], s_[:])]
                    if n > 0:
                        pairs.append((Sb[hb:hb + 64, :], rqd[hb:hb + 64, cs]))
                    self.mm(t_ps[by], ps[by][:, 0:128], pairs, reads=[t_rvt, ts_, t_Sb[hh], t_rqd[n]])
                    S.op("act", lambda e, hh=hh, cs=cs, by=by: e.activation(out=yT[rp][hh][:, cs], in_=ps[by][:, 0:128], func=AF.Copy),
                         reads=[t_ps[by]], writes=[t_yT[rp][hh][n // 4]])
                if n < NCH - 1:
                    self.mm(t_ps[6], ps[6][:, 0:256], [(kdT[:, n, :], rvt[:, n, :])], reads=[t_kdT[n], t_rvt])
                    for hh in range(2):
                        hb = hh * 64
                        S.op("dve", lambda e, hb=hb, hh=hh: e.scalar_tensor_tensor(
                            out=Sf[hb:hb + 64, :], in0=Sf[hb:hb + 64, :], scalar=cd[hh], in1=ps[6][hb:hb + 64, hh * 128:(hh + 1) * 128],
                            op0=ALU.mult, op1=ALU.add), reads=[t_ps[6], t_Sf[hh]], writes=[t_Sf[hh]])
                        S.op("act", lambda e, hb=hb: e.activation(out=Sb[hb:hb + 64, :], in_=Sf[hb:hb + 64, :], func=AF.Copy),
                             reads=[t_Sf[hh]], writes=[t_Sb[hh]])

            def hn_tile(rp, hh, i):
                h = 2 * rp + hh
                t0 = i * TT
                y_ = yT[rp][hh][:, t0:t0 + TT]
                ty = t_yT[rp][hh][i]
                cnt = st["cnt"]
                st["cnt"] += 1
                sgx, tsgx = sgt[cnt % 2], t_sgt[cnt % 2]
                r_, tr_ = ro[cnt % 2], t_ro[cnt % 2]
                sq_, tsq_ = sq[cnt % 2], t_sq[cnt % 2]
                bm, bv = 4, 5
                mean, t_mean = mean2[cnt % 2], t_mean2[cnt % 2]
                lrs, t_lrs = lrs2[cnt % 2], t_lrs2[cnt % 2]
                S.dma("sp", sgx[:], self.SG[h * 128:(h + 1) * 128, t0:t0 + TT], tsgx, reads=[self.t_SG], writes=[tsgx])
                S.op("pe", lambda e: e.matmul(ps[bm][:], lhsT=self.ones_f[:], rhs=y_, start=True, stop=True),
                     reads=[ty, self.t_const], writes=[t_ps[bm]])
                S.op("act", lambda e: e.activation(out=sq_[:], in_=y_, func=AF.Square), reads=[ty], writes=[tsq_])
                S.op("pe", lambda e: e.matmul(ps[bv][:], lhsT=self.ones_f[:], rhs=sq_[:], start=True, stop=True),
                     reads=[tsq_, self.t_const], writes=[t_ps[bv]])
                S.op("act", lambda e: e.activation(out=mean[:], in_=ps[bm][:], func=AF.Identity, scale=1.0 / 128),
                     reads=[t_ps[bm]], writes=[t_mean])
                S.op("act", lambda e: e.activation(out=lrs[:], in_=mean[:], func=AF.Square), reads=[t_mean], writes=[t_lrs])
                S.op("dve", lambda e: e.scalar_tensor_tensor(out=lrs[:], in0=ps[bv][:], scalar=1.0 / 128, in1=lrs[:],
                                                             op0=ALU.mult, op1=ALU.subtract),
                     reads=[t_ps[bv], t_lrs], writes=[t_lrs])
                S.op("act", lambda e: e.activation(out=lrs[:], in_=lrs[:], func=AF.Ln, bias=self.eps_c[:], scale=1.0),
                     reads=[t_lrs, self.t_const], writes=[t_lrs])
                S.op("act", lambda e: e.activation(out=lrs[:], in_=lrs[:], func=AF.Exp, scale=-0.5), reads=[t_lrs], writes=[t_lrs])
                S.op("dve", lambda e: e.tensor_tensor(out=y_, in0=y_, in1=mean[:], op=ALU.subtract), reads=[ty, t_mean], writes=[ty])
                S.op("dve", lambda e: e.tensor_tensor(out=y_, in0=y_, in1=lrs[:], op=ALU.mult), reads=[ty, t_lrs], writes=[ty])
                S.op("dve", lambda e: e.tensor_tensor(out=r_[:], in0=y_, in1=sgx[:], op=ALU.mult), reads=[ty, tsgx], writes=[tr_])
                S.dma("sp", self.MIXT[512 + h * 128:512 + (h + 1) * 128, t0:t0 + TT], r_[:], tr_, reads=[tr_], writes=[self.t_MIXT])

            load_pre(0)
            for n in range(NCH):
                chunk(0, n)
            load_pre(1)
            hn_list = [(0, hh, i) for hh in range(2) for i in range(NT)]
            for n in range(NCH):
                chunk(1, n)
                if n % 2 == 1 and hn_list:
                    hn_tile(*hn_list.pop(0))
            while hn_list:
                hn_tile(*hn_list.pop(0))
            for hh in range(2):
                for i in range(NT):
                    hn_tile(1, hh, i)
            S.barrier(release=[t_rq, t_rk, t_rvt] + t_sgt + t_ro)

    def odd_out(self, j):
        nc, S = self.nc, self.S
        with ExitStack() as ph:
            w_out, t_wout = self.load_w(ph, "od_wout", self.od_w_out[j], D, D)
            xt = [self.sb(ph, f"xt{i}", [128, 8, TT], F32) for i in range(2)]
            t_xt = [T(f"xt{i}") for i in range(2)]
            mx = [self.sb(ph, f"mx{i}", [128, 8, TT], BF16) for i in range(2)]
            t_mx = [T(f"mx{i}") for i in range(2)]
            ps, t_ps = self.ps, self.t_ps
            for i in range(NT):
                t0 = i * TT
                x, tx = xt[i % 2], t_xt[i % 2]
                m, tm = mx[i % 2], t_mx[i % 2]
                S.dma("sp", x[:], self.xview(self.XT, t0), tx, reads=[self.t_XT], writes=[tx])
                S.dma("sp", m[:], self.xview(self.MIXT, t0), tm, reads=[self.t_MIXT], writes=[tm])
                for oc in range(8):
                    b = oc % 6
                    self.mm(t_ps[b], ps[b][:], [(w_out[:, k, oc * 128:(oc + 1) * 128], m[:, k, :]) for k in range(8)], reads=[t_wout, tm])
                    S.op("dve", lambda e, oc=oc, b=b, x=x: e.tensor_tensor(out=x[:, oc, :], in0=ps[b][:], in1=x[:, oc, :], op=ALU.add),
                         reads=[t_ps[b], tx], writes=[tx])
                S.dma("sp", self.xview(self.XT, t0), x[:], tx, reads=[tx], writes=[self.t_XT])
            S.barrier(release=[t_wout] + t_xt + t_mx)


def const_tables():
    t = np.arange(S_LEN, dtype=np.float32)
    inv = (np.float32(10000.0) ** (-(np.arange(0, 64, 2, dtype=np.float32)) / np.float32(64))).astype(np.float32)
    ang = (t[:, None] * inv[None, :]).astype(np.float32)
    cos = np.cos(ang).astype(np.float32).T
    sin = np.sin(ang).astype(np.float32).T
    cos64 = np.concatenate([cos, cos], 0)
    sins64 = np.concatenate([-sin, sin], 0)
    cos128 = np.concatenate([cos64, cos64], 0)
    sin128 = np.concatenate([sins64, sins64], 0)
    rope = np.stack([cos128, sin128, cos128 * np.float32(0.125), sin128 * np.float32(0.125)]).astype(np.float32)
    return np.ascontiguousarray(rope)


def ret_tables():
    i = np.arange(128, dtype=np.float64)
    tab = np.zeros((128, 1024), np.float64)
    for h in range(4):
        lg = np.log1p(-(2.0 ** (-5.0 - h)))
        diff = i[None, :] - i[:, None]
        tab[:, h * 128:(h + 1) * 128] = np.where(diff >= 0, np.exp(np.maximum(diff, 0.0) * lg), 0.0)
        rp, hh = h // 2, h % 2
        qd = np.exp((i + 1.0) * lg)
        tab[hh * 64:(hh + 1) * 64, 512 + rp * 128:512 + (rp + 1) * 128] = qd[None, :]
        kd = np.exp((127.0 - i) * lg)
        tab[:, 768 + rp * 128 + hh * 64:768 + rp * 128 + (hh + 1) * 64] = kd[:, None]
    return np.ascontiguousarray(tab.astype(np.float32))


def mask_tables():
    ki = np.arange(128)[:, None]
    qi = np.arange(128)[None, :]
    m = np.zeros((128, 640), np.float32)
    m[:, 0:128] = np.where(ki >= qi, 0.0, -30000.0)
    m[:, 128:256] = np.where(ki <= qi, 0.0, -30000.0)
    m[:, 256:384] = np.eye(128, dtype=np.float32)
    m[:, 384:512] = np.where(ki >= qi, 1.0, 0.0)
    m[:, 512:640] = np.where(ki <= qi, 1.0, 0.0)
    return m


def kernel(**inp):
    inp = {k: np.asarray(v) for k, v in inp.items()}
    return run(inp, DEPTH)


def run(inp, n_layers, cores=8, trace=False):
    pc = param_layout(inp)
    prm = pc.build()
    b = Builder(n_layers, pc.off, pc.n)
    nc = b.build()
    x = inp["x"].astype(np.float32)
    f32 = lambda a: np.ascontiguousarray(np.asarray(a, np.float32))
    swp = []
    for (a0, a1) in ((0, 512), (512, 1024), (1536, 1792), (1792, 2048)):
        blk = inp["od_w_in"][:, :, a0:a1]
        nh = (a1 - a0) // 64
        blk = blk.reshape(2, D, nh, 2, 32)[:, :, :, ::-1, :].reshape(2, D, a1 - a0)
        swp.append(blk)
    od_w_sw = f32(np.concatenate(swp, axis=2))
    shared = {
        "prm": prm, "ev_w_in": f32(inp["ev_w_in"]), "ev_w_out": f32(inp["ev_w_out"]),
        "od_w_in": f32(inp["od_w_in"]), "od_w_sw": od_w_sw, "od_w_out": f32(inp["od_w_out"]),
        "ffn_w_up": f32(inp["ffn_w_up"]), "ffn_w_down": f32(inp["ffn_w_down"]),
        "rope": const_tables(), "ctab": ret_tables(), "maskb": mask_tables(),
    }
    in_maps = []
    for c in range(cores):
        m = dict(shared)
        m["xT"] = np.ascontiguousarray(x[c].T)
        in_maps.append(m)
    res = run_bass_kernel_spmd(nc, in_maps, core_ids=list(range(cores)), trace=trace)
    out = np.stack([np.ascontiguousarray(res.results[c]["yT"].T) for c in range(cores)], axis=0)
    if trace:
        return out.astype(np.float32), res
    return out.astype(np.float32)
```

```python
import numpy as np
from contextlib import ExitStack
import concourse.bass as bass
import concourse.mybir as mybir
from concourse.bass_utils import run_bass_kernel_spmd

F32 = mybir.dt.float32
BF16 = mybir.dt.bfloat16
AF = mybir.ActivationFunctionType
ALU = mybir.AluOpType

D = 1024
S_LEN = 4096
TT = 512
NT = S_LEN // TT
DEPTH = 4
DFF = 2816
EPS = 1e-6
SEM_LIMIT = 30000


class T:
    def __init__(self, name="", excl=False):
        self.name = name
        self.w = None
        self.r = {}
        self.excl = excl
        self.dsem = None
        self.dcount = 0


class Sched:
    def __init__(self, nc, ctx):
        self.nc = nc
        self.ctx = ctx
        self.eng = {"pe": nc.tensor, "act": nc.scalar, "dve": nc.vector,
                    "pool": nc.gpsimd, "sp": nc.sync}
        self.sems = {}
        self.semeng = {}
        self.semcnt = {}
        self.nsem = 0
        self.cursem = {}
        self.cnt = {}
        self.pending = {}
        self.waited = {e: {} for e in self.eng}
        self.dma_pool = []
        for e in self.eng:
            self.cursem[e] = self._newsem(e)
            self.cnt[e] = 0
            self.pending[e] = False
        self.ninstr = 0

    def _newsem(self, e):
        key = self.nsem
        self.nsem += 1
        self.sems[key] = self.ctx.enter_context(self.nc.semaphore(f"s{key}"))
        self.semeng[key] = e
        self.semcnt[key] = 0
        return key

    def _deps(self, e, reads, writes):
        deps = {}

        def add(tok):
            if tok is None:
                return
            k, v = tok
            if deps.get(k, 0) < v:
                deps[k] = v

        for t in reads:
            add(t.w)
            if t.excl:
                for k, v in t.r.items():
                    if self.semeng[k] != e:
                        add((k, v))
        for t in writes:
            add(t.w)
            for k, v in t.r.items():
                add((k, v))
        out = []
        for k, v in deps.items():
            if e == "pe" and self.semeng[k] == "pe":
                continue
            if self.waited[e].get(k, 0) >= v:
                continue
            self.waited[e][k] = v
            out.append((k, v))
        return out

    def _emit(self, e, fn, waits, inc):
        eng = self.eng[e]
        for (k, v) in waits[1:]:
            eng.wait_ge(self.sems[k], v)
            self.ninstr += 1
        ins = fn(eng)
        if waits:
            k, v = waits[0]
            ins._wait_ge(self.sems[k], v)
        if inc is not None:
            ins.then_inc(self.sems[inc[0]], inc[1])
            self.semcnt[inc[0]] += inc[1]
        self.ninstr += 1
        return ins

    def op(self, e, fn, reads=(), writes=(), inc=True):
        waits = self._deps(e, reads, writes)
        if inc and not self.pending[e] and self.cnt[e] >= SEM_LIMIT:
            self.cursem[e] = self._newsem(e)
            self.cnt[e] = 0
        sk = self.cursem[e]
        if inc:
            self.cnt[e] += 1
            tok = (sk, self.cnt[e])
            self.pending[e] = False
        else:
            tok = (sk, self.cnt[e] + 1)
            self.pending[e] = True
        self._emit(e, fn, waits, (sk, 1) if inc else None)
        for t in reads:
            if t.r.get(tok[0], 0) < tok[1]:
                t.r[tok[0]] = tok[1]
        for t in writes:
            t.w = tok
            t.r = {}
        return tok

    def dma(self, q, out_ap, in_ap, semtile, reads=(), writes=(), group=False):
        if semtile.dsem is None:
            if self.dma_pool:
                semtile.dsem = self.dma_pool.pop()
                semtile.dcount = self.semcnt[semtile.dsem]
            else:
                semtile.dsem = self._newsem(None)
        sk = semtile.dsem
        saved = []
        if group:
            saved = [(t, t.w) for t in writes if t.w is not None and t.w[0] == sk]
            for t, _ in saved:
                t.w = None
        waits = self._deps(q, reads, writes)
        for t, w in saved:
            t.w = w
        semtile.dcount += 16
        tok = (sk, semtile.dcount)
        self._emit(q, lambda eng: eng.dma_start(out=out_ap, in_=in_ap), waits, (sk, 16))
        for t in reads:
            if t.r.get(sk, 0) < tok[1]:
                t.r[sk] = tok[1]
        for t in writes:
            t.w = tok
            t.r = {}
        return tok

    def barrier(self, release=()):
        toks = []
        for f in self.eng:
            assert not self.pending[f]
            if self.cnt[f] > 0:
                toks.append((self.cursem[f], self.cnt[f]))
        for k, e in self.semeng.items():
            if e is None and self.semcnt[k] > 0:
                toks.append((k, self.semcnt[k]))
        for e in self.eng:
            for (k, v) in toks:
                if self.waited[e].get(k, 0) >= v:
                    continue
                self.waited[e][k] = v
                self.eng[e].wait_ge(self.sems[k], v)
                self.ninstr += 1
        for t in release:
            if t.dsem is not None:
                self.dma_pool.append(t.dsem)
                t.dsem = None


def _cols(v):
    v = np.asarray(v, np.float32)
    return np.ascontiguousarray(v.reshape(-1, 128).T)


def _conv_cols(w):
    w = np.asarray(w, np.float32)
    K, C = w.shape
    return np.ascontiguousarray(w.T.reshape(C // 128, 128, K).transpose(1, 0, 2).reshape(128, -1))


class PCols:
    def __init__(self):
        self.blocks = []
        self.off = {}
        self.n = 0

    def add(self, name, arr):
        self.off[name] = self.n
        self.blocks.append(arr)
        self.n += arr.shape[1]

    def build(self):
        return np.ascontiguousarray(np.concatenate(self.blocks, axis=1))


def param_layout(inp):
    pc = PCols()
    for j in range(2):
        pc.add(f"ev_norm{j}", _cols(inp["ev_norm"][j]))
        pc.add(f"ev_aconv{j}", _conv_cols(inp["ev_a_conv"][j]))
        pc.add(f"ev_aconvb{j}", _cols(inp["ev_a_conv_b"][j]))
        pc.add(f"ev_lng{j}", _cols(inp["ev_a_ln_g"][j]))
        pc.add(f"ev_lnb{j}", _cols(inp["ev_a_ln_b"][j]))
        pc.add(f"ev_bconv{j}", _conv_cols(inp["ev_b_conv"][j]))
        pc.add(f"od_norm{j}", _cols(inp["od_norm"][j]))
    for l in range(DEPTH):
        pc.add(f"ffn_norm{l}", _cols(inp["ffn_norm"][l]))
        pc.add(f"ffn_conv{l}", _conv_cols(inp["ffn_conv"][l]))
        pc.add(f"ffn_convb{l}", _cols(inp["ffn_conv_b"][l]))
    pc.add("final_norm", _cols(inp["final_norm"]))
    return pc


class Builder:
    def __init__(self, n_layers, pc_off, pc_n):
        self.n_layers = n_layers
        self.off = pc_off
        nc = self.nc = bass.Bass("TRN2", target_bir_lowering=False)
        dt = nc.dram_tensor
        self.xin = dt("xT", [D, S_LEN], F32, kind="ExternalInput").ap()
        self.prm_d = dt("prm", [128, pc_n], F32, kind="ExternalInput").ap()
        self.ev_w_in = dt("ev_w_in", [2, D, 2560], F32, kind="ExternalInput").ap()
        self.ev_w_out = dt("ev_w_out", [2, D, D], F32, kind="ExternalInput").ap()
        self.od_w_in = dt("od_w_in", [2, D, 3072], F32, kind="ExternalInput").ap()
        self.od_w_sw = dt("od_w_sw", [2, D, 1536], F32, kind="ExternalInput").ap()
        self.od_w_out = dt("od_w_out", [2, D, D], F32, kind="ExternalInput").ap()
        self.w_up = dt("ffn_w_up", [DEPTH, D, 2 * DFF], F32, kind="ExternalInput").ap()
        self.w_down = dt("ffn_w_down", [DEPTH, DFF, D], F32, kind="ExternalInput").ap()
        self.rope_d = dt("rope", [4, 128, S_LEN], F32, kind="ExternalInput").ap()
        self.ctab_d = dt("ctab", [128, 1024], F32, kind="ExternalInput").ap()
        self.maskb_d = dt("maskb", [128, 640], F32, kind="ExternalInput").ap()
        self.yout = dt("yT", [D, S_LEN], F32, kind="ExternalOutput").ap()
        self.XT = dt("XT_s", [D, S_LEN], F32, kind="Internal").ap()
        self.HT = dt("HT_s", [D, S_LEN], BF16, kind="Internal").ap()
        self.QT = dt("QT_s", [768, S_LEN], BF16, kind="Internal").ap()
        self.KT = dt("KT_s", [768, S_LEN], BF16, kind="Internal").ap()
        self.V = dt("V_s", [S_LEN, 512], BF16, kind="Internal").ap()
        self.RV = dt("RV_s", [S_LEN, 512], BF16, kind="Internal").ap()
        self.SG = dt("SG_s", [512, S_LEN], F32, kind="Internal").ap()
        self.MIXT = dt("MIXT_s", [D, S_LEN], BF16, kind="Internal").ap()
        self.t_XT, self.t_HT, self.t_QT, self.t_KT = T("XT"), T("HT"), T("QT"), T("KT")
        self.t_V, self.t_RV, self.t_SG, self.t_MIXT, self.t_Y = T("V"), T("RV"), T("SG"), T("MIXT"), T("Y")

    def sb(self, ph, name, shape, dtype):
        self._uid = getattr(self, "_uid", 0) + 1
        return ph.enter_context(self.nc.sbuf_tensor(f"{name}_u{self._uid}", shape, dtype))

    def mm(self, bank_t, out_ap, pairs, reads, start=True, stop=True, skip=False):
        n = len(pairs)
        for i, (l, r) in enumerate(pairs):
            st = start and i == 0
            sp_ = stop and i == n - 1
            kw = dict(skip_group_check=True) if skip else {}
            self.S.op("pe", lambda e, l=l, r=r, st=st, sp_=sp_, kw=kw: e.matmul(out_ap, lhsT=l, rhs=r, start=st, stop=sp_, **kw),
                      reads=reads, writes=[bank_t], inc=(i == n - 1))

    def pcol(self, name, c):
        o = self.off[name] + c
        return self.prm[:, o:o + 1]

    def xview(self, dram, t0):
        return dram.rearrange("(c p) t -> p c t", p=128)[:, :, t0:t0 + TT]

    def load_w(self, ph, name, src2d, rows, cols, q="pool"):
        kc = rows // 128
        w = self.sb(ph, name, [128, kc, cols], BF16)
        t = T(name)
        v = src2d.rearrange("(kc p) n -> p kc n", p=128)
        for k in range(kc):
            self.S.dma(q, w[:, k, :], v[:, k, :], t, writes=[t], group=True)
        return w, t

    def load_wg(self, ph, name, src2d, rows, cols, groups, w=None, dst0=0, q="pool"):
        kc = rows // 128
        if w is None:
            w = self.sb(ph, name, [128, kc, cols], BF16)
        v = src2d.rearrange("(kc p) n -> p kc n", p=128)
        tiles = []
        for gi, (c0, c1) in enumerate(groups):
            t = T(f"{name}_g{gi}")
            for k in range(kc):
                self.S.dma(q, w[:, k, dst0 + c0:dst0 + c1], v[:, k, c0:c1], t, writes=[t], group=True)
            tiles.append((c0, c1, t))

        def tile_for(col):
            for (c0, c1, t) in tiles:
                if c0 <= col < c1:
                    return t
            raise KeyError(col)
        return w, tile_for, [t for (_, _, t) in tiles]

    def rms_hT(self, xt, t_xt, hT, t_hT, gname, sq, t_sq, rstd, t_rstd, bank, t_bank):
        S = self.S
        for c in range(8):
            b = c % 2
            S.op("act", lambda e, c=c, b=b: e.activation(out=sq[b][:], in_=xt[:, c, :], func=AF.Square),
                 reads=[t_xt], writes=[t_sq[b]])
            S.op("pe", lambda e, c=c, b=b: e.matmul(bank[:], lhsT=self.ones_f[:], rhs=sq[b][:], start=(c == 0), stop=(c == 7)),
                 reads=[t_sq[b], self.t_const], writes=[t_bank], inc=True)
        S.op("act", lambda e: e.activation(out=rstd[:], in_=bank[:], func=AF.Sqrt, bias=self.eps_c[:], scale=1.0 / D),
             reads=[t_bank, self.t_const], writes=[t_rstd])
        S.op("dve", lambda e: e.reciprocal(out=rstd[:], in_=rstd[:]), reads=[t_rstd], writes=[t_rstd])
        for c in range(8):
            S.op("dve", lambda e, c=c: e.scalar_tensor_tensor(out=hT[:, c, :], in0=xt[:, c, :], scalar=self.pcol(gname, c),
                                                              in1=rstd[:], op0=ALU.mult, op1=ALU.mult),
                 reads=[t_xt, t_rstd, self.t_const], writes=[t_hT[c]])

    def build(self):
        nc = self.nc
        with ExitStack() as ctx:
            S = self.S = Sched(nc, ctx)
            self.prm = nc.alloc_sbuf_tensor("prm_sb", [128, self.prm_d.shape[1]], F32)
            self.ones_f = nc.alloc_sbuf_tensor("ones_f", [128, 128], F32)
            self.eps_c = nc.alloc_sbuf_tensor("eps_c", [128, 1], F32)
            self.t_const = T("const")
            S.dma("sp", self.prm[:], self.prm_d, self.t_const, writes=[self.t_const])
            S.op("pool", lambda e: e.memset(self.ones_f[:], 1.0), writes=[self.t_const])
            S.op("pool", lambda e: e.memset(self.eps_c[:], EPS), writes=[self.t_const])
            self.ctab = nc.alloc_sbuf_tensor("ctab_sb", [128, 1024], F32)
            self.maskb = nc.alloc_sbuf_tensor("maskb_sb", [128, 256], BF16)
            self.ident = nc.alloc_sbuf_tensor("ident_sb", [128, 128], BF16)
            self.ones_b = nc.alloc_sbuf_tensor("ones_b", [128, 64], BF16)
            self.t_cb = T("constb")
            S.dma("sp", self.ctab[:], self.ctab_d, self.t_cb, writes=[self.t_cb])
            S.barrier()
            S.dma("pool", self.maskb[:], self.maskb_d[:, 0:256], self.t_cb, writes=[self.t_cb])
            S.barrier()
            S.dma("pool", self.ident[:], self.maskb_d[:, 256:384], self.t_cb, writes=[self.t_cb])
            S.barrier()
            self.mask01 = nc.alloc_sbuf_tensor("mask01_sb", [128, 256], BF16)
            S.dma("pool", self.mask01[:], self.maskb_d[:, 384:640], self.t_cb, writes=[self.t_cb])
            S.op("pool", lambda e: e.memset(self.ones_b[:], 1.0), writes=[self.t_cb])
            self.ps = [nc.alloc_psum_tensor(f"ps{i}", [128, 512], F32) for i in range(7)]
            self.t_ps = [T(f"ps{i}", excl=True) for i in range(7)]
            self.psb = nc.alloc_psum_tensor("psb", [128, 1024], BF16)
            self.t_psb = T("psb", excl=True)
            S.barrier()

            src = self.xin
            for layer in range(self.n_layers):
                j = layer // 2
                if layer % 2 == 0:
                    self.phase_even(j, src)
                else:
                    self.phase_odd(j)
                src = self.XT
                self.phase_ffn(layer, 0, fuse_j=(j if layer % 2 == 1 else None))
                self.phase_ffn(layer, 1, final=(layer == self.n_layers - 1))
            print("ninstr", S.ninstr, "nsem", S.nsem, flush=True)
        return nc

    def phase_even(self, j, src):
        nc, S = self.nc, self.S
        with ExitStack() as ph:
            w_in, win_t, win_tl = self.load_wg(ph, "ev_win", self.ev_w_in[j], D, 2560,
                                               [(512, 1024), (0, 512), (1536, 2048), (2048, 2560), (1024, 1536)])
            w_out, t_wout = self.load_w(ph, "ev_wout", self.ev_w_out[j], D, D)
            xt = [self.sb(ph, f"xt{i}", [128, 8, TT], F32) for i in range(2)]
            t_xt = [T(f"xt{i}") for i in range(2)]
            hTb = [self.sb(ph, f"hT{i}", [128, 8, TT], BF16) for i in range(2)]
            t_hTb = [[T(f"hT{i}_{c}") for c in range(8)] for i in range(2)]
            sq = [self.sb(ph, f"sq{i}", [128, TT], F32) for i in range(2)]
            t_sq = [T(f"sq{i}") for i in range(2)]
            rstd = self.sb(ph, "rstd", [128, TT], F32)
            t_rstd = T("rstd")
            abuf = [self.sb(ph, f"abuf{c}", [128, 30 + TT], BF16) for c in range(4)]
            dg = self.sb(ph, "dg", [128, 124, 128], BF16)
            t_dg = T("dg")
            t_abuf = [T(f"abuf{c}") for c in range(4)]
            acv = [self.sb(ph, f"acv{c}", [128, TT], F32) for c in range(4)]
            t_acv = [T(f"acv{c}") for c in range(4)]
            acp = [self.sb(ph, f"acp{c}", [128, TT], F32) for c in range(2)]
            t_acp = [T(f"acp{c}") for c in range(2)]
            sig = [self.sb(ph, f"sig{i}", [128, TT], F32) for i in range(2)]
            t_sig = [T(f"sig{i}") for i in range(2)]
            ctmp = [self.sb(ph, f"ctmp{i}", [128, TT], F32) for i in range(3)]
            t_ctmp = [T(f"ctmp{i}") for i in range(3)]
            ktmp = 0
            bbuf = [self.sb(ph, f"bbuf{c}", [128, 2 + TT], F32) for c in range(4)]
            t_bbuf = [T(f"bbuf{c}") for c in range(4)]
            bacc = [self.sb(ph, f"bacc{i}", [128, TT], F32) for i in range(2)]
            t_bacc = [T(f"bacc{i}") for i in range(2)]
            mix = self.sb(ph, "mix", [128, 8, TT], BF16)
            t_mix = [T(f"mix{c}") for c in range(8)]
            mean = self.sb(ph, "mean", [128, TT], F32)
            t_mean = T("mean")
            lrs = self.sb(ph, "lrs", [128, TT], F32)
            t_lrs = T("lrs")
            ps, t_ps = self.ps, self.t_ps
            for c in range(4):
                S.op("pool", lambda e, c=c: e.memset(abuf[c][:, 0:30], 0.0), writes=[t_abuf[c]])
                S.op("pool", lambda e, c=c: e.memset(bbuf[c][:, 0:2], 0.0), writes=[t_bbuf[c]])
            gname = f"ev_norm{j}"
            for c in range(4):
                for k in range(31):
                    wcol = self.off[f"ev_aconv{j}"] + c * 31 + k
                    S.op("dve", lambda e, c=c, k=k, wcol=wcol: e.tensor_scalar(out=dg[:, c * 31 + k, :], in0=self.ident[:],
                                                                              scalar1=self.prm[:, wcol:wcol + 1], scalar2=None, op0=ALU.mult),
                         reads=[self.t_cb, self.t_const], writes=[t_dg])
            bk = 0

            def nb():
                nonlocal bk
                b = 1 + (bk % 6)
                bk += 1
                return b

            def prologue(i):
                S.dma("sp", xt[i % 2][:], self.xview(src, i * TT), t_xt[i % 2], reads=[self.t_XT], writes=[t_xt[i % 2]])
                self.rms_hT(xt[i % 2], t_xt[i % 2], hTb[i % 2], t_hTb[i % 2], gname, sq, t_sq, rstd, t_rstd, ps[0], t_ps[0])

            prologue(0)
            for i in range(NT):
                t0 = i * TT
                x, tx = xt[i % 2], t_xt[i % 2]
                hT, t_hT = hTb[i % 2], t_hTb[i % 2]

                def proj(col0, b):
                    self.mm(t_ps[b], ps[b][:], [(w_in[:, k, col0:col0 + 128], hT[:, k, :]) for k in range(8)],
                            reads=[win_t(col0)] + t_hT)

                for c in range(4):
                    bg, bv = nb(), nb()
                    proj(512 + c * 128, bg)
                    proj(c * 128, bv)
                    sg, tsg = sig[c % 2], t_sig[c % 2]
                    S.op("act", lambda e, bg=bg, sg=sg: e.activation(out=sg[:], in_=ps[bg][:], func=AF.Sigmoid),
                         reads=[t_ps[bg]], writes=[tsg])
                    S.op("dve", lambda e, bv=bv, sg=sg, c=c: e.tensor_tensor(out=abuf[c][:, 30:30 + TT], in0=ps[bv][:], in1=sg[:], op=ALU.mult),
                         reads=[t_ps[bv], tsg], writes=[t_abuf[c]])
                    wc = self.off[f"ev_aconv{j}"] + c * 31
                    bcv = nb()
                    self.mm(t_ps[bcv], ps[bcv][:], [(dg[:, c * 31 + k, :], abuf[c][:, k:k + TT]) for k in range(31)],
                            reads=[t_dg, t_abuf[c]])
                    S.op("act", lambda e, c=c, bcv=bcv: e.activation(out=acv[c][:], in_=ps[bcv][:], func=AF.Identity,
                                                                     bias=self.pcol(f"ev_aconvb{j}", c), scale=1.0),
                         reads=[t_ps[bcv], self.t_const], writes=[t_acv[c]])
                    S.op("pool", lambda e, c=c: e.tensor_copy(out=abuf[c][:, 0:30], in_=abuf[c][:, TT:TT + 30]),
                         reads=[t_abuf[c]], writes=[t_abuf[c]])
                bm, bv2 = nb(), nb()
                for c in range(4):
                    S.op("pe", lambda e, c=c, bm=bm: e.matmul(ps[bm][:], lhsT=self.ones_f[:], rhs=acv[c][:], start=(c == 0), stop=(c == 3)),
                         reads=[t_acv[c], self.t_const], writes=[t_ps[bm]])
                for c in range(4):
                    b = c % 2
                    S.op("act", lambda e, c=c, b=b: e.activation(out=sq[b][:], in_=acv[c][:], func=AF.Square),
                         reads=[t_acv[c]], writes=[t_sq[b]])
                    S.op("pe", lambda e, c=c, b=b, bv2=bv2: e.matmul(ps[bv2][:], lhsT=self.ones_f[:], rhs=sq[b][:], start=(c == 0), stop=(c == 3)),
                         reads=[t_sq[b], self.t_const], writes=[t_ps[bv2]])
                S.op("act", lambda e, bm=bm: e.activation(out=mean[:], in_=ps[bm][:], func=AF.Identity, scale=1.0 / 512),
                     reads=[t_ps[bm]], writes=[t_mean])
                S.op("act", lambda e: e.activation(out=lrs[:], in_=mean[:], func=AF.Square), reads=[t_mean], writes=[t_lrs])
                S.op("dve", lambda e, bv2=bv2: e.scalar_tensor_tensor(out=lrs[:], in0=ps[bv2][:], scalar=1.0 / 512, in1=lrs[:],
                                                                      op0=ALU.mult, op1=ALU.subtract),
                     reads=[t_ps[bv2], t_lrs], writes=[t_lrs])
                S.op("act", lambda e: e.activation(out=lrs[:], in_=lrs[:], func=AF.Sqrt, bias=self.eps_c[:], scale=1.0),
                     reads=[t_lrs, self.t_const], writes=[t_lrs])
                S.op("dve", lambda e: e.reciprocal(out=lrs[:], in_=lrs[:]), reads=[t_lrs], writes=[t_lrs])
                for c in range(4):
                    S.op("dve", lambda e, c=c: e.tensor_tensor(out=acv[c][:], in0=acv[c][:], in1=mean[:], op=ALU.subtract),
                         reads=[t_acv[c], t_mean], writes=[t_acv[c]])
                    S.op("dve", lambda e, c=c: e.tensor_tensor(out=acv[c][:], in0=acv[c][:], in1=lrs[:], op=ALU.mult),
                         reads=[t_acv[c], t_lrs], writes=[t_acv[c]])
                    S.op("act", lambda e, c=c: e.activation(out=mix[:, c, :], in_=acv[c][:], func=AF.Silu,
                                                            bias=self.pcol(f"ev_lnb{j}", c), scale=self.pcol(f"ev_lng{j}", c)),
                         reads=[t_acv[c], self.t_const], writes=[t_mix[c]])
                for c in range(4):
                    bc_, bh_, bb_ = nb(), nb(), nb()
                    proj(1536 + c * 128, bc_)
                    proj(2048 + c * 128, bh_)
                    proj(1024 + c * 128, bb_)
                    sg, tsg = sig[c % 2], t_sig[c % 2]
                    S.op("act", lambda e, bh_=bh_, sg=sg: e.activation(out=sg[:], in_=ps[bh_][:], func=AF.Copy),
                         reads=[t_ps[bh_]], writes=[tsg])
                    S.op("dve", lambda e, bc_=bc_, sg=sg, c=c: e.tensor_tensor(out=bbuf[c][:, 2:2 + TT], in0=ps[bc_][:], in1=sg[:], op=ALU.mult),
                         reads=[t_ps[bc_], tsg], writes=[t_bbuf[c]])
                    wc = self.off[f"ev_bconv{j}"] + c * 3
                    ba, tba = bacc[c % 2], t_bacc[c % 2]
                    S.op("act", lambda e, c=c, wc=wc, ba=ba: e.activation(out=ba[:], in_=bbuf[c][:, 2:2 + TT], func=AF.Identity,
                                                                          scale=self.prm[:, wc + 2:wc + 3]),
                         reads=[t_bbuf[c], self.t_const], writes=[tba])
                    for k in range(2):
                        S.op("dve", lambda e, c=c, wc=wc, k=k, ba=ba: e.scalar_tensor_tensor(
                            out=ba[:], in0=bbuf[c][:, k:k + TT], scalar=self.prm[:, wc + k:wc + k + 1], in1=ba[:],
                            op0=ALU.mult, op1=ALU.add), reads=[t_bbuf[c], tba, self.t_const], writes=[tba])
                    S.op("dve", lambda e, c=c, bb_=bb_, ba=ba: e.tensor_tensor(out=mix[:, 4 + c, :], in0=ps[bb_][:], in1=ba[:], op=ALU.mult),
                         reads=[t_ps[bb_], tba], writes=[t_mix[4 + c]])
                    S.op("pool", lambda e, c=c: e.tensor_copy(out=bbuf[c][:, 0:2], in_=bbuf[c][:, TT:TT + 2]),
                         reads=[t_bbuf[c]], writes=[t_bbuf[c]])
                if i + 1 < NT:
                    prologue(i + 1)
                for oc in range(8):
                    b = nb()
                    self.mm(t_ps[b], ps[b][:], [(w_out[:, k, oc * 128:(oc + 1) * 128], mix[:, k, :]) for k in range(8)],
                            reads=[t_wout] + t_mix)
                    S.op("dve", lambda e, oc=oc, b=b, x=x: e.tensor_tensor(out=x[:, oc, :], in0=ps[b][:], in1=x[:, oc, :], op=ALU.add),
                         reads=[t_ps[b], tx], writes=[tx])
                S.dma("sp", self.xview(self.XT, t0), x[:], tx, reads=[tx], writes=[self.t_XT])
            S.barrier(release=win_tl + [t_wout] + t_xt)

    def phase_ffn(self, layer, hf, final=False, fuse_j=None):
        nc, S = self.nc, self.S
        HC = 11
        with ExitStack() as ph:
            wu = self.sb(ph, "wu", [128, 8, 2 * HC * 128], BF16)
            upv = self.w_up[layer].rearrange("(kc p) n -> p kc n", p=128)
            GP = ((0, 2), (2, 5), (5, 8), (8, 11))
            t_wug = [T(f"wu_g{gi}") for gi in range(len(GP))]
            for gi, (p0, p1) in enumerate(GP):
                for k in range(8):
                    S.dma("pool", wu[:, k, p0 * 128:p1 * 128], upv[:, k, (hf * HC + p0) * 128:(hf * HC + p1) * 128], t_wug[gi],
                          writes=[t_wug[gi]], group=True)
                    S.dma("pool", wu[:, k, (HC + p0) * 128:(HC + p1) * 128], upv[:, k, DFF + (hf * HC + p0) * 128:DFF + (hf * HC + p1) * 128],
                          t_wug[gi], writes=[t_wug[gi]], group=True)

            def t_wu_for(pc_):
                for gi, (p0, p1) in enumerate(GP):
                    if p0 <= pc_ < p1:
                        return t_wug[gi]
            wd, t_wd = self.load_w(ph, "wd", self.w_down[layer][hf * HC * 128:(hf + 1) * HC * 128, :], HC * 128, D)
            xt = [self.sb(ph, f"xt{i}", [128, 8, TT], F32) for i in range(2)]
            t_xt = [T(f"xt{i}") for i in range(2)]
            hTb = [self.sb(ph, f"hT{i}", [128, 8, TT], BF16) for i in range(2)]
            t_hTb = [[T(f"hT{i}_{c}") for c in range(8)] for i in range(2)]
            t_hTall = [T(f"hTall{i}") for i in range(2)]
            sq8 = self.sb(ph, "sq8", [128, 8, TT], F32)
            t_sq8 = [T(f"sq8_{c}") for c in range(8)]
            rstd = self.sb(ph, "rstd", [128, TT], F32)
            t_rstd = T("rstd")
            NU = 4
            ug = [self.sb(ph, f"ug{i}", [128, TT], F32) for i in range(NU)]
            t_ug = [T(f"ug{i}") for i in range(NU)]
            uu = [self.sb(ph, f"uu{i}", [128, TT], F32) for i in range(NU)]
            t_uu = [T(f"uu{i}") for i in range(NU)]
            Bg = [self.sb(ph, f"Bg{i}", [128, TT], F32) for i in range(3)]
            t_Bg = [T(f"Bg{i}") for i in range(3)]
            hal = self.sb(ph, "hal", [128, 2, 2 * HC, 2], F32)
            t_hal = [[T(f"hal{p}_{c}") for c in range(2 * HC)] for p in range(2)]
            g = self.sb(ph, "g", [128, HC, TT], BF16)
            t_g = [T(f"g{c}") for c in range(HC)]
            if fuse_j is not None:
                w_out, t_wout = self.load_w(ph, "od_wout", self.od_w_out[fuse_j], D, D)
                mx = self.sb(ph, "mx", [128, 8, TT], BF16)
                t_mx = T("mx")
            if final:
                yo = self.sb(ph, "yo", [128, 8, TT], F32)
                t_yo = [T(f"yo{c}") for c in range(8)]
                t_yoall = T("yoall")
                rstd_f = self.sb(ph, "rstd_f", [128, TT], F32)
                t_rstd_f = T("rstd_f")
            ps, t_ps = self.ps, self.t_ps
            S.op("pool", lambda e: e.memset(hal[:], 0.0), writes=t_hal[0] + t_hal[1])
            gname = f"ffn_norm{layer}"
            cw = self.off[f"ffn_conv{layer}"]
            cb = self.off[f"ffn_convb{layer}"]
            bk = 0

            def nb():
                nonlocal bk
                b = 1 + (bk % 6)
                bk += 1
                return b

            def pro_load(i):
                t0 = i * TT
                x, tx = xt[i % 2], t_xt[i % 2]
                S.dma("sp", x[:], self.xview(self.XT, t0), tx, reads=[self.t_XT], writes=[tx])
                if hf == 1:
                    S.dma("sp", hTb[i % 2][:], self.xview(self.HT, t0), t_hTall[i % 2], reads=[self.t_HT], writes=[t_hTall[i % 2]])
                if fuse_j is not None:
                    S.dma("sp", mx[:], self.xview(self.MIXT, t0), t_mx, reads=[self.t_MIXT], writes=[t_mx])

            def pro_out(i):
                if fuse_j is None:
                    return
                x, tx = xt[i % 2], t_xt[i % 2]
                for oc in range(8):
                    b = nb()
                    self.mm(t_ps[b], ps[b][:], [(w_out[:, k, oc * 128:(oc + 1) * 128], mx[:, k, :]) for k in range(8)], reads=[t_wout, t_mx])
                    S.op("dve", lambda e, oc=oc, b=b, x=x: e.tensor_tensor(out=x[:, oc, :], in0=ps[b][:], in1=x[:, oc, :], op=ALU.add),
                         reads=[t_ps[b], tx], writes=[tx])

            def pro_sq(i):
                if hf == 1:
                    return
                x, tx = xt[i % 2], t_xt[i % 2]
                for c in range(8):
                    if i == 0:
                        S.op("act", lambda e, c=c, x=x: e.activation(out=sq8[:, c, :], in_=x[:, c, :], func=AF.Square),
                             reads=[tx], writes=[t_sq8[c]])
                    else:
                        S.op("pool", lambda e, c=c, x=x: e.tensor_tensor(out=sq8[:, c, :], in0=x[:, c, :], in1=x[:, c, :], op=ALU.mult),
                             reads=[tx], writes=[t_sq8[c]])

            def pro_stat(i):
                if hf == 1:
                    return
                for c in range(8):
                    S.op("pe", lambda e, c=c: e.matmul(ps[0][:], lhsT=self.ones_f[:], rhs=sq8[:, c, :], start=(c == 0), stop=(c == 7)),
                         reads=[t_sq8[c], self.t_const], writes=[t_ps[0]], inc=(c == 7))
                S.op("act", lambda e: e.activation(out=rstd[:], in_=ps[0][:], func=AF.Sqrt, bias=self.eps_c[:], scale=1.0 / D),
                     reads=[t_ps[0], self.t_const], writes=[t_rstd])

            def pro_norm(i):
                if hf == 1:
                    return
                t0 = i * TT
                x, tx = xt[i % 2], t_xt[i % 2]
                h_, th_ = hTb[i % 2], t_hTb[i % 2]
                S.op("dve", lambda e: e.reciprocal(out=rstd[:], in_=rstd[:]), reads=[t_rstd], writes=[t_rstd])
                for c in range(8):
                    S.op("dve", lambda e, c=c, x=x, h_=h_: e.scalar_tensor_tensor(out=h_[:, c, :], in0=x[:, c, :], scalar=self.pcol(gname, c),
                                                                                  in1=rstd[:], op0=ALU.mult, op1=ALU.mult),
                         reads=[tx, t_rstd, self.t_const], writes=[th_[c]])
                S.dma("sp", self.xview(self.HT, t0), h_[:], t_hTall[i % 2], reads=th_, writes=[self.t_HT])

            def evac_back(i, pc_):
                u_g, tg_ = ug[pc_ % NU], t_ug[pc_ % NU]
                u_u, tu_ = uu[pc_ % NU], t_uu[pc_ % NU]
                S.op("act", lambda e, u_g=u_g: e.activation(out=u_g[:], in_=u_g[:], func=AF.Silu), reads=[tg_], writes=[tg_])
                S.op("pool", lambda e, u_g=u_g, u_u=u_u, pc_=pc_: e.tensor_tensor(out=g[:, pc_, :], in0=u_g[:], in1=u_u[:], op=ALU.mult),
                     reads=[tg_, tu_], writes=[t_g[pc_]])

            def up_pair(i, pc_):
                hT = hTb[i % 2]
                hreads = t_hTb[i % 2] if hf == 0 else [t_hTall[i % 2]]
                hp_, hn_ = i % 2, (i + 1) % 2
                banks = []
                for role in range(2):
                    b = nb()
                    ch = role * HC + pc_
                    self.mm(t_ps[b], ps[b][:], [(wu[:, k, ch * 128:(ch + 1) * 128], hT[:, k, :]) for k in range(8)],
                            reads=[t_wu_for(pc_)] + hreads)
                    banks.append(b)
                for role in range(2):
                    b = banks[role]
                    ch = role * HC + pc_
                    gch = role * 22 + hf * HC + pc_
                    u = (ug if role == 0 else uu)[pc_ % NU]
                    tu = (t_ug if role == 0 else t_uu)[pc_ % NU]
                    w0 = self.prm[:, cw + gch * 3:cw + gch * 3 + 1]
                    w1 = self.prm[:, cw + gch * 3 + 1:cw + gch * 3 + 2]
                    w2 = self.prm[:, cw + gch * 3 + 2:cw + gch * 3 + 3]
                    bias = self.prm[:, cb + gch:cb + gch + 1]
                    S.op("act", lambda e, b=b, ch=ch: e.activation(out=hal[:, hn_, ch, :], in_=ps[b][:, TT - 2:TT], func=AF.Copy),
                         reads=[t_ps[b]], writes=[t_hal[hn_][ch]])
                    S.op("act", lambda e, u=u, b=b, w2=w2, bias=bias: e.activation(out=u[:], in_=ps[b][:], func=AF.Identity, bias=bias, scale=w2),
                         reads=[t_ps[b], self.t_const], writes=[tu])
                    if role == 0:
                        bg_, tbg_ = Bg[pc_ % 3], t_Bg[pc_ % 3]
                        S.op("act", lambda e, bg_=bg_, b=b, w1=w1: e.activation(out=bg_[:], in_=ps[b][:], func=AF.Identity, scale=w1),
                             reads=[t_ps[b], self.t_const], writes=[tbg_])
                    else:
                        S.op("dve", lambda e, u=u, b=b, w1=w1: e.scalar_tensor_tensor(out=u[:, 1:TT], in0=ps[b][:, 0:TT - 1], scalar=w1, in1=u[:, 1:TT],
                                                                                     op0=ALU.mult, op1=ALU.add),
                             reads=[t_ps[b], tu, self.t_const], writes=[tu])
                    S.op("dve", lambda e, u=u, b=b, w0=w0: e.scalar_tensor_tensor(out=u[:, 2:TT], in0=ps[b][:, 0:TT - 2], scalar=w0, in1=u[:, 2:TT],
                                                                                 op0=ALU.mult, op1=ALU.add),
                         reads=[t_ps[b], tu, self.t_const], writes=[tu])
                    S.op("dve", lambda e, u=u, ch=ch, w0=w0: e.scalar_tensor_tensor(out=u[:, 0:2], in0=hal[:, hp_, ch, 0:2], scalar=w0, in1=u[:, 0:2],
                                                                                   op0=ALU.mult, op1=ALU.add),
                         reads=[t_hal[hp_][ch], tu, self.t_const], writes=[tu])
                    S.op("dve", lambda e, u=u, ch=ch, w1=w1: e.scalar_tensor_tensor(out=u[:, 0:1], in0=hal[:, hp_, ch, 1:2], scalar=w1, in1=u[:, 0:1],
                                                                                   op0=ALU.mult, op1=ALU.add),
                         reads=[t_hal[hp_][ch], tu, self.t_const], writes=[tu])
                    if role == 0:
                        S.op("pool", lambda e, u=u, bg_=bg_: e.tensor_tensor(out=u[:, 1:TT], in0=u[:, 1:TT], in1=bg_[:, 0:TT - 1], op=ALU.add),
                             reads=[tu, tbg_], writes=[tu])
                if pc_ >= 2:
                    evac_back(i, pc_ - 2)
                if i + 1 < NT:
                    if pc_ == 2:
                        pro_load(i + 1)
                    elif pc_ == 3:
                        pro_out(i + 1)
                    elif pc_ == 5:
                        pro_sq(i + 1)

            LEAD = 0 if hf == 0 else 2
            pro_load(0)
            pro_out(0)
            pro_sq(0)
            pro_stat(0)
            pro_norm(0)
            for i in range(NT):
                t0 = i * TT
                x, tx = xt[i % 2], t_xt[i % 2]
                for pc_ in range(LEAD if i > 0 else 0, HC):
                    up_pair(i, pc_)
                if i + 1 < NT:
                    pro_stat(i + 1)
                evac_back(i, HC - 2)
                evac_back(i, HC - 1)
                if i + 1 < NT:
                    pro_norm(i + 1)
                    for pc_ in range(LEAD):
                        up_pair(i + 1, pc_)
                for oc in range(8):
                    b = nb()
                    self.mm(t_ps[b], ps[b][:], [(wd[:, k, oc * 128:(oc + 1) * 128], g[:, k, :]) for k in range(HC)],
                            reads=[t_wd] + t_g)
                    S.op("dve", lambda e, oc=oc, b=b, x=x: e.tensor_tensor(out=x[:, oc, :], in0=ps[b][:], in1=x[:, oc, :], op=ALU.add),
                         reads=[t_ps[b], tx], writes=[tx])
                if not final:
                    S.dma("sp", self.xview(self.XT, t0), x[:], tx, reads=[tx], writes=[self.t_XT])
                else:
                    for c in range(8):
                        S.op("pool", lambda e, c=c, x=x: e.tensor_tensor(out=sq8[:, c, :], in0=x[:, c, :], in1=x[:, c, :], op=ALU.mult),
                             reads=[tx], writes=[t_sq8[c]])
                    for c in range(8):
                        S.op("pe", lambda e, c=c: e.matmul(ps[0][:], lhsT=self.ones_f[:], rhs=sq8[:, c, :], start=(c == 0), stop=(c == 7)),
                             reads=[t_sq8[c], self.t_const], writes=[t_ps[0]], inc=(c == 7))
                    S.op("act", lambda e: e.activation(out=rstd_f[:], in_=ps[0][:], func=AF.Sqrt, bias=self.eps_c[:], scale=1.0 / D),
                         reads=[t_ps[0], self.t_const], writes=[t_rstd_f])
                    S.op("dve", lambda e: e.reciprocal(out=rstd_f[:], in_=rstd_f[:]), reads=[t_rstd_f], writes=[t_rstd_f])
                    for c in range(8):
                        S.op("dve", lambda e, c=c, x=x: e.scalar_tensor_tensor(out=yo[:, c, :], in0=x[:, c, :], scalar=self.pcol("final_norm", c),
                                                                              in1=rstd_f[:], op0=ALU.mult, op1=ALU.mult),
                             reads=[tx, t_rstd_f, self.t_const], writes=[t_yo[c]])
                    S.dma("sp", self.xview(self.yout, t0), yo[:], t_yoall, reads=t_yo, writes=[self.t_Y])
            S.barrier(release=t_wug + [t_wd] + t_hTall + t_xt + ([t_yoall] if final else []) + ([t_wout, t_mx] if fuse_j is not None else []))

    def phase_final(self, src):
        nc, S = self.nc, self.S
        with ExitStack() as ph:
            xt = [self.sb(ph, f"xt{i}", [128, 8, TT], F32) for i in range(2)]
            t_xt = [T(f"xt{i}") for i in range(2)]
            yo = [self.sb(ph, f"yo{i}", [128, 8, TT], F32) for i in range(2)]
            t_yo = [[T(f"yo{i}_{c}") for c in range(8)] for i in range(2)]
            t_yoall = [T(f"yoall{i}") for i in range(2)]
            sq = [self.sb(ph, f"sq{i}", [128, TT], F32) for i in range(2)]
            t_sq = [T(f"sq{i}") for i in range(2)]
            rstd = self.sb(ph, "rstd", [128, TT], F32)
            t_rstd = T("rstd")
            for i in range(NT):
                t0 = i * TT
                x, tx = xt[i % 2], t_xt[i % 2]
                S.dma("sp", x[:], self.xview(src, t0), tx, reads=[self.t_XT], writes=[tx])
                self.rms_hT(x, tx, yo[i % 2], t_yo[i % 2], "final_norm", sq, t_sq, rstd, t_rstd, self.ps[0], self.t_ps[0])
                S.dma("sp", self.xview(self.yout, t0), yo[i % 2][:], t_yoall[i % 2], reads=t_yo[i % 2], writes=[self.t_Y])
            S.barrier(release=t_xt + t_yoall)

    def phase_odd(self, j):
        self.odd_proj(j)
        self.odd_attn_all()
        self.odd_ret_out(j)

    def odd_proj(self, j):
        nc, S = self.nc, self.S
        with ExitStack() as ph:
            w_in = self.sb(ph, "od_win", [128, 8, 3072], BF16)
            w_sw = self.sb(ph, "od_wsw", [128, 8, 1536], BF16)
            _, win_a, tl_a = self.load_wg(ph, "od_win_a", self.od_w_in[j], D, 3072, [(0, 512)], w=w_in)
            _, wsw_a, tl_b = self.load_wg(ph, "od_wsw_a", self.od_w_sw[j], D, 1536, [(0, 512)], w=w_sw)
            _, win_b, tl_c = self.load_wg(ph, "od_win_b", self.od_w_in[j], D, 3072, [(512, 1024)], w=w_in)
            _, wsw_b, tl_d = self.load_wg(ph, "od_wsw_b", self.od_w_sw[j], D, 1536, [(512, 1024)], w=w_sw)
            _, win_c, tl_e = self.load_wg(ph, "od_win_c", self.od_w_in[j], D, 3072, [(1536, 2048)], w=w_in)
            _, wsw_c, tl_f = self.load_wg(ph, "od_wsw_c", self.od_w_sw[j], D, 1536, [(1024, 1536)], w=w_sw)
            _, win_d, tl_g = self.load_wg(ph, "od_win_d", self.od_w_in[j], D, 3072, [(1024, 1536), (2048, 2560), (2560, 3072)], w=w_in)
            wtl = tl_a + tl_b + tl_c + tl_d + tl_e + tl_f + tl_g

            def t_win(col):
                for f in (win_a, win_b, win_c, win_d):
                    try:
                        return f(col)
                    except KeyError:
                        pass

            def t_wsw(col):
                for f in (wsw_a, wsw_b, wsw_c):
                    try:
                        return f(col)
                    except KeyError:
                        pass
            xt = [self.sb(ph, f"xt{i}", [128, 8, TT], F32) for i in range(2)]
            t_xt = [T(f"xt{i}") for i in range(2)]
            hTb = [self.sb(ph, f"hT{i}", [128, 8, TT], BF16) for i in range(2)]
            t_hTb = [[T(f"hT{i}_{c}") for c in range(8)] for i in range(2)]
            sq = [self.sb(ph, f"sq{i}", [128, TT], F32) for i in range(2)]
            t_sq = [T(f"sq{i}") for i in range(2)]
            rstd = self.sb(ph, "rstd", [128, TT], F32)
            t_rstd = T("rstd")
            rp_ = [self.sb(ph, f"rope{i}", [128, 4, TT], F32) for i in range(2)]
            t_rp = [T(f"rope{i}") for i in range(2)]
            t1 = [self.sb(ph, f"t1_{i}", [128, TT], F32) for i in range(2)]
            t_t1 = [T(f"t1_{i}") for i in range(2)]
            t2 = [self.sb(ph, f"t2_{i}", [128, TT], F32) for i in range(2)]
            t_t2 = [T(f"t2_{i}") for i in range(2)]
            ob = [self.sb(ph, f"ob{i}", [128, TT], BF16) for i in range(4)]
            t_ob = [T(f"ob{i}") for i in range(4)]
            vb = [self.sb(ph, f"vb{i}", [128, 4, 512], BF16) for i in range(2)]
            t_vb = [T(f"vb{i}") for i in range(2)]
            sgo = [self.sb(ph, f"sgo{i}", [128, TT], F32) for i in range(2)]
            t_sgo = [T(f"sgo{i}") for i in range(2)]
            ps, t_ps = self.ps, self.t_ps
            gname = f"od_norm{j}"
            bk = 0
            cnt = 0

            def nb():
                nonlocal bk
                b = 1 + (bk % 6)
                bk += 1
                return b

            qk = []
            for c in range(4):
                qk.append((c * 128, c * 128, self.QT, self.t_QT, c * 128, True))
            for c in range(4):
                qk.append((512 + c * 128, 512 + c * 128, self.KT, self.t_KT, c * 128, False))
            for c in range(2):
                qk.append((1536 + c * 128, 1024 + c * 128, self.QT, self.t_QT, 512 + c * 128, False))
            for c in range(2):
                qk.append((1792 + c * 128, 1280 + c * 128, self.KT, self.t_KT, 512 + c * 128, True))

            def prologue(i):
                S.dma("sp", xt[i % 2][:], self.xview(self.XT, i * TT), t_xt[i % 2], reads=[self.t_XT], writes=[t_xt[i % 2]])
                S.dma("sp", rp_[i % 2][:], self.rope_d.rearrange("a p t -> p a t")[:, :, i * TT:(i + 1) * TT], t_rp[i % 2], writes=[t_rp[i % 2]])
                self.rms_hT(xt[i % 2], t_xt[i % 2], hTb[i % 2], t_hTb[i % 2], gname, sq, t_sq, rstd, t_rstd, ps[0], t_ps[0])

            prologue(0)
            for i in range(NT):
                t0 = i * TT
                x, tx = xt[i % 2], t_xt[i % 2]
                rt, trt = rp_[i % 2], t_rp[i % 2]
                hT, t_hT = hTb[i % 2], t_hTb[i % 2]
                for (mc, sc, dst, tdst, row0, scaled) in qk:
                    ba, bb = nb(), nb()
                    self.mm(t_ps[ba], ps[ba][:], [(w_in[:, k, mc:mc + 128], hT[:, k, :]) for k in range(8)], reads=[t_win(mc)] + t_hT)
                    self.mm(t_ps[bb], ps[bb][:], [(w_sw[:, k, sc:sc + 128], hT[:, k, :]) for k in range(8)], reads=[t_wsw(sc)] + t_hT)
                    ci, si = (2, 3) if scaled else (0, 1)
                    a1, ta1 = t1[cnt % 2], t_t1[cnt % 2]
                    a2, ta2 = t2[cnt % 2], t_t2[cnt % 2]
                    o, to = ob[cnt % 4], t_ob[cnt % 4]
                    cnt += 1
                    S.op("dve", lambda e, a1=a1, ba=ba, ci=ci, rt=rt: e.tensor_tensor(out=a1[:], in0=ps[ba][:], in1=rt[:, ci, :], op=ALU.mult),
                         reads=[t_ps[ba], trt], writes=[ta1])
                    S.op("dve", lambda e, a2=a2, bb=bb, si=si, rt=rt: e.tensor_tensor(out=a2[:], in0=ps[bb][:], in1=rt[:, si, :], op=ALU.mult),
                         reads=[t_ps[bb], trt], writes=[ta2])
                    S.op("pool", lambda e, a1=a1, a2=a2, o=o: e.tensor_tensor(out=o[:], in0=a1[:], in1=a2[:], op=ALU.add),
                         reads=[ta1, ta2], writes=[to])
                    S.dma("sp", dst[row0:row0 + 128, t0:t0 + TT], o[:], to, reads=[to], writes=[tdst])
                if i + 1 < NT:
                    prologue(i + 1)
                for vi, (c0, dst, tdst) in enumerate(((1024, self.V, self.t_V), (2048, self.RV, self.t_RV))):
                    v, tv = vb[vi], t_vb[vi]
                    for ts in range(4):
                        b = nb()
                        self.mm(t_ps[b], ps[b][:], [(hT[:, k, ts * 128:(ts + 1) * 128], w_in[:, k, c0:c0 + 512]) for k in range(8)],
                                reads=[t_win(c0)] + t_hT)
                        S.op("act", lambda e, v=v, ts=ts, b=b: e.activation(out=v[:, ts, :], in_=ps[b][:], func=AF.Copy),
                             reads=[t_ps[b]], writes=[tv])
                    S.dma("sp", dst[t0:t0 + TT, :].rearrange("(ts p) f -> p ts f", p=128), v[:], tv, reads=[tv], writes=[tdst])
                for c in range(4):
                    b = nb()
                    self.mm(t_ps[b], ps[b][:], [(w_in[:, k, 2560 + c * 128:2560 + (c + 1) * 128], hT[:, k, :]) for k in range(8)],
                            reads=[t_win(2560)] + t_hT)
                    sg, tsg = sgo[c % 2], t_sgo[c % 2]
                    S.op("act", lambda e, sg=sg, b=b: e.activation(out=sg[:], in_=ps[b][:], func=AF.Silu), reads=[t_ps[b]], writes=[tsg])
                    S.dma("sp", self.SG[c * 128:(c + 1) * 128, t0:t0 + TT], sg[:], tsg, reads=[tsg], writes=[self.t_SG])
            S.barrier(release=wtl + t_xt + t_rp + t_ob + t_vb + t_sgo)

    def odd_attn_all(self):
        nc, S = self.nc, self.S
        DS = (1, 4, 16)
        with ExitStack() as ph:
            qN = [self.sb(ph, f"qN{i}", [128, S_LEN], BF16) for i in range(2)]
            kN = [self.sb(ph, f"kN{i}", [128, S_LEN], BF16) for i in range(2)]
            t_qN = [T(f"qN{i}") for i in range(2)]
            t_kN = [T(f"kN{i}") for i in range(2)]
            qP = [self.sb(ph, f"qP16_{i}", [128, S_LEN], BF16) for i in range(2)]
            kP = [self.sb(ph, f"kP16_{i}", [128, S_LEN], BF16) for i in range(2)]
            t_qP = [[T(f"qP16_{i}_{k}") for k in range(4)] for i in range(2)]
            t_kP = [[T(f"kP16_{i}_{k}") for k in range(4)] for i in range(2)]
            vt = [{d: self.sb(ph, f"vt{i}_{d}", [128, 32, 128], BF16) for d in DS} for i in range(2)]
            t_vt = [{d: T(f"vt{i}_{d}") for d in DS} for i in range(2)]
            acc = self.sb(ph, "accnd", [64, 2, S_LEN], F32)
            t_acc = T("acc")
            pt = [self.sb(ph, f"pt{i}", [128, 256], BF16) for i in range(6)]
            t_pt = [T(f"pt{i}") for i in range(6)]
            cout = self.sb(ph, "cout", [64, S_LEN], BF16)
            t_cout = T("cout")
            rec = [self.sb(ph, f"rec{i}", [64, TT], F32) for i in range(2)]
            t_rec = [T(f"rec{i}") for i in range(2)]
            ps, t_ps = self.ps, self.t_ps

            def load(hp):
                i = hp % 2
                S.dma("sp", qN[i][:], self.QT[hp * 128:(hp + 1) * 128, :], t_qN[i], reads=[self.t_QT], writes=[t_qN[i]])
                S.dma("sp", kN[i][:], self.KT[hp * 128:(hp + 1) * 128, :], t_kN[i], reads=[self.t_KT], writes=[t_kN[i]])
                for d in DS:
                    nbk = 32 // d
                    vsrc = self.V[:, hp * 128:(hp + 1) * 128].rearrange("(n p r) f -> r p n f", p=128, r=d)
                    for r in range(d):
                        S.dma("sp", vt[i][d][:, r * nbk:(r + 1) * nbk, :], vsrc[r], t_vt[i][d], reads=[self.t_V], writes=[t_vt[i][d]], group=True)

            def perm_piece(hp, k):
                i = hp % 2
                src, tsrc, dst, tdst = (qN[i], t_qN[i], qP[i], t_qP[i]) if k < 4 else (kN[i], t_kN[i], kP[i], t_kP[i])
                kk = k % 4
                S.op("act", lambda e, src=src, dst=dst, kk=kk: e.activation(
                    out=dst[:].rearrange("p (r m) -> p r m", r=16)[:, :, kk * 64:(kk + 1) * 64],
                    in_=src[:, kk * 1024:(kk + 1) * 1024].rearrange("p (m r) -> p r m", r=16), func=AF.Copy),
                    reads=[tsrc], writes=[tdst[kk]])

            load(0)
            for k in range(8):
                perm_piece(0, k)
            gblk = 0
            gpair = 0
            for hp in range(4):
                pi = hp % 2
                if hp + 1 < 4:
                    load(hp + 1)
                pblk = 0
                for hh in range(2):
                    hb = hh * 64
                    blocks = [(d, r, n) for d in DS for r in range(d) for n in range(32 // d)]
                    info = {}

                    def QK(bi):
                        nonlocal gblk, gpair
                        d, r, n = blocks[bi]
                        if d == 1:
                            qa = qN[pi][hb:hb + 64, n * 128:(n + 1) * 128]
                            kc = kN[pi][hb:hb + 64, n * 128:(n + 1) * 128]
                            kp = kN[pi][hb:hb + 64, (n - 1) * 128:n * 128] if n > 0 else None
                            rd = [t_qN[pi], t_kN[pi], self.t_cb]
                        elif d == 4:
                            qv = qN[pi][:].rearrange("p (m r) -> p r m", r=4)
                            kv = kN[pi][:].rearrange("p (m r) -> p r m", r=4)
                            qa = qv[hb:hb + 64, r, n * 128:(n + 1) * 128]
                            kc = kv[hb:hb + 64, r, n * 128:(n + 1) * 128]
                            kp = kv[hb:hb + 64, r, (n - 1) * 128:n * 128] if n > 0 else None
                            rd = [t_qN[pi], t_kN[pi], self.t_cb]
                        else:
                            J0 = r * 256 + n * 128
                            qa = qP[pi][hb:hb + 64, J0:J0 + 128]
                            kc = kP[pi][hb:hb + 64, J0:J0 + 128]
                            kp = kP[pi][hb:hb + 64, J0 - 128:J0] if n > 0 else None
                            rd = t_qP[pi] + t_kP[pi] + [self.t_cb]
                        bs = gblk % 4
                        pbuf = gblk % 6
                        gblk += 1
                        if n % 2 == 0:
                            gpair += 1
                        bo = 4 + gpair % 3
                        info[bi] = (bs, pbuf, bo)
                        lo = 0 if n > 0 else 128
                        pairs = []
                        if n > 0:
                            pairs.append((kp, qa, ps[bs][:, 0:128]))
                        pairs.append((kc, qa, ps[bs][:, 128:256]))
                        for ii, (l, rr, oo) in enumerate(pairs):
                            S.op("pe", lambda e, l=l, rr=rr, oo=oo: e.matmul(oo, lhsT=l, rhs=rr, start=True, stop=True),
                                 reads=rd, writes=[t_ps[bs]], inc=(ii == len(pairs) - 1))

                    def REST(bi):
                        d, r, n = blocks[bi]
                        nbk = 32 // d
                        bs, pbuf, bo = info[bi]
                        p_, tp_ = pt[pbuf], t_pt[pbuf]
                        v_ = vt[pi][d]
                        lo = 0 if n > 0 else 128
                        S.op("act", lambda e, p_=p_, bs=bs, lo=lo: e.activation(out=p_[:, lo:256], in_=ps[bs][:, lo:256], func=AF.Exp),
                             reads=[t_ps[bs]], writes=[tp_])
                        S.op("dve", lambda e, p_=p_, lo=lo: e.tensor_tensor(out=p_[:, lo:256], in0=p_[:, lo:256], in1=self.mask01[:, lo:256], op=ALU.mult),
                             reads=[tp_, self.t_cb], writes=[tp_])
                        vb_ = r * nbk + n
                        c0 = (n % 2) * 128
                        onum = ps[bo][0:64, c0:c0 + 128]
                        oden = ps[bo][0:64, 256 + c0:256 + c0 + 128]
                        pv = []
                        if n > 0:
                            pv.append((v_[:, vb_ - 1, hb:hb + 64], p_[:, 0:128], onum, True, False))
                        pv.append((v_[:, vb_, hb:hb + 64], p_[:, 128:256], onum, n == 0, True))
                        if n > 0:
                            pv.append((self.ones_b[:, 0:64], p_[:, 0:128], oden, True, False))
                        pv.append((self.ones_b[:, 0:64], p_[:, 128:256], oden, n == 0, True))
                        for ii, (l, rr, oo, st, sp_) in enumerate(pv):
                            S.op("pe", lambda e, l=l, rr=rr, oo=oo, st=st, sp_=sp_: e.matmul(
                                oo, lhsT=l, rhs=rr, start=st, stop=sp_, skip_group_check=True),
                                reads=[t_vt[pi][d], tp_, self.t_cb], writes=[t_ps[bo]], inc=(ii == len(pv) - 1))
                        if n % 2 == 1:
                            av = acc[:].rearrange("p a (m r) -> p a r m", r=d)[:, :, r, (n - 1) * 128:(n + 1) * 128]

                            def do_acc(av=av, bo=bo, d=d):
                                if d == 1:
                                    S.op("dve", lambda e: e.tensor_copy(
                                        out=av, in_=ps[bo][0:64, 0:512].rearrange("p (a m) -> p a m", a=2)),
                                        reads=[t_ps[bo]], writes=[t_acc])
                                else:
                                    S.op("dve", lambda e: e.tensor_tensor(
                                        out=av, in0=av, in1=ps[bo][0:64, 0:512].rearrange("p (a m) -> p a m", a=2), op=ALU.add),
                                        reads=[t_ps[bo], t_acc], writes=[t_acc])
                            while pend_acc:
                                pend_acc.pop(0)()
                            pend_acc.append(do_acc)

                    LA = 3
                    pend_acc = []
                    for bi in range(min(LA, len(blocks))):
                        QK(bi)
                    for bi in range(len(blocks)):
                        if bi + LA < len(blocks):
                            QK(bi + LA)
                        REST(bi)
                        pblk += 1
                        if hp + 1 < 4 and pblk % 20 == 0 and pblk // 20 <= 8:
                            perm_piece(hp + 1, pblk // 20 - 1)
                    while pend_acc:
                        pend_acc.pop(0)()
                    for i in range(NT):
                        t0 = i * TT
                        rc, trc = rec[i % 2], t_rec[i % 2]
                        S.op("act", lambda e, rc=rc, t0=t0: e.activation(out=rc[:], in_=acc[:, 1, t0:t0 + TT], func=AF.Ln), reads=[t_acc], writes=[trc])
                        S.op("act", lambda e, rc=rc: e.activation(out=rc[:], in_=rc[:], func=AF.Exp, scale=-1.0), reads=[trc], writes=[trc])
                        S.op("dve", lambda e, rc=rc, t0=t0: e.tensor_tensor(out=cout[:, t0:t0 + TT], in0=acc[:, 0, t0:t0 + TT], in1=rc[:], op=ALU.mult),
                             reads=[t_acc, trc], writes=[t_cout])
                    hg = hp * 2 + hh
                    S.dma("sp", self.MIXT[hg * 64:(hg + 1) * 64, :], cout[:], t_cout, reads=[t_cout], writes=[self.t_MIXT])
            S.barrier(release=t_qN + t_kN + [t_cout] + [t_vt[i][d] for i in range(2) for d in DS])

    def odd_ret_out(self, j):
        nc, S = self.nc, self.S
        NCH = 32
        with ExitStack() as ph:
            rq = self.sb(ph, "rq", [128, S_LEN], BF16)
            rk = self.sb(ph, "rk", [128, S_LEN], BF16)
            rqd = self.sb(ph, "rqd", [128, S_LEN], BF16)
            t_rq, t_rk = T("rq"), T("rk")
            t_rqd = [T(f"rqd{n}") for n in range(NCH)]
            rvt = self.sb(ph, "rvt", [128, NCH, 256], BF16)
            t_rvt = T("rvt")
            kdT = self.sb(ph, "kdT", [128, NCH, 128], BF16)
            t_kdT = [T(f"kdT{n}") for n in range(NCH)]
            yT = [[self.sb(ph, f"yT{r}_{h}", [128, S_LEN], F32) for h in range(2)] for r in range(2)]
            t_yT = [[[T(f"yT{r}_{h}_{i}") for i in range(NT)] for h in range(2)] for r in range(2)]
            Sf = self.sb(ph, "Sf", [128, 128], F32)
            Sb = self.sb(ph, "Sb", [128, 128], BF16)
            t_Sf = [T("Sf0"), T("Sf1")]
            t_Sb = [T("Sb0"), T("Sb1")]
            sm = [self.sb(ph, f"sm{i}", [128, 128], BF16) for i in range(3)]
            t_sm = [T(f"sm{i}") for i in range(3)]
            sq = [self.sb(ph, f"sq{i}", [128, TT], F32) for i in range(2)]
            t_sq = [T(f"sq{i}") for i in range(2)]
            mean2 = [self.sb(ph, f"mean{i}", [128, TT], F32) for i in range(2)]
            t_mean2 = [T(f"mean{i}") for i in range(2)]
            lrs2 = [self.sb(ph, f"lrs{i}", [128, TT], F32) for i in range(2)]
            t_lrs2 = [T(f"lrs{i}") for i in range(2)]
            sgt = [self.sb(ph, f"sgt{i}", [128, TT], F32) for i in range(2)]
            t_sgt = [T(f"sgt{i}") for i in range(2)]
            ro = [self.sb(ph, f"ro{i}", [128, TT], BF16) for i in range(2)]
            t_ro = [T(f"ro{i}") for i in range(2)]
            ps, t_ps = self.ps, self.t_ps
            st = {"bk": 0, "cnt": 0}

            def load_pre(rp):
                S.dma("sp", rq[:], self.QT[512 + rp * 128:512 + (rp + 1) * 128, :], t_rq, reads=[self.t_QT], writes=[t_rq])
                S.dma("sp", rk[:], self.KT[512 + rp * 128:512 + (rp + 1) * 128, :], t_rk, reads=[self.t_KT], writes=[t_rk])
                S.dma("sp", rvt[:], self.RV[:, rp * 256:(rp + 1) * 256].rearrange("(n p) f -> p n f", p=128), t_rvt,
                      reads=[self.t_RV], writes=[t_rvt])
                S.op("pool", lambda e: e.memset(Sf[:], 0.0), writes=t_Sf)
                S.op("pool", lambda e: e.memset(Sb[:], 0.0), writes=t_Sb)
                QD = self.ctab[:, 512 + rp * 128:512 + (rp + 1) * 128]
                KD = self.ctab[:, 768 + rp * 128:768 + (rp + 1) * 128]
                for n in range(NCH):
                    cs = slice(n * 128, (n + 1) * 128)
                    S.op("pool", lambda e, cs=cs, QD=QD: e.tensor_tensor(out=rqd[:, cs], in0=rq[:, cs], in1=QD, op=ALU.mult),
                         reads=[t_rq, self.t_cb], writes=[t_rqd[n]])
                    S.op("pe", lambda e, cs=cs, n=n: e.transpose(out=self.psb[:, (n % 8) * 128:(n % 8 + 1) * 128], in_=rk[:, cs], identity=self.ident[:]),
                         reads=[t_rk, self.t_cb], writes=[self.t_psb])
                    S.op("dve", lambda e, n=n, KD=KD: e.tensor_tensor(out=kdT[:, n, :], in0=self.psb[:, (n % 8) * 128:(n % 8 + 1) * 128], in1=KD, op=ALU.mult),
                         reads=[self.t_psb, self.t_cb], writes=[t_kdT[n]])

            def chunk(rp, n):
                cd = [float(np.exp(128.0 * np.log1p(-(2.0 ** (-5.0 - (2 * rp + hh)))))) for hh in range(2)]
                cs = slice(n * 128, (n + 1) * 128)
                for hh in range(2):
                    hb = hh * 64
                    h = 2 * rp + hh
                    bk = st["bk"]
                    st["bk"] += 1
                    bs, by = bk % 2, 2 + bk % 2
                    s_, ts_ = sm[bk % 3], t_sm[bk % 3]
                    self.mm(t_ps[bs], ps[bs][:, 0:128], [(rk[hb:hb + 64, cs], rq[hb:hb + 64, cs])], reads=[t_rk, t_rq])
                    S.op("dve", lambda e, s_=s_, bs=bs, h=h: e.tensor_tensor(out=s_[:], in0=ps[bs][:, 0:128], in1=self.ctab[:, h * 128:(h + 1) * 128], op=ALU.mult),
                         reads=[t_ps[bs], self.t_cb], writes=[ts_])
                    pairs = [(rvt[:, n, hh * 128:(hh + 1) * 128], s_[:])]
                    if n > 0:
                        pairs.append((Sb[hb:hb + 64, :], rqd[hb:hb + 64, cs]))
                    self.mm(t_ps[by], ps[by][:, 0:128], pairs, reads=[t_rvt, ts_, t_Sb[hh], t_rqd[n]])
                    S.op("act", lambda e, hh=hh, cs=cs, by=by: e.activation(out=yT[rp][hh][:, cs], in_=ps[by][:, 0:128], func=AF.Copy),
                         reads=[t_ps[by]], writes=[t_yT[rp][hh][n // 4]])
                if n < NCH - 1:
                    self.mm(t_ps[6], ps[6][:, 0:256], [(kdT[:, n, :], rvt[:, n, :])], reads=[t_kdT[n], t_rvt])
                    for hh in range(2):
                        hb = hh * 64
                        S.op("dve", lambda e, hb=hb, hh=hh: e.scalar_tensor_tensor(
                            out=Sf[hb:hb + 64, :], in0=Sf[hb:hb + 64, :], scalar=cd[hh], in1=ps[6][hb:hb + 64, hh * 128:(hh + 1) * 128],
                            op0=ALU.mult, op1=ALU.add), reads=[t_ps[6], t_Sf[hh]], writes=[t_Sf[hh]])
                        S.op("act", lambda e, hb=hb: e.activation(out=Sb[hb:hb + 64, :], in_=Sf[hb:hb + 64, :], func=AF.Copy),
                             reads=[t_Sf[hh]], writes=[t_Sb[hh]])

            def hn_tile(rp, hh, i):
                h = 2 * rp + hh
                t0 = i * TT
                y_ = yT[rp][hh][:, t0:t0 + TT]
                ty = t_yT[rp][hh][i]
                cnt = st["cnt"]
                st["cnt"] += 1
                sgx, tsgx = sgt[cnt % 2], t_sgt[cnt % 2]
                r_, tr_ = ro[cnt % 2], t_ro[cnt % 2]
                sq_, tsq_ = sq[cnt % 2], t_sq[cnt % 2]
                bm, bv = 4, 5
                mean, t_mean = mean2[cnt % 2], t_mean2[cnt % 2]
                lrs, t_lrs = lrs2[cnt % 2], t_lrs2[cnt % 2]
                S.dma("sp", sgx[:], self.SG[h * 128:(h + 1) * 128, t0:t0 + TT], tsgx, reads=[self.t_SG], writes=[tsgx])
                S.op("pe", lambda e: e.matmul(ps[bm][:], lhsT=self.ones_f[:], rhs=y_, start=True, stop=True),
                     reads=[ty, self.t_const], writes=[t_ps[bm]])
                S.op("act", lambda e: e.activation(out=sq_[:], in_=y_, func=AF.Square), reads=[ty], writes=[tsq_])
                S.op("pe", lambda e: e.matmul(ps[bv][:], lhsT=self.ones_f[:], rhs=sq_[:], start=True, stop=True),
                     reads=[tsq_, self.t_const], writes=[t_ps[bv]])
                S.op("act", lambda e: e.activation(out=mean[:], in_=ps[bm][:], func=AF.Identity, scale=1.0 / 128),
                     reads=[t_ps[bm]], writes=[t_mean])
                S.op("act", lambda e: e.activation(out=lrs[:], in_=mean[:], func=AF.Square), reads=[t_mean], writes=[t_lrs])
                S.op("dve", lambda e: e.scalar_tensor_tensor(out=lrs[:], in0=ps[bv][:], scalar=1.0 / 128, in1=lrs[:],
                                                             op0=ALU.mult, op1=ALU.subtract),
                     reads=[t_ps[bv], t_lrs], writes=[t_lrs])
                S.op("act", lambda e: e.activation(out=lrs[:], in_=lrs[:], func=AF.Ln, bias=self.eps_c[:], scale=1.0),
                     reads=[t_lrs, self.t_const], writes=[t_lrs])
                S.op("act", lambda e: e.activation(out=lrs[:], in_=lrs[:], func=AF.Exp, scale=-0.5), reads=[t_lrs], writes=[t_lrs])
                S.op("dve", lambda e: e.tensor_tensor(out=y_, in0=y_, in1=mean[:], op=ALU.subtract), reads=[ty, t_mean], writes=[ty])
                S.op("dve", lambda e: e.tensor_tensor(out=y_, in0=y_, in1=lrs[:], op=ALU.mult), reads=[ty, t_lrs], writes=[ty])
                S.op("dve", lambda e: e.tensor_tensor(out=r_[:], in0=y_, in1=sgx[:], op=ALU.mult), reads=[ty, tsgx], writes=[tr_])
                S.dma("sp", self.MIXT[512 + h * 128:512 + (h + 1) * 128, t0:t0 + TT], r_[:], tr_, reads=[tr_], writes=[self.t_MIXT])

            load_pre(0)
            for n in range(NCH):
                chunk(0, n)
            load_pre(1)
            hn_list = [(0, hh, i) for hh in range(2) for i in range(NT)]
            for n in range(NCH):
                chunk(1, n)
                if n % 2 == 1 and hn_list:
                    hn_tile(*hn_list.pop(0))
            while hn_list:
                hn_tile(*hn_list.pop(0))
            for hh in range(2):
                for i in range(NT):
                    hn_tile(1, hh, i)
            S.barrier(release=[t_rq, t_rk, t_rvt] + t_sgt + t_ro)

    def odd_out(self, j):
        nc, S = self.nc, self.S
        with ExitStack() as ph:
            w_out, t_wout = self.load_w(ph, "od_wout", self.od_w_out[j], D, D)
            xt = [self.sb(ph, f"xt{i}", [128, 8, TT], F32) for i in range(2)]
            t_xt = [T(f"xt{i}") for i in range(2)]
            mx = [self.sb(ph, f"mx{i}", [128, 8, TT], BF16) for i in range(2)]
            t_mx = [T(f"mx{i}") for i in range(2)]
            ps, t_ps = self.ps, self.t_ps
            for i in range(NT):
                t0 = i * TT
                x, tx = xt[i % 2], t_xt[i % 2]
                m, tm = mx[i % 2], t_mx[i % 2]
                S.dma("sp", x[:], self.xview(self.XT, t0), tx, reads=[self.t_XT], writes=[tx])
                S.dma("sp", m[:], self.xview(self.MIXT, t0), tm, reads=[self.t_MIXT], writes=[tm])
                for oc in range(8):
                    b = oc % 6
                    self.mm(t_ps[b], ps[b][:], [(w_out[:, k, oc * 128:(oc + 1) * 128], m[:, k, :]) for k in range(8)], reads=[t_wout, tm])
                    S.op("dve", lambda e, oc=oc, b=b, x=x: e.tensor_tensor(out=x[:, oc, :], in0=ps[b][:], in1=x[:, oc, :], op=ALU.add),
                         reads=[t_ps[b], tx], writes=[tx])
                S.dma("sp", self.xview(self.XT, t0), x[:], tx, reads=[tx], writes=[self.t_XT])
            S.barrier(release=[t_wout] + t_xt + t_mx)


def const_tables():
    t = np.arange(S_LEN, dtype=np.float32)
    inv = (np.float32(10000.0) ** (-(np.arange(0, 64, 2, dtype=np.float32)) / np.float32(64))).astype(np.float32)
    ang = (t[:, None] * inv[None, :]).astype(np.float32)
    cos = np.cos(ang).astype(np.float32).T
    sin = np.sin(ang).astype(np.float32).T
    cos64 = np.concatenate([cos, cos], 0)
    sins64 = np.concatenate([-sin, sin], 0)
    cos128 = np.concatenate([cos64, cos64], 0)
    sin128 = np.concatenate([sins64, sins64], 0)
    rope = np.stack([cos128, sin128, cos128 * np.float32(0.125), sin128 * np.float32(0.125)]).astype(np.float32)
    return np.ascontiguousarray(rope)


def ret_tables():
    i = np.arange(128, dtype=np.float64)
    tab = np.zeros((128, 1024), np.float64)
    for h in range(4):
        lg = np.log1p(-(2.0 ** (-5.0 - h)))
        diff = i[None, :] - i[:, None]
        tab[:, h * 128:(h + 1) * 128] = np.where(diff >= 0, np.exp(np.maximum(diff, 0.0) * lg), 0.0)
        rp, hh = h // 2, h % 2
        qd = np.exp((i + 1.0) * lg)
        tab[hh * 64:(hh + 1) * 64, 512 + rp * 128:512 + (rp + 1) * 128] = qd[None, :]
        kd = np.exp((127.0 - i) * lg)
        tab[:, 768 + rp * 128 + hh * 64:768 + rp * 128 + (hh + 1) * 64] = kd[:, None]
    return np.ascontiguousarray(tab.astype(np.float32))


def mask_tables():
    ki = np.arange(128)[:, None]
    qi = np.arange(128)[None, :]
    m = np.zeros((128, 640), np.float32)
    m[:, 0:128] = np.where(ki >= qi, 0.0, -30000.0)
    m[:, 128:256] = np.where(ki <= qi, 0.0, -30000.0)
    m[:, 256:384] = np.eye(128, dtype=np.float32)
    m[:, 384:512] = np.where(ki >= qi, 1.0, 0.0)
    m[:, 512:640] = np.where(ki <= qi, 1.0, 0.0)
    return m


def kernel(**inp):
    inp = {k: np.asarray(v) for k, v in inp.items()}
    return run(inp, DEPTH)


def run(inp, n_layers, cores=8, trace=False):
    pc = param_layout(inp)
    prm = pc.build()
    b = Builder(n_layers, pc.off, pc.n)
    nc = b.build()
    x = inp["x"].astype(np.float32)
    f32 = lambda a: np.ascontiguousarray(np.asarray(a, np.float32))
    swp = []
    for (a0, a1) in ((0, 512), (512, 1024), (1536, 1792), (1792, 2048)):
        blk = inp["od_w_in"][:, :, a0:a1]
        nh = (a1 - a0) // 64
        blk = blk.reshape(2, D, nh, 2, 32)[:, :, :, ::-1, :].reshape(2, D, a1 - a0)
        swp.append(blk)
    od_w_sw = f32(np.concatenate(swp, axis=2))
    shared = {
        "prm": prm, "ev_w_in": f32(inp["ev_w_in"]), "ev_w_out": f32(inp["ev_w_out"]),
        "od_w_in": f32(inp["od_w_in"]), "od_w_sw": od_w_sw, "od_w_out": f32(inp["od_w_out"]),
        "ffn_w_up": f32(inp["ffn_w_up"]), "ffn_w_down": f32(inp["ffn_w_down"]),
        "rope": const_tables(), "ctab": ret_tables(), "maskb": mask_tables(),
    }
    in_maps = []
    for c in range(cores):
        m = dict(shared)
        m["xT"] = np.ascontiguousarray(x[c].T)
        in_maps.append(m)
    res = run_bass_kernel_spmd(nc, in_maps, core_ids=list(range(cores)), trace=trace)
    out = np.stack([np.ascontiguousarray(res.results[c]["yT"].T) for c in range(cores)], axis=0)
    if trace:
        return out.astype(np.float32), res
    return out.astype(np.float32)
```
